# Optimizing a Trainium2 kernel written in Bass

```python
import math
import jax
import jax.numpy as jnp
from jax import lax
import numpy as np

D_MODEL = 1024
BATCH = 8
SEQ = 2048
DEPTH = 2

GROUP_WIDTH = D_MODEL // 4
D_MIX = 4 * GROUP_WIDTH

GDN_HEADS = 4
GDN_DK = 64
GDN_DV = GROUP_WIDTH // GDN_HEADS
GDN_CONV = 4
GDN_CHUNK = 64
GDN_QKV = GDN_HEADS * (2 * GDN_DK + GDN_DV)

MLA_HEADS = 4
MLA_NOPE = 64
MLA_ROPE = 32
MLA_V = GROUP_WIDTH // MLA_HEADS
MLA_Q_RANK = 192
MLA_KV_RANK = 128
ROPE_BASE = 10000.0
ATTN_BLOCK = 128

HGRN_HEADS = 4
HGRN_DK = 64
HGRN_DV = GROUP_WIDTH // HGRN_HEADS
HGRN_CHUNK = 16

DSA_HEADS = 4
DSA_DH = GROUP_WIDTH // DSA_HEADS
DSA_BRANCHES = ((128, 1), (512, 4), (2048, 16))
DSA_BLOCK = 128

D_FF = 4 * D_MODEL
DEEPNORM_ALPHA = (2 * DEPTH) ** 0.25
DEEPNORM_BETA = (8 * DEPTH) ** -0.25
NORM_EPS = 1e-6
MASK_VALUE = -1e30

IN_SIZES = (
    GDN_QKV, GDN_HEADS, GDN_HEADS, GDN_HEADS * GDN_DV,
    MLA_Q_RANK, MLA_KV_RANK, MLA_ROPE,
    HGRN_HEADS * HGRN_DK, HGRN_HEADS * HGRN_DK,
    HGRN_HEADS * HGRN_DV, HGRN_HEADS * HGRN_DV,
    3 * DSA_HEADS * DSA_DH,
)
D_IN = sum(IN_SIZES)

kernel_name = 'hybrid_parallel_heads_deepnorm'


def _layer_norm(x, g, b):
    xf = x.astype(jnp.float32)
    mu = jnp.mean(xf, axis=-1, keepdims=True)
    var = jnp.mean(jnp.square(xf - mu), axis=-1, keepdims=True)
    return ((xf - mu) * lax.rsqrt(var + NORM_EPS) * g + b).astype(x.dtype)


def _rms_norm(x, g):
    xf = x.astype(jnp.float32)
    return (xf * lax.rsqrt(jnp.mean(xf * xf, axis=-1, keepdims=True) + NORM_EPS) * g).astype(x.dtype)


def _l2norm(x):
    return x * lax.rsqrt(jnp.sum(x * x, axis=-1, keepdims=True) + NORM_EPS)


def _masked_exp(mask, log_ratio):
    return jnp.where(mask, jnp.exp(jnp.where(mask, log_ratio, 0.0)), 0.0)


def _rope(x, pos):
    half = x.shape[-1] // 2
    inv_freq = ROPE_BASE ** (-jnp.arange(half, dtype=jnp.float32) / half)
    ang = pos[:, None] * inv_freq[None, :]
    cos = jnp.cos(ang)[None, :, None, :].astype(x.dtype)
    sin = jnp.sin(ang)[None, :, None, :].astype(x.dtype)
    x1, x2 = x[..., :half], x[..., half:]
    return jnp.concatenate([x1 * cos - x2 * sin, x1 * sin + x2 * cos], axis=-1)


def _causal_depthwise_conv_silu(x, w):
    k_width, ch = w.shape
    y = lax.conv_general_dilated(
        x, w[:, None, :].astype(x.dtype), window_strides=(1,), padding=((k_width - 1, 0),),
        dimension_numbers=('NWC', 'WIO', 'NWC'), feature_group_count=ch)
    return jax.nn.silu(y)


def _gated_delta_rule(q, k, v, g, beta):
    b_, s_, h, dk = q.shape
    dv = v.shape[-1]
    c = GDN_CHUNK
    n = s_ // c
    q = q.reshape(b_, n, c, h, dk)
    k = k.reshape(b_, n, c, h, dk)
    v = v.reshape(b_, n, c, h, dv)
    beta = beta.reshape(b_, n, c, h)
    g = jnp.cumsum(g.reshape(b_, n, c, h), axis=2)
    causal = jnp.tril(jnp.ones((c, c), bool))
    strict = jnp.tril(jnp.ones((c, c), bool), -1)
    g_h = jnp.moveaxis(g, 2, 3)
    decay = _masked_exp(causal, g_h[..., :, None] - g_h[..., None, :])
    k_beta = k * beta[..., None]
    lower = jnp.where(strict, jnp.einsum('bnihd,bnjhd->bnhij', k_beta, k) * decay, 0.0)
    eye = jnp.eye(c, dtype=q.dtype)
    t_inv = lax.linalg.triangular_solve(lower + eye, jnp.broadcast_to(eye, lower.shape),
                                        left_side=True, lower=True, unit_diagonal=True)
    u = jnp.einsum('bnhij,bnjhd->bnihd', t_inv, v * beta[..., None])
    w = jnp.einsum('bnhij,bnjhd->bnihd', t_inv, k_beta * jnp.exp(g)[..., None])
    qk = jnp.einsum('bnihd,bnjhd->bnhij', q, k) * decay
    q_dec = q * jnp.exp(g)[..., None]
    g_last = g[:, :, -1]
    k_dec = k * jnp.exp(g_last[:, :, None] - g)[..., None]

    def step(state, xs):
        u_n, w_n, q_n, k_n, qk_n, gl_n = xs
        v_new = u_n - jnp.einsum('bchk,bhkv->bchv', w_n, state)
        o_n = jnp.einsum('bchk,bhkv->bchv', q_n, state) + jnp.einsum('bhij,bjhv->bihv', qk_n, v_new)
        state = state * jnp.exp(gl_n)[..., None, None] + jnp.einsum('bchk,bchv->bhkv', k_n, v_new)
        return state, o_n

    xs = tuple(jnp.moveaxis(t, 1, 0) for t in (u, w, q_dec, k_dec, qk, g_last))
    state0 = jnp.zeros((b_, h, dk, dv), q.dtype)
    _, o = lax.scan(step, state0, xs)
    return jnp.moveaxis(o, 0, 1).reshape(b_, s_, h, dv)


def _gated_deltanet(qkv, a, b, z, conv_w, a_log, dt_bias, norm_g):
    b_, s_, _ = qkv.shape
    qkv = _causal_depthwise_conv_silu(qkv, conv_w).astype(jnp.float32)
    q, k, v = jnp.split(qkv, [GDN_HEADS * GDN_DK, 2 * GDN_HEADS * GDN_DK], axis=-1)
    q = _l2norm(q.reshape(b_, s_, GDN_HEADS, GDN_DK)) * (GDN_DK ** -0.5)
    k = _l2norm(k.reshape(b_, s_, GDN_HEADS, GDN_DK))
    v = v.reshape(b_, s_, GDN_HEADS, GDN_DV)
    beta = jax.nn.sigmoid(b.astype(jnp.float32))
    g = -jnp.exp(a_log.astype(jnp.float32)) * jax.nn.softplus(a.astype(jnp.float32) + dt_bias.astype(jnp.float32))
    o = _gated_delta_rule(q, k, v, g, beta)
    o = _rms_norm(o, norm_g) * jax.nn.silu(z.reshape(b_, s_, GDN_HEADS, GDN_DV).astype(jnp.float32))
    return o.reshape(b_, s_, GDN_HEADS * GDN_DV).astype(z.dtype)


def _causal_block_attention(q, k, v):
    b_, s_, h, dqk = q.shape
    nblk = s_ // ATTN_BLOCK
    scale = dqk ** -0.5
    q_blocks = jnp.moveaxis(q.reshape(b_, nblk, ATTN_BLOCK, h, dqk), 1, 0)
    k_pos = jnp.arange(s_)

    def attend(args):
        qb, n = args
        s = jnp.einsum('bqhd,bkhd->bhqk', qb, k).astype(jnp.float32) * scale
        q_pos = n * ATTN_BLOCK + jnp.arange(ATTN_BLOCK)
        s = jnp.where(k_pos[None, :] <= q_pos[:, None], s, MASK_VALUE)
        p = jax.nn.softmax(s, axis=-1)
        return jnp.einsum('bhqk,bkhd->bqhd', p.astype(v.dtype), v)

    o = lax.map(attend, (q_blocks, jnp.arange(nblk)))
    return jnp.moveaxis(o, 0, 1).reshape(b_, s_, h, v.shape[-1])


def _mla(c_q, c_kv, k_rope, q_norm_g, kv_norm_g, w_uq, w_ukv, pos):
    b_, s_, _ = c_q.shape
    q = (_rms_norm(c_q, q_norm_g) @ w_uq).reshape(b_, s_, MLA_HEADS, MLA_NOPE + MLA_ROPE)
    kv = (_rms_norm(c_kv, kv_norm_g) @ w_ukv).reshape(b_, s_, MLA_HEADS, MLA_NOPE + MLA_V)
    q = jnp.concatenate([q[..., :MLA_NOPE], _rope(q[..., MLA_NOPE:], pos)], axis=-1)
    k_r = jnp.broadcast_to(_rope(k_rope[:, :, None, :], pos), (b_, s_, MLA_HEADS, MLA_ROPE))
    k = jnp.concatenate([kv[..., :MLA_NOPE], k_r], axis=-1)
    v = kv[..., MLA_NOPE:]
    o = _causal_block_attention(q, k, v)
    return o.reshape(b_, s_, MLA_HEADS * MLA_V)


def _gla_chunked(q, k, v, log_f):
    b_, s_, h, dk = q.shape
    dv = v.shape[-1]
    c = HGRN_CHUNK
    n = s_ // c
    q = q.reshape(b_, n, c, h, dk)
    k = k.reshape(b_, n, c, h, dk)
    v = v.reshape(b_, n, c, h, dv)
    cum = jnp.cumsum(log_f.reshape(b_, n, c, h, dk), axis=2)
    causal = jnp.tril(jnp.ones((c, c), bool))[:, :, None, None]
    decay = _masked_exp(causal, cum[:, :, :, None] - cum[:, :, None, :])
    scores = jnp.sum(q[:, :, :, None] * k[:, :, None, :] * decay, axis=-1)
    o_intra = jnp.einsum('bntsh,bnshv->bnthv', scores, v)
    cum_last = cum[:, :, -1]
    k_dec = k * jnp.exp(cum_last[:, :, None] - cum)

    def step(state, xs):
        k_n, v_n, gl_n = xs
        new = state * jnp.exp(gl_n)[..., None] + jnp.einsum('bchk,bchv->bhkv', k_n, v_n)
        return new, state

    xs = tuple(jnp.moveaxis(t, 1, 0) for t in (k_dec, v, cum_last))
    _, s_prev = lax.scan(step, jnp.zeros((b_, h, dk, dv), q.dtype), xs)
    o_inter = jnp.einsum('bnchk,nbhkv->bnchv', q * jnp.exp(cum), s_prev)
    return (o_intra + o_inter).reshape(b_, s_, h, dv)


def _hgrn2(q, f, i, g, lb, norm_g):
    b_, s_, _ = q.shape
    qf = q.reshape(b_, s_, HGRN_HEADS, HGRN_DK).astype(jnp.float32)
    fl = f.reshape(b_, s_, HGRN_HEADS, HGRN_DK).astype(jnp.float32)
    lb = lb.reshape(HGRN_HEADS, HGRN_DK)
    log_f = jnp.log(lb + (1.0 - lb) * jax.nn.sigmoid(fl))
    k = (1.0 - lb) * jax.nn.sigmoid(-fl)
    v = i.reshape(b_, s_, HGRN_HEADS, HGRN_DV).astype(jnp.float32)
    o = _gla_chunked(qf, k, v, log_f)
    o = _rms_norm(o, norm_g) * jax.nn.silu(g.reshape(b_, s_, HGRN_HEADS, HGRN_DV).astype(jnp.float32))
    return o.reshape(b_, s_, HGRN_HEADS * HGRN_DV).astype(q.dtype)


def _alibi_slopes(n):
    return jnp.asarray([2.0 ** (-8.0 * (j + 1) / n) for j in range(n)], dtype=jnp.float32)


def _dilated_branch(q, k, v, slopes, window, dilation):
    b_, s_, h, dh = q.shape
    length = s_ // dilation
    steps_max = window // dilation
    bb = b_ * dilation

    def sub(t):
        return jnp.moveaxis(t.reshape(b_, length, dilation, h, dh), 2, 1).reshape(bb, length, h, dh)

    nblk = -(-length // DSA_BLOCK)
    lp = nblk * DSA_BLOCK
    pad = ((0, 0), (0, lp - length), (0, 0), (0, 0))
    qs, ks, vs = (jnp.pad(sub(t), pad).reshape(bb, nblk, DSA_BLOCK, h, dh) for t in (q, k, v))
    k_cat = jnp.concatenate([jnp.concatenate([jnp.zeros_like(ks[:, :1]), ks[:, :-1]], axis=1), ks], axis=2)
    v_cat = jnp.concatenate([jnp.concatenate([jnp.zeros_like(vs[:, :1]), vs[:, :-1]], axis=1), vs], axis=2)
    qi = jnp.arange(DSA_BLOCK)[:, None]
    ki = jnp.arange(2 * DSA_BLOCK)[None, :]
    steps = DSA_BLOCK + qi - ki
    blk = jnp.arange(nblk)[:, None, None]
    valid = (steps >= 0) & (steps <= steps_max) & (blk * DSA_BLOCK + ki - DSA_BLOCK >= 0)
    alibi = -slopes[:, None, None] * (steps * dilation).astype(jnp.float32)
    s = jnp.einsum('bnqhd,bnkhd->bnhqk', qs, k_cat).astype(jnp.float32) * (dh ** -0.5) + alibi[None, None]
    s = jnp.where(valid[None, :, None], s, MASK_VALUE)
    lse = jax.nn.logsumexp(s, axis=-1)
    p = jnp.exp(s - lse[..., None])
    o = jnp.einsum('bnhqk,bnkhd->bnqhd', p.astype(v.dtype), v_cat)
    o = o.reshape(bb, lp, h, dh)[:, :length].reshape(b_, dilation, length, h, dh)
    o = jnp.moveaxis(o, 1, 2).reshape(b_, s_, h, dh)
    lse = jnp.swapaxes(lse, 2, 3).reshape(bb, lp, h)[:, :length].reshape(b_, dilation, length, h)
    lse = jnp.moveaxis(lse, 1, 2).reshape(b_, s_, h)
    return o, lse


def _dilated_attention(qkv):
    b_, s_, _ = qkv.shape
    q, k, v = (t.reshape(b_, s_, DSA_HEADS, DSA_DH) for t in jnp.split(qkv, 3, axis=-1))
    slopes = _alibi_slopes(DSA_HEADS)
    outs, lses = [], []
    for window, dilation in DSA_BRANCHES:
        o, lse = _dilated_branch(q, k, v, slopes, window, dilation)
        outs.append(o)
        lses.append(lse)
    weights = jax.nn.softmax(jnp.stack(lses, axis=0), axis=0)
    o = jnp.sum(weights[..., None] * jnp.stack(outs, axis=0).astype(jnp.float32), axis=0)
    return o.reshape(b_, s_, DSA_HEADS * DSA_DH).astype(qkv.dtype)


def _sq_relu_mlp(x, w1, w2):
    return jnp.square(jax.nn.relu(x @ w1)) @ w2


def setup_inputs(seed: int = 0) -> dict:
    key = jax.random.key(seed)
    ks = jax.random.split(key, 20)
    f32 = jnp.float32

    def normal(k, shape, scale):
        return jax.random.normal(k, shape, f32) * scale

    def gain(k, shape):
        return 1.0 + normal(k, shape, 0.02)

    dt = jnp.exp(jax.random.uniform(ks[4], (DEPTH, GDN_HEADS), f32, math.log(1e-3), math.log(1e-1)))
    return {
        'x': normal(ks[0], (BATCH, SEQ, D_MODEL), 1.0),
        'w_in': normal(ks[1], (DEPTH, D_MODEL, D_IN), D_MODEL ** -0.5),
        'gdn_conv_w': normal(ks[2], (DEPTH, GDN_CONV, GDN_QKV), GDN_CONV ** -0.5),
        'gdn_a_log': jnp.log(jax.random.uniform(ks[3], (DEPTH, GDN_HEADS), f32, 1.0, 16.0)),
        'gdn_dt_bias': dt + jnp.log(-jnp.expm1(-dt)),
        'gdn_norm_g': gain(ks[5], (DEPTH, GDN_DV)),
        'mla_q_norm_g': gain(ks[6], (DEPTH, MLA_Q_RANK)),
        'mla_kv_norm_g': gain(ks[7], (DEPTH, MLA_KV_RANK)),
        'mla_w_uq': normal(ks[8], (DEPTH, MLA_Q_RANK, MLA_HEADS * (MLA_NOPE + MLA_ROPE)), MLA_Q_RANK ** -0.5),
        'mla_w_ukv': normal(ks[9], (DEPTH, MLA_KV_RANK, MLA_HEADS * (MLA_NOPE + MLA_V)), MLA_KV_RANK ** -0.5),
        'hgrn_lb_logits': normal(ks[10], (DEPTH, HGRN_HEADS * HGRN_DK), 0.5),
        'hgrn_norm_g': gain(ks[11], (DEPTH, HGRN_DV)),
        'w_out': normal(ks[12], (DEPTH, D_MIX, D_MODEL), D_MIX ** -0.5 * DEEPNORM_BETA),
        'ln1_g': gain(ks[13], (DEPTH, D_MODEL)),
        'ln1_b': normal(ks[14], (DEPTH, D_MODEL), 0.02),
        'w_ff1': normal(ks[15], (DEPTH, D_MODEL, D_FF), D_MODEL ** -0.5),
        'w_ff2': normal(ks[16], (DEPTH, D_FF, D_MODEL), D_FF ** -0.5 * DEEPNORM_BETA),
        'ln2_g': gain(ks[17], (DEPTH, D_MODEL)),
        'ln2_b': normal(ks[18], (DEPTH, D_MODEL), 0.02),
    }


def reference(x, w_in, gdn_conv_w, gdn_a_log, gdn_dt_bias, gdn_norm_g, mla_q_norm_g, mla_kv_norm_g,
              mla_w_uq, mla_w_ukv, hgrn_lb_logits, hgrn_norm_g, w_out, ln1_g, ln1_b, w_ff1, w_ff2,
              ln2_g, ln2_b):
    pos = jnp.arange(x.shape[1], dtype=jnp.float32)
    p_lb = jax.nn.softmax(hgrn_lb_logits.astype(jnp.float32), axis=0)
    lower_bounds = jnp.cumsum(p_lb, axis=0) - p_lb[:1]
    split_points = np.cumsum(IN_SIZES)[:-1].tolist()
    for l in range(DEPTH):
        h = x @ w_in[l]
        (a_qkv, a_a, a_b, a_z, b_cq, b_ckv, b_kr,
         c_q, c_f, c_i, c_g, d_qkv) = jnp.split(h, split_points, axis=-1)
        o_a = _gated_deltanet(a_qkv, a_a, a_b, a_z, gdn_conv_w[l], gdn_a_log[l], gdn_dt_bias[l], gdn_norm_g[l])
        o_b = _mla(b_cq, b_ckv, b_kr, mla_q_norm_g[l], mla_kv_norm_g[l], mla_w_uq[l], mla_w_ukv[l], pos)
        o_c = _hgrn2(c_q, c_f, c_i, c_g, lower_bounds[l], hgrn_norm_g[l])
        o_d = _dilated_attention(d_qkv)
        mixed = jnp.concatenate([o_a, o_b.astype(x.dtype), o_c, o_d], axis=-1) @ w_out[l]
        x = _layer_norm(DEEPNORM_ALPHA * x + mixed, ln1_g[l], ln1_b[l])
        x = _layer_norm(DEEPNORM_ALPHA * x + _sq_relu_mlp(x, w_ff1[l], w_ff2[l]), ln2_g[l], ln2_b[l])
    return x
```

```python
import math
import numpy as np
import concourse.bass as bass
import concourse.mybir as mybir
from concourse.bass_utils import run_bass_kernel_spmd
from contextlib import ExitStack, contextmanager

F32 = mybir.dt.float32
BF16 = mybir.dt.bfloat16
AF = mybir.ActivationFunctionType
ALU = mybir.AluOpType
AX = mybir.AxisListType

COMPUTE = ('pe', 'act', 'dve', 'pool')
ENGS = ('pe', 'act', 'dve', 'pool', 'sp')
N_DMA_SEMS = 6

S = 2048
D = 1024
NCORES = 8
DEPTH = 2
D_IN = 3176
D_FF = 4096
ALPHA = (2 * DEPTH) ** 0.25
EPS = 1e-6


class Res:
    __slots__ = ('name', 'w', 'r')

    def __init__(self, name='', barrier=None):
        self.name = name
        self.w = None
        self.r = dict(barrier) if barrier else {}


class T:
    def __init__(self, handle, name, barrier=None):
        self.h = handle
        self.name = name
        self.res = Res(name, barrier)
        self.subs = {}
        self._barrier = barrier

    def __getitem__(self, k):
        return self.h[k]

    def sub(self, key):
        r = self.subs.get(key)
        if r is None:
            r = self.subs[key] = Res(f'{self.name}.{key}', self._barrier)
        return r

    def all_res(self):
        return [self.res] + list(self.subs.values())


def _res(x):
    return x.res if isinstance(x, T) else x


def ones_f(P, b):
    t = getattr(b, '_ones_f', None)
    if t is None:
        raise RuntimeError('ones_f not allocated')
    return t


def _mx(d, k, v):
    if v > d.get(k, -1):
        d[k] = v


class Prog:
    def __init__(self, nc):
        self.nc = nc
        self.stack = ExitStack()
        self.ins = {e: [] for e in ENGS}
        self.nalloc = 0
        self.barrier = {}
        self.scopes = []
        self.dma_k = {e: 0 for e in ENGS}
        self.dma_cnt = {}

    def _reg(self, t):
        if self.scopes:
            self.scopes[-1][1].append(t)
        return t

    def sb(self, shape, dt=F32, name=None):
        self.nalloc += 1
        name = f'{name or "t"}{self.nalloc}'
        st = self.scopes[-1][0] if self.scopes else self.stack
        h = st.enter_context(self.nc.sbuf_tensor(name, list(shape), dt))
        return self._reg(T(h, name, dict(self.barrier)))

    def ps(self, shape, dt=F32, name=None):
        self.nalloc += 1
        name = f'{name or "p"}{self.nalloc}'
        h = self.stack.enter_context(self.nc.psum_tensor(name, list(shape), dt))
        return T(h, name)

    def dram(self, shape, dt=F32, name=None, kind='Internal'):
        self.nalloc += 1
        name = name or f'd{self.nalloc}'
        h = self.nc.dram_tensor(name, list(shape), dt, kind=kind)
        return T(h.ap(), name)

    @contextmanager
    def scope(self):
        st = ExitStack()
        tiles = []
        self.scopes.append((st, tiles))
        try:
            yield
        finally:
            self.scopes.pop()
            st.close()
            b = self.barrier
            for t in tiles:
                for r in t.all_res():
                    if r.w is not None:
                        _mx(b, r.w[0], r.w[1])
                    for k, v in r.r.items():
                        _mx(b, k, v)

    def op(self, eng, fn, reads=(), writes=(), dma=False):
        idx = len(self.ins[eng])
        deps = {}
        for r in reads:
            r = _res(r)
            if r.w is not None:
                _mx(deps, r.w[0], r.w[1])
        for w in writes:
            w = _res(w)
            if w.w is not None:
                _mx(deps, w.w[0], w.w[1])
            for k, v in w.r.items():
                _mx(deps, k, v)
        rec = dict(fn=fn, deps=deps, dma=dma, inc=False)
        if dma:
            key = ('d', eng, self.dma_k[eng] % N_DMA_SEMS)
            self.dma_k[eng] += 1
            prev = self.dma_cnt.get(key, 0)
            if prev:
                _mx(deps, key, prev)
            self.dma_cnt[key] = prev + 16
            tok = (key, prev + 16)
            rec['dsem'] = key
        else:
            tok = (('c', eng), idx)
        self.ins[eng].append(rec)
        for r in reads:
            _mx(_res(r).r, tok[0], tok[1])
        for w in writes:
            w = _res(w)
            w.w = tok
            w.r = {}
        return tok

    def dma(self, eng, out, in_, reads=(), writes=(), **kw):
        return self.op(eng, lambda e: e.dma_start(out=out, in_=in_, **kw), reads=reads, writes=writes, dma=True)

    def emit(self):
        nc = self.nc
        ins = self.ins
        for E in ENGS:
            seen = {}
            for idx, rec in enumerate(ins[E]):
                waits = []
                for k, val in rec['deps'].items():
                    if k == ('c', 'pe') and E == 'pe':
                        continue
                    if k == ('c', E) and val >= idx:
                        continue
                    if val <= seen.get(k, -1):
                        continue
                    seen[k] = val
                    waits.append((k, val))
                    if k[0] == 'c':
                        ins[k[1]][val]['inc'] = True
                rec['waits'] = waits
        for e in ENGS:
            c = 0
            for rec in ins[e]:
                if rec['inc'] and not rec['dma']:
                    c += 1
                    rec['semval'] = c
        st = self.stack
        csem = {e: st.enter_context(nc.semaphore(f's_{e}')) for e in ENGS}
        dsem = {key: st.enter_context(nc.semaphore(f'd_{key[1]}{key[2]}')) for key in self.dma_cnt}
        block = st.enter_context(nc.Block())

        def replay(E, eng):
            for rec in ins[E]:
                for (k, val) in rec['waits']:
                    if k[0] == 'c':
                        eng.wait_ge(csem[k[1]], ins[k[1]][val]['semval'])
                    else:
                        eng.wait_ge(dsem[k], val)
                r = rec['fn'](eng)
                if rec['dma']:
                    r.then_inc(dsem[rec['dsem']], 16)
                elif rec['inc']:
                    r.then_inc(csem[E], 1)

        @block.tensor
        def _(e):
            replay('pe', e)

        @block.scalar
        def _(e):
            replay('act', e)

        @block.vector
        def _(e):
            replay('dve', e)

        @block.gpsimd
        def _(e):
            replay('pool', e)

        @block.sync
        def _(e):
            replay('sp', e)

    def close(self):
        self.stack.close()


def make_consts():
    c = {}
    c['ident'] = np.eye(128, dtype=np.float32)
    kj = np.arange(128)[:, None]
    qi = np.arange(128)[None, :]
    c['dist'] = (qi - kj).astype(np.float32)
    c['tri_ge'] = (qi >= kj).astype(np.float32)
    c['tri_le'] = (qi <= kj).astype(np.float32)
    c['mask16'] = ((qi >= kj) & (qi // 16 == kj // 16)).astype(np.float32)
    c['cm8'] = (np.arange(128)[:, None] // 16 == np.arange(8)[None, :]).astype(np.float32)
    f = np.arange(512)[None, None, :]
    p = np.arange(128)[:, None, None]
    r = np.arange(4)[None, :, None]
    c['cmask'] = (f >= 128 * r + p).astype(np.float32)
    return c


CONST_ORDER = ['ident', 'dist', 'tri_ge', 'tri_le', 'mask16', 'cm8', 'cmask']


def pack_consts():
    c = make_consts()
    cols = {}
    arrs = []
    off = 0
    for k in CONST_ORDER:
        a = c[k].reshape(128, -1).astype(np.float32)
        cols[k] = (off, a.shape[1])
        arrs.append(a)
        off += a.shape[1]
    return np.concatenate(arrs, axis=1), cols


class Builder:
    def __init__(self, stage='full', nlayers=DEPTH):
        self.stage = stage
        self.nlayers = nlayers
        nc = self.nc = bass.Bass("TRN2", target_bir_lowering=False)
        P = self.P = Prog(nc)
        self.cst_np, self.cst_cols = pack_consts()
        ein = lambda name, shape: P.dram(shape, F32, name, kind='ExternalInput')
        self.x_d = ein('x', [S, D])
        self.w_in_d = ein('w_in', [DEPTH, D, D_IN])
        self.w_out_d = ein('w_out', [DEPTH, D, D])
        self.w_ff1_d = ein('w_ff1', [DEPTH, D, D_FF])
        self.w_ff2_d = ein('w_ff2', [DEPTH, D_FF, D])
        self.ln1_g_d = ein('ln1_g', [DEPTH, D])
        self.ln1_b_d = ein('ln1_b', [DEPTH, D])
        self.ln2_g_d = ein('ln2_g', [DEPTH, D])
        self.ln2_b_d = ein('ln2_b', [DEPTH, D])
        self.cst_d = ein('cst', list(self.cst_np.shape))
        self.rope_d = ein('rope', [2, 32, S])
        self.rmask_d = ein('rmask', [2, S])
        self.gdn_conv_d = ein('gdn_conv_w', [DEPTH, 4, 768])
        self.gdn_alog_d = ein('gdn_a_log', [DEPTH, 4])
        self.gdn_dtb_d = ein('gdn_dt_bias', [DEPTH, 4])
        self.gdn_g_d = ein('gdn_norm_g', [DEPTH, 64])
        _k = 'ExternalOutput' if stage == 'A' else 'Internal'
        self.kkd = P.dram([128, 64, 64], F32, 'kk_scr', kind=_k)
        self.qkd = P.dram([128, 64, 64], F32, 'qk_scr', kind=_k)
        self.ttd = P.dram([128, 64, 64], F32, 'tt_scr', kind=_k)
        self.qktd = P.dram([128, 64, 64], F32, 'qkt_scr', kind=_k)
        self.tokd = P.dram([4, 3, 64, 32, 64], F32, 'tok_scr', kind=_k)
        self.qed = P.dram([4, 64, S], F32, 'qe_scr', kind=_k)
        self.gsd = P.dram([4, 64, S], F32, 'gs_scr', kind=_k)
        self.rows8d = P.dram([2, 8, S], F32, 'rows8_scr', kind=_k)
        self.hgrn_lb_d = ein('hgrn_lb_logits', [DEPTH, 256])
        self.hgrn_g_d = ein('hgrn_norm_g', [DEPTH, 64])
        self.mla_qg_d = ein('mla_q_norm_g', [DEPTH, 192])
        self.mla_kvg_d = ein('mla_kv_norm_g', [DEPTH, 128])
        self.mla_wuq_d = ein('mla_w_uq', [DEPTH, 192, 384])
        self.mla_wukv_d = ein('mla_w_ukv', [DEPTH, 128, 512])
        self.y_d = P.dram([S, D], F32, 'y', kind='ExternalOutput')
        self.ocat_d = P.dram([16, 64, S], BF16, 'ocat_scr', kind=('Internal' if stage in ('full', 'dense') else 'ExternalOutput'))
        self.xres_d = P.dram([S, D], F32, 'xres_scr')
        self.dbg = {}
        self.cst = P.sb(list(self.cst_np.shape), F32, 'cst')
        P.dma('sp', self.cst[:], self.cst_d[:], writes=[self.cst])
        self.xT = P.sb([128, 8, S], BF16, 'xT')
        self._ones_f = P.sb([32, 64], F32, 'ones_f')
        P.op('pool', lambda e: e.memset(self._ones_f[:], 1.0), writes=[self._ones_f])
        self.psum = [P.ps([128, 512], F32, f'ps{i}') for i in range(8)]
        self.build()
        outs = [self.y_d, self.ocat_d, self.kkd, self.qkd, self.ttd, self.qktd, self.tokd, self.qed, self.gsd, self.rows8d] + [t for t in self.dbg.values()]
        P.op('sp', lambda e: e.nop(), reads=outs)
        P.emit()
        P.close()

    def dump(self, name, t, ap, shape):
        if self.stage in ('full', 'dense'):
            return
        d = self.P.dram(list(shape), F32, 'dbg_' + name, kind='ExternalOutput')
        self.P.dma('sp', d[:], ap, reads=[t], writes=[d])
        self.dbg[name] = d

    def c(self, name):
        o, n = self.cst_cols[name]
        return self.cst[:, o:o + n]

    def build(self):
        P = self.P
        self.load_x_transposed()
        for l in range(self.nlayers):
            last = (l == self.nlayers - 1)
            with P.scope():
                self.mixers(l)
            if self.stage in ('full', 'dense'):
                with P.scope():
                    self.dense(l, last)

    def load_x_transposed(self):
        P = self.P
        xT = self.xT
        with P.scope():
            xt = [P.sb([128, D], F32, 'xin') for _ in range(2)]
            for tt in range(16):
                t = xt[tt % 2]
                P.dma('sp', t[:], self.x_d[128 * tt:128 * tt + 128, :], writes=[t])
                self.transpose_tile_to_xT(t, tt)

    def transpose_tile_to_xT(self, t, tt, pbase=6):
        P = self.P
        ident = self.c('ident')
        for half in range(2):
            ps = self.psum[pbase + half]
            for j in range(4):
                kc = half * 4 + j
                P.op('pe', lambda e, ps=ps, j=j, kc=kc: e.transpose(ps[:, 128 * j:128 * j + 128], t[:, 128 * kc:128 * kc + 128], ident),
                     reads=[t, self.cst], writes=[ps])
            P.op('act', lambda e, ps=ps, half=half: e.copy(
                out=self.xT[:, 4 * half:4 * half + 4, 128 * tt:128 * tt + 128],
                in_=ps[:].rearrange('p (k t) -> p k t', k=4)),
                reads=[ps], writes=[self.xT.sub(tt // 4)])

    def mixers(self, l):
        if self.stage == 'dense':
            self.mixer_stub(l)
            return
        if self.stage in ('full', 'D'):
            with self.P.scope():
                self.mixer_D(l)

        if self.stage in ('full', 'A'):
            with self.P.scope():
                self.mixer_A(l)
        if self.stage in ('full', 'B'):
            with self.P.scope():
                self.mixer_B(l)
        if self.stage in ('full', 'C'):
            with self.P.scope():
                self.mixer_C(l)

    def rms_gate_out(self, o, gate, gcol, ones, slot, tb, sq, rs, ob):
        P = self.P
        sl = slice(512 * tb, 512 * tb + 512)
        P.op('act', lambda e: e.activation(out=sq[:], in_=o[:], func=AF.Square), reads=[o], writes=[sq])
        ps = self.psum[self._pn % 2]; self._pn += 1
        P.op('pe', lambda e: e.matmul(ps[0:64, :], lhsT=ones[0:64, 0:64], rhs=sq[:], start=True, stop=True), reads=[ones, sq], writes=[ps])
        P.op('act', lambda e: e.activation(out=rs[:], in_=ps[0:64, :], func=AF.Ln, scale=1.0 / 64, bias=EPS), reads=[ps], writes=[rs])
        P.op('act', lambda e: e.activation(out=rs[:], in_=rs[:], func=AF.Exp, scale=-0.5), reads=[rs], writes=[rs])
        P.op('dve', lambda e: e.scalar_tensor_tensor(out=o[:], in0=o[:], scalar=gcol, in1=rs[:], op0=ALU.mult, op1=ALU.mult),
             reads=[o, rs], writes=[o])
        P.op('dve', lambda e: e.tensor_tensor(out=ob[:], in0=o[:], in1=gate[:, sl], op=ALU.mult), reads=[o, gate, gate.sub(tb)], writes=[ob])
        P.dma('sp', self.ocat_d[slot, :, sl], ob[:], reads=[ob], writes=[self.ocat_d])

    def mixer_A(self, l):
        P = self.P
        psum = self.psum
        xT = self.xT
        self._pn = 0
        ident = self.c('ident')
        WA = P.sb([128, 8, 1032], BF16, 'wA')
        P.dma('pool', WA[:], self.w_in_d[l, :, 0:1032].rearrange('(k p) n -> p k n', p=128), writes=[WA])
        ones = P.sb([128, 128], BF16, 'onesA')
        P.op('pool', lambda e: e.memset(ones[:], 1.0), writes=[ones])
        gn = P.sb([64, 1], F32, 'gnA')
        P.dma('sp', gn[:], self.gdn_g_d[l, :].rearrange('(p o) -> p o', o=1), writes=[gn])
        cw = P.sb([64, 12, 4], F32, 'cwA')
        for j in range(4):
            for b_ in range(12):
                P.dma('sp', cw[:, b_, j:j + 1], self.gdn_conv_d[l, j, 64 * b_:64 * b_ + 64].rearrange('(c o) -> c o', o=1), reads=[cw], writes=[cw])
        e8 = P.sb([32, S], F32, 'e8')
        SC = {}
        for nm in ('beta', 'e', 'ed', 'egl'):
            SC[nm] = P.sb([64, 32, 8], F32, 'sc_' + nm)
        be = P.sb([64, 32, 4], F32, 'sc_be')
        self._A_rows(l, WA, e8, SC, be)
        self._A_rest(l, WA, e8, SC, be, ones, gn, cw)

    def _A_rows(self, l, WA, e8, SC, be):
        P = self.P
        psum = self.psum
        ident = self.c('ident')
        with P.scope():
            self._A_rows_inner(l, WA, e8, SC, be)

    def _A_rows_inner(self, l, WA, e8, SC, be):
        P = self.P
        psum = self.psum
        ident = self.c('ident')
        rm8 = P.sb([32, S], F32, 'rm8')
        P.dma('sp', rm8[:], self.rmask_d[1:2, :].partition_broadcast(32), writes=[rm8])
        dtb = P.sb([32, 1], F32, 'dtb'); nA = P.sb([32, 1], F32, 'nA')
        P.op('pool', lambda e: e.memset(dtb[:], 0.0), writes=[dtb])
        P.op('pool', lambda e: e.memset(nA[:], 0.0), writes=[nA])
        P.dma('sp', dtb[0:4, :], self.gdn_dtb_d[l, :].rearrange('(p o) -> p o', o=1), reads=[dtb], writes=[dtb])
        P.dma('sp', nA[0:4, :], self.gdn_alog_d[l, :].rearrange('(p o) -> p o', o=1), reads=[nA], writes=[nA])
        P.op('act', lambda e: e.activation(out=nA[:], in_=nA[:], func=AF.Exp), reads=[nA], writes=[nA])
        P.op('dve', lambda e: e.tensor_scalar(out=nA[:], in0=nA[:], scalar1=-1.0, scalar2=None, op0=ALU.mult), reads=[nA], writes=[nA])
        ab = P.sb([32, S], F32, 'abA'); beta8 = P.sb([32, S], F32, 'beta8'); gc8 = P.sb([32, S], F32, 'gc8')
        self.proj_fm(WA, 768, 32, lambda ps, tb: P.op('act', lambda e: e.copy(out=ab[:, 512 * tb:512 * tb + 512], in_=ps[0:32, :]), reads=[ps], writes=[ab]))
        P.op('act', lambda e: e.activation(out=beta8[:], in_=ab[:], func=AF.Sigmoid), reads=[ab], writes=[beta8])
        P.op('act', lambda e: e.activation(out=ab[:], in_=ab[:], func=AF.Exp, bias=dtb[:, 0:1]), reads=[ab, dtb], writes=[ab])
        P.op('act', lambda e: e.activation(out=ab[:], in_=ab[:], func=AF.Ln, bias=1.0), reads=[ab], writes=[ab])
        P.op('dve', lambda e: e.tensor_scalar(out=ab[:], in0=ab[:], scalar1=nA[:, 0:1], scalar2=None, op0=ALU.mult), reads=[ab, nA], writes=[ab])
        P.op('dve', lambda e: e.tensor_tensor_scan(out=gc8[:], data0=rm8[:], data1=ab[:], initial=0.0, op0=ALU.mult, op1=ALU.add), reads=[rm8, ab], writes=[gc8])
        P.dma('sp', self.rows8d[0], gc8[0:8, :], reads=[gc8], writes=[self.rows8d])
        P.dma('sp', self.rows8d[1], beta8[0:8, :], reads=[beta8], writes=[self.rows8d])
        ed8 = P.sb([32, S], F32, 'ed8'); egl8 = P.sb([32, S], F32, 'egl8')
        gc3 = gc8[:].rearrange('p (n c) -> p n c', c=64)
        P.op('act', lambda e: e.activation(out=e8[:], in_=gc8[:], func=AF.Exp), reads=[gc8], writes=[e8])
        P.op('dve', lambda e: e.tensor_tensor(out=ed8[:].rearrange('p (n c) -> p n c', c=64), in0=gc3[:, :, 63:64].to_broadcast([32, 32, 64]), in1=gc3, op=ALU.subtract),
             reads=[gc8], writes=[ed8])
        P.op('act', lambda e: e.activation(out=ed8[:], in_=ed8[:], func=AF.Exp), reads=[ed8], writes=[ed8])
        P.op('dve', lambda e: e.tensor_copy(out=egl8[:].rearrange('p (n c) -> p n c', c=64), in_=gc3[:, :, 63:64].to_broadcast([32, 32, 64])), reads=[gc8], writes=[egl8])
        P.op('act', lambda e: e.activation(out=egl8[:], in_=egl8[:], func=AF.Exp), reads=[egl8], writes=[egl8])
        for qi_, (nm, src) in enumerate((('beta', beta8), ('e', e8), ('ed', ed8), ('egl', egl8))):
            t = SC[nm]
            for half in range(2):
                ps = psum[2 + half]
                for c in range(16):
                    n = 16 * half + c
                    P.op('pe', lambda e, ps=ps, c=c, n=n, src=src: e.transpose(ps[0:64, 32 * c:32 * c + 32], src[:, 64 * n:64 * n + 64], ident[0:32, 0:32]),
                         reads=[src, self.cst], writes=[ps])
                P.op('act', lambda e, ps=ps, t=t, half=half: e.copy(out=t[:, 16 * half:16 * half + 16, :], in_=ps[0:64, :].rearrange('p (n r) -> p n r', r=32)[:, :, 0:8]),
                     reads=[ps], writes=[t])
        P.op('dve', lambda e: e.tensor_tensor(out=be[:], in0=SC['beta'][:, :, 4:8], in1=SC['e'][:, :, 0:4], op=ALU.mult), reads=[SC['beta'], SC['e']], writes=[be])

    def _A_rest(self, l, WA, e8, SC, be, ones, gn, cw):
        P = self.P
        self.dump('e8', e8, e8[:], [32, S])
        for nm in SC:
            self.dump('sc_' + nm, SC[nm], SC[nm][:], [64, 32, 8])
        self.dump('be', be, be[:], [64, 32, 4])
        psum = self.psum
        xT = self.xT
        ident = self.c('ident')
        for h in range(4):
          with P.scope():
            xs = [P.sb([64, S], F32, 'xA') for _ in range(3)]
            ys = [P.sb([64, S], F32, 'yA') for _ in range(3)]
            gs = P.sb([64, S], F32, 'gsA')
            for i in range(3):
                self.proj_fm(WA, 256 * i + 64 * h, 64, lambda ps, tb, i=i: P.op('act', lambda e: e.copy(out=xs[i][:, 512 * tb:512 * tb + 512], in_=ps[0:64, :]),
                                                                               reads=[ps], writes=[xs[i]]))
            self.proj_fm(WA, 776 + 64 * h, 64, lambda ps, tb: P.op('act', lambda e: e.activation(out=gs[:, 512 * tb:512 * tb + 512], in_=ps[0:64, :], func=AF.Silu),
                                                                   reads=[ps], writes=[gs]))
            P.dma('sp', self.gsd[h], gs[:], reads=[gs], writes=[self.gsd])
            for i in range(3):
                x = xs[i]; y = ys[i]; blk = 4 * i + h
                P.op('dve', lambda e, x=x, y=y, blk=blk: e.tensor_scalar(out=y[:], in0=x[:], scalar1=cw[:, blk, 3:4], scalar2=None, op0=ALU.mult), reads=[x, cw], writes=[y])
                for sft in (1, 2, 3):
                    P.op('dve', lambda e, x=x, y=y, blk=blk, sft=sft: e.scalar_tensor_tensor(
                        out=y[:, sft:S], in0=x[:, 0:S - sft], scalar=cw[:, blk, 3 - sft:4 - sft], in1=y[:, sft:S], op0=ALU.mult, op1=ALU.add),
                        reads=[x, y, cw], writes=[y])
                P.op('act', lambda e, y=y: e.activation(out=y[:], in_=y[:], func=AF.Silu), reads=[y], writes=[y])
            sq = P.sb([64, 512], BF16, 'sqA'); rs = P.sb([64, 512], F32, 'rsA')
            for i in range(2):
                y = ys[i]
                for tb in range(4):
                    sl = slice(512 * tb, 512 * tb + 512)
                    P.op('act', lambda e, y=y, sl=sl: e.activation(out=sq[:], in_=y[:, sl], func=AF.Square), reads=[y], writes=[sq])
                    ps = psum[self._pn % 2]; self._pn += 1
                    P.op('pe', lambda e, ps=ps: e.matmul(ps[0:64, :], lhsT=ones[0:64, 0:64], rhs=sq[:], start=True, stop=True), reads=[ones, sq], writes=[ps])
                    P.op('act', lambda e, ps=ps: e.activation(out=rs[:], in_=ps[0:64, :], func=AF.Ln, bias=EPS), reads=[ps], writes=[rs])
                    P.op('act', lambda e: e.activation(out=rs[:], in_=rs[:], func=AF.Exp, scale=-0.5), reads=[rs], writes=[rs])
                    P.op('dve', lambda e, y=y, sl=sl, i=i: e.scalar_tensor_tensor(out=y[:, sl], in0=y[:, sl], scalar=(0.125 if i == 0 else 1.0), in1=rs[:],
                                                                                 op0=ALU.mult, op1=ALU.mult), reads=[y, rs], writes=[y])
            qn, kn, vv = ys
            stg = [P.sb([64, 32, 64], F32, 'stgA') for _ in range(2)]
            for gi, (lh, dst) in enumerate(((kn, self.kkd), (qn, self.qkd))):
                st = stg[gi]
                for g8 in range(4):
                    ps = psum[2 + (g8 % 2)]
                    for c in range(8):
                        n = 8 * g8 + c
                        csl = slice(64 * n, 64 * n + 64)
                        P.op('pe', lambda e, ps=ps, c=c, csl=csl, lh=lh: e.matmul(ps[0:64, 64 * c:64 * c + 64], lhsT=lh[:, csl], rhs=kn[:, csl], start=True, stop=True),
                             reads=[lh, kn], writes=[ps])
                    P.op('act', lambda e, ps=ps, st=st, g8=g8: e.copy(out=st[:, 8 * g8:8 * g8 + 8, :].rearrange('p n j -> p (n j)'), in_=ps[0:64, :]), reads=[ps], writes=[st])
                P.dma('sp', dst[32 * h:32 * h + 32].rearrange('n i j -> i n j'), st[:], reads=[st], writes=[dst])
            sel = P.sb([32, 64], F32, 'selA')
            P.op('pool', lambda e: e.memset(sel[:], 0.0), writes=[sel])
            P.op('pool', lambda e, h=h: e.affine_select(out=sel[:], in_=ones_f(P, self)[:], pattern=[[0, 64]], compare_op=ALU.is_equal, fill=0.0,
                                                        base=-h, channel_multiplier=1), reads=[sel], writes=[sel])
            qe = xs[0]
            for tb in range(4):
                sl = slice(512 * tb, 512 * tb + 512)
                ps = psum[self._pn % 2]; self._pn += 1
                P.op('pe', lambda e, ps=ps, sl=sl: e.matmul(ps[0:64, :], lhsT=sel[:], rhs=e8[:, sl], start=True, stop=True), reads=[sel, e8], writes=[ps])
                P.op('dve', lambda e, ps=ps, sl=sl: e.tensor_tensor(out=qe[:, sl], in0=qn[:, sl], in1=ps[0:64, :], op=ALU.mult), reads=[qn, ps], writes=[qe])
            P.dma('sp', self.qed[h], qe[:], reads=[qe], writes=[self.qed])
            tk = [P.sb([64, 32, 64], F32, 'tkA') for _ in range(3)]
            for g8 in range(4):
                for si, src in enumerate((kn, vv)):
                    ps = psum[4 + si]
                    for c in range(8):
                        n = 8 * g8 + c
                        P.op('pe', lambda e, ps=ps, c=c, n=n, src=src: e.transpose(ps[0:64, 64 * c:64 * c + 64], src[:, 64 * n:64 * n + 64], ident[0:64, 0:64]),
                             reads=[src, self.cst], writes=[ps])
                    p3 = ps[0:64, :].rearrange('p (n d) -> p n d', d=64)
                    nsl = slice(8 * g8, 8 * g8 + 8)
                    if si == 0:
                        P.op('dve', lambda e, p3=p3, nsl=nsl, h=h: e.tensor_tensor(out=tk[0][:, nsl, :], in0=p3, in1=be[:, nsl, h:h + 1].to_broadcast([64, 8, 64]), op=ALU.mult),
                             reads=[ps, be], writes=[tk[0]])
                        P.op('dve', lambda e, p3=p3, nsl=nsl, h=h: e.tensor_tensor(out=tk[1][:, nsl, :], in0=p3, in1=SC['ed'][:, nsl, h:h + 1].to_broadcast([64, 8, 64]), op=ALU.mult),
                             reads=[ps, SC['ed']], writes=[tk[1]])
                    else:
                        P.op('dve', lambda e, p3=p3, nsl=nsl, h=h: e.tensor_tensor(out=tk[2][:, nsl, :], in0=p3, in1=SC['beta'][:, nsl, 4 + h:5 + h].to_broadcast([64, 8, 64]), op=ALU.mult),
                             reads=[ps, SC['beta']], writes=[tk[2]])
            for i in range(3):
                P.dma('sp', self.tokd[h, i], tk[i][:], reads=[tk[i]], writes=[self.tokd])
        self._A_phase2()
        self._A_phase3(SC, ones, gn)

    def _A_phase2(self):
        P = self.P
        with P.scope():
            KKs = P.sb([128, 64, 64], F32, 'KKs'); QKs = P.sb([128, 64, 64], F32, 'QKs')
            Dm = P.sb([128, 64, 64], F32, 'Dms'); X = P.sb([128, 64, 64], F32, 'Xs'); tmp = P.sb([128, 64, 64], F32, 'tmps')
            gcs = P.sb([128, 64], F32, 'gcs'); bts = P.sb([128, 64], F32, 'bts')
            P.dma('sp', KKs[:], self.kkd[:], reads=[self.kkd], writes=[KKs])
            P.dma('sp', QKs[:], self.qkd[:], reads=[self.qkd], writes=[QKs])
            P.dma('sp', gcs[:], self.rows8d[0, 0:4, :].rearrange('h (n c) -> (h n) c', c=64), reads=[self.rows8d], writes=[gcs])
            P.dma('sp', bts[:], self.rows8d[1, 4:8, :].rearrange('h (n c) -> (h n) c', c=64), reads=[self.rows8d], writes=[bts])
            P.op('dve', lambda e: e.tensor_tensor(out=Dm[:], in0=gcs[:].unsqueeze(2).to_broadcast([128, 64, 64]), in1=gcs[:].unsqueeze(1).to_broadcast([128, 64, 64]),
                                                  op=ALU.subtract), reads=[gcs], writes=[Dm])
            P.op('dve', lambda e: e.tensor_scalar(out=Dm[:], in0=Dm[:], scalar1=0.0, scalar2=None, op0=ALU.min), reads=[Dm], writes=[Dm])
            P.op('act', lambda e: e.activation(out=Dm[:], in_=Dm[:], func=AF.Exp), reads=[Dm], writes=[Dm])
            P.op('dve', lambda e: e.tensor_tensor(out=tmp[:].rearrange('p j i -> p i j'), in0=QKs[:], in1=Dm[:], op=ALU.mult), reads=[QKs, Dm], writes=[tmp])
            P.op('pool', lambda e: e.affine_select(out=tmp[:], in_=tmp[:], pattern=[[-1, 64], [1, 64]], compare_op=ALU.is_ge, fill=0.0, base=0, channel_multiplier=0),
                 reads=[tmp], writes=[tmp])
            P.dma('sp', self.qktd[:], tmp[:], reads=[tmp], writes=[self.qktd])
            P.op('dve', lambda e: e.tensor_tensor(out=KKs[:], in0=KKs[:], in1=Dm[:], op=ALU.mult), reads=[KKs, Dm], writes=[KKs])
            P.op('dve', lambda e: e.tensor_tensor(out=KKs[:], in0=KKs[:], in1=bts[:].unsqueeze(2).to_broadcast([128, 64, 64]), op=ALU.mult), reads=[KKs, bts], writes=[KKs])
            P.op('pool', lambda e: e.affine_select(out=KKs[:], in_=KKs[:], pattern=[[1, 64], [-1, 64]], compare_op=ALU.is_gt, fill=0.0, base=0, channel_multiplier=0),
                 reads=[KKs], writes=[KKs])
            P.op('pool', lambda e: e.memset(X[:], 1.0), writes=[X])
            P.op('pool', lambda e: e.affine_select(out=X[:], in_=X[:], pattern=[[1, 64], [-1, 64]], compare_op=ALU.is_equal, fill=0.0, base=0, channel_multiplier=0),
                 reads=[X], writes=[X])
            tmp2 = QKs
            for b in range(1, 64):
                P.op('dve', lambda e, b=b: e.tensor_tensor(out=tmp2[:, 0:b, 0:b], in0=X[:, 0:b, 0:b], in1=KKs[:, b:b + 1, 0:b].to_broadcast([128, b, b]), op=ALU.mult),
                     reads=[X, KKs, tmp], writes=[tmp2])
                P.op('dve', lambda e, b=b: e.tensor_reduce(out=X[:, 0:b, b:b + 1], in_=tmp2[:, 0:b, 0:b], axis=AX.X, op=ALU.add, negate=True), reads=[tmp2], writes=[X])
            P.dma('sp', self.ttd[:], X[:], reads=[X], writes=[self.ttd])

    def _A_phase3(self, SC, ones, gn):
        P = self.P
        psum = self.psum
        ident = self.c('ident')
        for h in range(4):
          with P.scope():
            TT = P.sb([64, 32, 64], F32, 'TTA'); QKT = P.sb([64, 32, 64], F32, 'QKTA')
            tk = [P.sb([64, 32, 64], F32, 'tk3A') for _ in range(3)]
            qe = P.sb([64, S], F32, 'qe3A'); gs = P.sb([64, S], F32, 'gs3A')
            P.dma('sp', TT[:], self.ttd[32 * h:32 * h + 32].rearrange('n a b -> a n b'), reads=[self.ttd], writes=[TT])
            P.dma('sp', QKT[:], self.qktd[32 * h:32 * h + 32].rearrange('n a b -> a n b'), reads=[self.qktd], writes=[QKT])
            for i in range(3):
                P.dma('sp', tk[i][:], self.tokd[h, i], reads=[self.tokd], writes=[tk[i]])
            P.dma('sp', qe[:], self.qed[h], reads=[self.qed], writes=[qe])
            P.dma('sp', gs[:], self.gsd[h], reads=[self.gsd], writes=[gs])
            kbe, kd, vb = tk
            VK = P.sb([64, 32, 128], F32, 'VKA')
            P.op('pool', lambda e: e.tensor_copy(out=VK[:, :, 0:64], in_=vb[:]), reads=[vb], writes=[VK])
            P.op('pool', lambda e: e.tensor_copy(out=VK[:, :, 64:128], in_=kbe[:]), reads=[kbe, VK], writes=[VK])
            UW = P.sb([64, 32, 128], F32, 'UWA')
            for g4 in range(8):
                ps = psum[2 + g4 % 2]
                for c in range(4):
                    n = 4 * g4 + c
                    P.op('pe', lambda e, ps=ps, c=c, n=n: e.matmul(ps[0:64, 128 * c:128 * c + 128], lhsT=TT[:, n, :], rhs=VK[:, n, :], start=True, stop=True),
                         reads=[TT, VK], writes=[ps])
                P.op('act', lambda e, ps=ps, g4=g4: e.copy(out=UW[:, 4 * g4:4 * g4 + 4, :].rearrange('p n d -> p (n d)'), in_=ps[0:64, :]), reads=[ps], writes=[UW])
            QhT = P.sb([64, S], F32, 'QhTA'); AT = P.sb([64, 32, 64], F32, 'ATA'); Bn = P.sb([64, 32, 64], F32, 'BnA')
            id64 = P.sb([64, 64], F32, 'id64A')
            P.op('pool', lambda e: e.tensor_copy(out=id64[:], in_=ident[0:64, 0:64]), reads=[self.cst], writes=[id64])
            for g8 in range(4):
                ps = psum[4 + g8 % 2]
                for c in range(8):
                    n = 8 * g8 + c
                    P.op('pe', lambda e, ps=ps, c=c, n=n: e.matmul(ps[0:64, 64 * c:64 * c + 64], lhsT=UW[:, n, 64:128], rhs=QKT[:, n, :], start=True, stop=True),
                         reads=[UW, QKT], writes=[ps])
                sl = slice(512 * g8, 512 * g8 + 512)
                P.op('dve', lambda e, ps=ps, sl=sl: e.tensor_tensor(out=QhT[:, sl], in0=qe[:, sl], in1=ps[0:64, :], op=ALU.subtract), reads=[qe, ps], writes=[QhT])
                ps = psum[6]
                for c in range(8):
                    n = 8 * g8 + c
                    P.op('pe', lambda e, ps=ps, c=c, n=n: e.matmul(ps[0:64, 64 * c:64 * c + 64], lhsT=UW[:, n, 64:128], rhs=kd[:, n, :], start=True, stop=True),
                         reads=[UW, kd], writes=[ps])
                for c in range(8):
                    n = 8 * g8 + c
                    P.op('dve', lambda e, ps=ps, c=c, n=n, h=h: e.scalar_tensor_tensor(out=AT[:, n, :], in0=id64[:], scalar=SC['egl'][:, n, h:h + 1], in1=ps[0:64, 64 * c:64 * c + 64],
                                                                                   op0=ALU.mult, op1=ALU.subtract), reads=[id64, SC['egl'], ps], writes=[AT])
                ps = psum[7]
                for c in range(8):
                    n = 8 * g8 + c
                    P.op('pe', lambda e, ps=ps, c=c, n=n: e.matmul(ps[0:64, 64 * c:64 * c + 64], lhsT=kd[:, n, :], rhs=UW[:, n, 0:64], start=True, stop=True),
                         reads=[UW, kd], writes=[ps])
                P.op('act', lambda e, ps=ps, g8=g8: e.copy(out=Bn[:, 8 * g8:8 * g8 + 8, :].rearrange('p n d -> p (n d)'), in_=ps[0:64, :]), reads=[ps], writes=[Bn])
            Sall = P.sb([64, 33, 64], F32, 'SallA')
            P.op('pool', lambda e: e.memset(Sall[:, 0, :], 0.0), writes=[Sall.sub(0)])
            for n in range(32):
                ps = psum[2 + n % 2]
                P.op('pe', lambda e, ps=ps, n=n: e.matmul(ps[0:64, 0:64], lhsT=AT[:, n, :], rhs=Sall[:, n, :], start=True, stop=True), reads=[AT, Sall.sub(n)], writes=[ps])
                P.op('dve', lambda e, ps=ps, n=n: e.tensor_tensor(out=Sall[:, n + 1, :], in0=Bn[:, n, :], in1=ps[0:64, 0:64], op=ALU.add), reads=[Bn, ps], writes=[Sall.sub(n + 1)])
            oi = [P.sb([64, 512], F32, 'oiA') for _ in range(2)]
            sq = P.sb([64, 512], BF16, 'sq3A'); rs = P.sb([64, 512], F32, 'rs3A')
            ob = [P.sb([64, 512], BF16, 'obA') for _ in range(2)]
            for tb in range(4):
                ps = psum[4 + tb % 2]
                for c in range(8):
                    n = 8 * tb + c
                    P.op('pe', lambda e, ps=ps, c=c, n=n: e.matmul(ps[0:64, 64 * c:64 * c + 64], lhsT=Sall[:, n, :], rhs=QhT[:, 64 * n:64 * n + 64], start=True, stop=False),
                         reads=[Sall.sub(n), QhT], writes=[ps])
                    P.op('pe', lambda e, ps=ps, c=c, n=n: e.matmul(ps[0:64, 64 * c:64 * c + 64], lhsT=UW[:, n, 0:64], rhs=QKT[:, n, :], start=False, stop=True),
                         reads=[UW, QKT], writes=[ps])
                o = oi[tb % 2]
                P.op('act', lambda e, o=o, ps=ps: e.copy(out=o[:], in_=ps[0:64, :]), reads=[ps], writes=[o])
                self.rms_gate_out(o, gs, gn[:, 0:1], ones, h, tb, sq, rs, ob[tb % 2])

    def mixer_C(self, l):
        P = self.P
        psum = self.psum
        xT = self.xT
        self._pn = 0
        WC = P.sb([128, 8, 1024], BF16, 'wC')
        P.dma('pool', WC[:], self.w_in_d[l, :, 1384:2408].rearrange('(k p) n -> p k n', p=128), writes=[WC])
        ones = P.sb([128, 128], BF16, 'onesC')
        P.op('pool', lambda e: e.memset(ones[:], 1.0), writes=[ones])
        zer = P.sb([64, 16], BF16, 'zerC')
        P.op('pool', lambda e: e.memset(zer[:], 0.0), writes=[zer])
        rmask = P.sb([64, S], F32, 'rmaskC')
        P.dma('sp', rmask[:], self.rmask_d[0:1, :].partition_broadcast(64), writes=[rmask])
        gn = P.sb([64, 1], F32, 'gnC')
        P.dma('sp', gn[:], self.hgrn_g_d[l, :].rearrange('(p o) -> p o', o=1), writes=[gn])
        lb = P.sb([64, 4], F32, 'lbC'); oml = P.sb([64, 4], F32, 'omlC'); noml = P.sb([64, 4], F32, 'nomlC')
        if l == 0:
            P.op('pool', lambda e: e.memset(lb[:], 0.0), writes=[lb])
        else:
            z = P.sb([64, 2, 4], F32, 'zC')
            for li in range(2):
                for hh in range(4):
                    P.dma('sp', z[:, li, hh:hh + 1], self.hgrn_lb_d[li, 64 * hh:64 * hh + 64].rearrange('(c o) -> c o', o=1), reads=[z], writes=[z])
            P.op('dve', lambda e: e.tensor_tensor(out=lb[:], in0=z[:, 1, :], in1=z[:, 0, :], op=ALU.subtract), reads=[z], writes=[lb])
            P.op('act', lambda e: e.activation(out=lb[:], in_=lb[:], func=AF.Sigmoid), reads=[lb], writes=[lb])
        P.op('dve', lambda e: e.tensor_scalar(out=oml[:], in0=lb[:], scalar1=-1.0, scalar2=1.0, op0=ALU.mult, op1=ALU.add), reads=[lb], writes=[oml])
        P.op('dve', lambda e: e.tensor_scalar(out=noml[:], in0=oml[:], scalar1=-1.0, scalar2=None, op0=ALU.mult), reads=[oml], writes=[noml])
        m16 = P.sb([128, 128], F32, 'm16')
        P.op('pool', lambda e: e.tensor_copy(out=m16[:], in_=self.c('mask16')), reads=[self.cst], writes=[m16])
        cm8 = P.sb([128, 8], BF16, 'cm8')
        P.op('pool', lambda e: e.tensor_copy(out=cm8[:], in_=self.c('cm8')), reads=[self.cst], writes=[cm8])
        Vtok = P.sb([128, 16, 256], BF16, 'VtokC')
        for tt in range(16):
            ps = psum[self._pn % 2]; self._pn += 1
            for kc in range(8):
                P.op('pe', lambda e, ps=ps, kc=kc, tt=tt: e.matmul(ps[:, 0:256], lhsT=xT[:, kc, 128 * tt:128 * tt + 128], rhs=WC[:, kc, 512:768],
                                                                   start=(kc == 0), stop=(kc == 7)), reads=[WC, xT.sub(tt // 4)], writes=[ps])
            P.op('act', lambda e, ps=ps, tt=tt: e.copy(out=Vtok[:, tt, :], in_=ps[:, 0:256]), reads=[ps], writes=[Vtok])
        gs = P.sb([64, S], F32, 'gsC')
        qt = P.sb([64, S], BF16, 'qtC'); kt = P.sb([64, S], BF16, 'ktC')
        adec = P.sb([64, 128], F32, 'adecC')
        Bst = P.sb([64, 64, 128], F32, 'BstC')
        kdt = [P.sb([128, 64], BF16, 'kdtC') for _ in range(2)]
        vex = [P.sb([128, 8, 64], BF16, 'vexC') for _ in range(2)]
        sm = [P.sb([128, 4, 128], BF16, 'smC') for _ in range(2)]
        oi = [P.sb([64, 512], F32, 'oiC') for _ in range(2)]
        sq = P.sb([64, 512], BF16, 'sqC'); rs = P.sb([64, 512], F32, 'rsC')
        ob = [P.sb([64, 512], BF16, 'obC') for _ in range(2)]
        ident = self.c('ident')
        for h in range(4):
          with P.scope():
            qT = P.sb([64, S], F32, 'qTC'); sg = P.sb([64, S], F32, 'sgC')
            kk = P.sb([64, S], F32, 'kkC'); cum = P.sb([64, S], F32, 'cumC'); ex = P.sb([64, S], F32, 'exC')
            kd = P.sb([64, S], F32, 'kdC')
            self.proj_fm(WC, 64 * h, 64, lambda ps, tb: P.op('act', lambda e: e.copy(out=qT[:, 512 * tb:512 * tb + 512], in_=ps[0:64, :]),
                                                             reads=[ps], writes=[qT.sub(tb)]))
            self.proj_fm(WC, 256 + 64 * h, 64, lambda ps, tb: P.op('act', lambda e: e.activation(out=sg[:, 512 * tb:512 * tb + 512], in_=ps[0:64, :], func=AF.Sigmoid),
                                                                   reads=[ps], writes=[sg]))
            self.proj_fm(WC, 768 + 64 * h, 64, lambda ps, tb: P.op('act', lambda e: e.activation(out=gs[:, 512 * tb:512 * tb + 512], in_=ps[0:64, :], func=AF.Silu),
                                                                   reads=[ps], writes=[gs.sub(tb)]))
            P.op('dve', lambda e, h=h: e.tensor_scalar(out=kk[:], in0=sg[:], scalar1=noml[:, h:h + 1], scalar2=oml[:, h:h + 1], op0=ALU.mult, op1=ALU.add),
                 reads=[sg, noml, oml], writes=[kk])
            P.op('dve', lambda e, h=h: e.tensor_scalar(out=sg[:], in0=sg[:], scalar1=oml[:, h:h + 1], scalar2=lb[:, h:h + 1], op0=ALU.mult, op1=ALU.add),
                 reads=[sg, oml, lb], writes=[sg])
            P.op('act', lambda e: e.activation(out=sg[:], in_=sg[:], func=AF.Ln), reads=[sg], writes=[sg])
            P.op('dve', lambda e: e.tensor_tensor_scan(out=cum[:], data0=rmask[:], data1=sg[:], initial=0.0, op0=ALU.mult, op1=ALU.add),
                 reads=[rmask, sg], writes=[cum])
            P.op('act', lambda e: e.activation(out=ex[:], in_=cum[:], func=AF.Exp), reads=[cum], writes=[ex])
            P.op('dve', lambda e: e.tensor_tensor(out=qt[:], in0=qT[:], in1=ex[:], op=ALU.mult), reads=[qT.sub(0), qT.sub(1), qT.sub(2), qT.sub(3), ex], writes=[qt])
            P.op('act', lambda e: e.activation(out=ex[:], in_=cum[:], func=AF.Exp, scale=-1.0), reads=[cum], writes=[ex])
            P.op('dve', lambda e: e.tensor_tensor(out=kt[:], in0=kk[:], in1=ex[:], op=ALU.mult), reads=[kk, ex], writes=[kt])
            cum3 = cum[:].rearrange('p (c j) -> p c j', j=16)
            cl = cum3[:, :, 15:16]
            P.op('dve', lambda e, cl=cl, cum3=cum3: e.tensor_tensor(out=ex[:].rearrange('p (c j) -> p c j', j=16), in0=cl.to_broadcast([64, 128, 16]), in1=cum3, op=ALU.subtract),
                 reads=[cum], writes=[ex])
            P.op('act', lambda e: e.activation(out=ex[:], in_=ex[:], func=AF.Exp), reads=[ex], writes=[ex])
            P.op('dve', lambda e: e.tensor_tensor(out=kd[:], in0=kk[:], in1=ex[:], op=ALU.mult), reads=[kk, ex], writes=[kd])
            P.op('act', lambda e, cl=cl: e.activation(out=adec[:].rearrange('p (c o) -> p c o', o=1), in_=cl, func=AF.Exp), reads=[cum], writes=[adec])
            P.op('pool', lambda e: e.memset(adec[:, 0:1], 0.0), reads=[adec], writes=[adec])
            for tt in range(16):
                tsl = slice(128 * tt, 128 * tt + 128)
                pt = psum[2 + (tt % 2)]
                kdtt = kdt[tt % 2]; vx = vex[tt % 2]
                P.op('pe', lambda e, pt=pt, tsl=tsl: e.transpose(pt[:, 0:64], kd[:, tsl], ident[0:64, 0:64]), reads=[kd, self.cst], writes=[pt])
                P.op('act', lambda e, pt=pt, kdtt=kdtt: e.copy(out=kdtt[:], in_=pt[:, 0:64]), reads=[pt], writes=[kdtt])
                P.op('dve', lambda e, vx=vx, tt=tt, h=h: e.tensor_tensor(
                    out=vx[:], in0=Vtok[:, tt, 64 * h:64 * h + 64].unsqueeze(1).to_broadcast([128, 8, 64]),
                    in1=cm8[:].unsqueeze(2).to_broadcast([128, 8, 64]), op=ALU.mult), reads=[Vtok, cm8], writes=[vx])
                pb = psum[4 + (tt % 2)]
                P.op('pe', lambda e, pb=pb, kdtt=kdtt, vx=vx: e.matmul(pb[0:64, :], lhsT=kdtt[:], rhs=vx[:].rearrange('p n v -> p (n v)'), start=True, stop=True),
                     reads=[kdtt, vx], writes=[pb])
                P.op('act', lambda e, pb=pb, tt=tt: e.copy(out=Bst[:, :, 8 * tt:8 * tt + 8], in_=pb[0:64, :].rearrange('p (n v) -> p v n', v=64)),
                     reads=[pb], writes=[Bst])
          with P.scope():
            Sst = P.sb([64, 64, 128], F32, 'SstC'); Sb = P.sb([64, 64, 128], BF16, 'SbC')
            for v in range(64):
                P.op('dve', lambda e, v=v: e.tensor_tensor_scan(out=Sst[:, v, :], data0=adec[:], data1=Bst[:, v, :], initial=0.0,
                                                                                         op0=ALU.mult, op1=ALU.add), reads=[adec, Bst], writes=[Sst.sub(v)])
            P.op('act', lambda e: e.copy(out=Sb[:], in_=Sst[:]), reads=[Sst.sub(v) for v in range(64)], writes=[Sb])
            for tb in range(4):
                pst = psum[2 + (tb % 2)]
                smt = sm[tb % 2]
                for t4 in range(4):
                    tsl = slice(512 * tb + 128 * t4, 512 * tb + 128 * t4 + 128)
                    P.op('pe', lambda e, pst=pst, t4=t4, tsl=tsl: e.matmul(pst[:, 128 * t4:128 * t4 + 128], lhsT=kt[:, tsl], rhs=qt[:, tsl], start=True, stop=True),
                         reads=[kt, qt], writes=[pst])
                P.op('dve', lambda e, pst=pst, smt=smt: e.tensor_tensor(out=smt[:], in0=pst[:].rearrange('p (a t) -> p a t', a=4),
                                                                       in1=m16[:].unsqueeze(1).to_broadcast([128, 4, 128]), op=ALU.mult),
                     reads=[pst, m16], writes=[smt])
                po = psum[6]; po2 = psum[7]
                for t4 in range(4):
                    tt = 4 * tb + t4
                    P.op('pe', lambda e, t4=t4, tt=tt, smt=smt, h=h: e.matmul(po[0:64, 128 * t4:128 * t4 + 128], lhsT=Vtok[:, tt, 64 * h:64 * h + 64], rhs=smt[:, t4, :],
                                                                              start=True, stop=True), reads=[Vtok, smt], writes=[po])
                for c in range(32):
                    n = 32 * tb + c
                    if n == 0:
                        P.op('pe', lambda e, c=c: e.matmul(po2[0:64, 16 * c:16 * c + 16], lhsT=Sb[:, :, 0], rhs=zer[:], start=True, stop=True),
                             reads=[Sb, zer], writes=[po2])
                    else:
                        P.op('pe', lambda e, c=c, n=n: e.matmul(po2[0:64, 16 * c:16 * c + 16], lhsT=Sb[:, :, n - 1], rhs=qt[:, 16 * n:16 * n + 16], start=True, stop=True),
                             reads=[Sb, qt], writes=[po2])
                o = oi[tb % 2]
                P.op('act', lambda e, o=o: e.copy(out=o[:], in_=po[0:64, :]), reads=[po], writes=[o])
                P.op('dve', lambda e, o=o: e.tensor_tensor(out=o[:], in0=o[:], in1=po2[0:64, :], op=ALU.add), reads=[o, po2], writes=[o])
                self.rms_gate_out(o, gs, gn[:, 0:1], ones, 8 + h, tb, sq, rs, ob[tb % 2])

    def proj_fm(self, W, c0, M, evac, xsrc=None, nk=8):
        P = self.P
        for tb in range(4):
            ps = self.psum[self._pn % 2]; self._pn += 1
            for kc in range(nk):
                P.op('pe', lambda e, ps=ps, kc=kc, tb=tb: e.matmul(
                    ps[0:M, :], lhsT=W[:, kc, c0:c0 + M], rhs=self.xT[:, kc, 512 * tb:512 * tb + 512],
                    start=(kc == 0), stop=(kc == nk - 1)), reads=[W, self.xT.sub(tb)], writes=[ps])
            evac(ps, tb)

    def mixer_B(self, l):
        P = self.P
        psum = self.psum
        self._pn = 0
        SC = 96 ** -0.5
        WB = P.sb([128, 8, 352], BF16, 'wB')
        P.dma('pool', WB[:], self.w_in_d[l, :, 1032:1384].rearrange('(k p) n -> p k n', p=128), writes=[WB])
        WBr = P.sb([128, 8, 32], BF16, 'wBr')
        P.op('act', lambda e: e.mul(out=WBr[:, :, 0:16], in_=WB[:, :, 336:352], mul=-1.0), reads=[WB], writes=[WBr])
        P.op('act', lambda e: e.copy(out=WBr[:, :, 16:32], in_=WB[:, :, 320:336]), reads=[WB, WBr], writes=[WBr])
        wuqa = P.sb([128, 384], BF16, 'wuqa'); wuqb = P.sb([64, 384], BF16, 'wuqb')
        P.dma('pool', wuqa[:], self.mla_wuq_d[l, 0:128, :], writes=[wuqa])
        P.dma('pool', wuqb[:], self.mla_wuq_d[l, 128:192, :], writes=[wuqb])
        wra = P.sb([128, 4, 32], BF16, 'wra'); wrb = P.sb([64, 4, 32], BF16, 'wrb')
        for (src, dst) in ((wuqa, wra), (wuqb, wrb)):
            v = src[:].rearrange('p (h c) -> p h c', c=96)
            P.op('act', lambda e, v=v, dst=dst: e.mul(out=dst[:, :, 0:16], in_=v[:, :, 80:96], mul=-1.0), reads=[src], writes=[dst])
            P.op('act', lambda e, v=v, dst=dst: e.copy(out=dst[:, :, 16:32], in_=v[:, :, 64:80]), reads=[src, dst], writes=[dst])
        wukv = P.sb([128, 512], BF16, 'wukv')
        P.dma('pool', wukv[:], self.mla_wukv_d[l], writes=[wukv])
        wv = P.sb([128, 4, 64], BF16, 'wv')
        P.op('act', lambda e: e.copy(out=wv[:], in_=wukv[:].rearrange('p (h c) -> p h c', c=128)[:, :, 64:128]), reads=[wukv], writes=[wv])
        gqa = P.sb([128, 1], F32, 'gqa'); gqb = P.sb([64, 1], F32, 'gqb'); gkv = P.sb([128, 1], F32, 'gkv')
        P.dma('sp', gqa[:], self.mla_qg_d[l, 0:128].rearrange('(p o) -> p o', o=1), writes=[gqa])
        P.dma('sp', gqb[:], self.mla_qg_d[l, 128:192].rearrange('(p o) -> p o', o=1), writes=[gqb])
        P.dma('sp', gkv[:], self.mla_kvg_d[l, :].rearrange('(p o) -> p o', o=1), writes=[gkv])
        ones = P.sb([128, 128], BF16, 'onesB')
        P.op('pool', lambda e: e.memset(ones[:], 1.0), writes=[ones])
        cosT = P.sb([32, S], F32, 'cosT'); sinT = P.sb([32, S], F32, 'sinT')
        P.dma('sp', cosT[:], self.rope_d[0], writes=[cosT])
        P.dma('sp', sinT[:], self.rope_d[1], writes=[sinT])
        cqna = P.sb([128, S], BF16, 'cqna'); cqnb = P.sb([64, S], BF16, 'cqnb'); ckvn = P.sb([128, S], BF16, 'ckvn')
        KrT = P.sb([32, S], BF16, 'KrT')
        with P.scope():
            cqa = P.sb([128, S], F32, 'cqa'); cqb = P.sb([64, S], F32, 'cqb'); ckv = P.sb([128, S], F32, 'ckv')
            sqa = P.sb([128, S], BF16, 'sqa'); sqb = P.sb([64, S], BF16, 'sqb'); sqk = P.sb([128, S], BF16, 'sqk')
            for (dst, sq, c0, M) in ((cqa, sqa, 0, 128), (cqb, sqb, 128, 64), (ckv, sqk, 192, 128)):
                def evac(ps, tb, dst=dst, sq=sq, M=M):
                    P.op('act', lambda e: e.copy(out=dst[:, 512 * tb:512 * tb + 512], in_=ps[0:M, :]), reads=[ps], writes=[dst.sub(tb)])
                    P.op('act', lambda e: e.activation(out=sq[:, 512 * tb:512 * tb + 512], in_=ps[0:M, :], func=AF.Square), reads=[ps], writes=[sq.sub(tb)])
                self.proj_fm(WB, c0, M, evac)
            kx = P.sb([32, S], F32, 'kx')
            self.proj_fm(WB, 320, 32, lambda ps, tb: P.op('dve', lambda e: e.tensor_tensor(
                out=kx[:, 512 * tb:512 * tb + 512], in0=ps[0:32, :], in1=cosT[:, 512 * tb:512 * tb + 512], op=ALU.mult),
                reads=[ps, cosT], writes=[kx.sub(tb)]))
            kx2 = P.sb([32, S], F32, 'kx2')
            self.proj_fm(WBr, 0, 32, lambda ps, tb: P.op('dve', lambda e: e.tensor_tensor(
                out=kx2[:, 512 * tb:512 * tb + 512], in0=ps[0:32, :], in1=sinT[:, 512 * tb:512 * tb + 512], op=ALU.mult),
                reads=[ps, sinT], writes=[kx2.sub(tb)]))
            for tb in range(4):
                P.op('dve', lambda e, tb=tb: e.tensor_tensor(out=KrT[:, 512 * tb:512 * tb + 512], in0=kx[:, 512 * tb:512 * tb + 512],
                                                              in1=kx2[:, 512 * tb:512 * tb + 512], op=ALU.add),
                     reads=[kx.sub(tb), kx2.sub(tb)], writes=[KrT.sub(tb)])
            rq = [P.sb([128, 512], F32, 'rq') for _ in range(2)]
            for tb in range(4):
                sl = slice(512 * tb, 512 * tb + 512)
                ps = psum[self._pn % 2]; self._pn += 1
                P.op('pe', lambda e, ps=ps, sl=sl: e.matmul(ps[:], lhsT=ones[:], rhs=sqa[:, sl], start=True, stop=False), reads=[ones, sqa.sub(tb)], writes=[ps])
                P.op('pe', lambda e, ps=ps, sl=sl: e.matmul(ps[:], lhsT=ones[0:64, :], rhs=sqb[:, sl], start=False, stop=True), reads=[ones, sqb.sub(tb)], writes=[ps])
                r = rq[0]
                P.op('act', lambda e, ps=ps, r=r: e.activation(out=r[:], in_=ps[:], func=AF.Ln, scale=1.0 / 192, bias=EPS), reads=[ps], writes=[r])
                P.op('act', lambda e, r=r: e.activation(out=r[:], in_=r[:], func=AF.Exp, scale=-0.5), reads=[r], writes=[r])
                P.op('dve', lambda e, r=r, sl=sl: e.scalar_tensor_tensor(out=cqna[:, sl], in0=cqa[:, sl], scalar=gqa[:, 0:1], in1=r[:], op0=ALU.mult, op1=ALU.mult),
                     reads=[cqa.sub(tb), gqa, r], writes=[cqna.sub(tb)])
                P.op('dve', lambda e, r=r, sl=sl: e.scalar_tensor_tensor(out=cqnb[:, sl], in0=cqb[:, sl], scalar=gqb[:, 0:1], in1=r[0:64, :], op0=ALU.mult, op1=ALU.mult),
                     reads=[cqb.sub(tb), gqb, r], writes=[cqnb.sub(tb)])
                ps = psum[self._pn % 2]; self._pn += 1
                P.op('pe', lambda e, ps=ps, sl=sl: e.matmul(ps[:], lhsT=ones[:], rhs=sqk[:, sl], start=True, stop=True), reads=[ones, sqk.sub(tb)], writes=[ps])
                r = rq[1]
                P.op('act', lambda e, ps=ps, r=r: e.activation(out=r[:], in_=ps[:], func=AF.Ln, scale=1.0 / 128, bias=EPS), reads=[ps], writes=[r])
                P.op('act', lambda e, r=r: e.activation(out=r[:], in_=r[:], func=AF.Exp, scale=-0.5), reads=[r], writes=[r])
                P.op('dve', lambda e, r=r, sl=sl: e.scalar_tensor_tensor(out=ckvn[:, sl], in0=ckv[:, sl], scalar=gkv[:, 0:1], in1=r[:], op0=ALU.mult, op1=ALU.mult),
                     reads=[ckv.sub(tb), gkv, r], writes=[ckvn.sub(tb)])
        QnT = [P.sb([64, S], BF16, 'QnT') for _ in range(4)]
        QrT = [P.sb([32, S], BF16, 'QrT') for _ in range(4)]
        KnT = [P.sb([64, S], BF16, 'KnT') for _ in range(4)]
        Vtok = P.sb([128, 16, 256], BF16, 'VtokB')
        qx = P.sb([32, 512], F32, 'qx'); qx2 = P.sb([32, 512], F32, 'qx2')
        for h in range(4):
            for tb in range(4):
                sl = slice(512 * tb, 512 * tb + 512)
                ps = psum[self._pn % 2]; self._pn += 1
                P.op('pe', lambda e, ps=ps, sl=sl, h=h: e.matmul(ps[0:64, :], lhsT=wuqa[:, 96 * h:96 * h + 64], rhs=cqna[:, sl], start=True, stop=False),
                     reads=[wuqa, cqna.sub(tb)], writes=[ps])
                P.op('pe', lambda e, ps=ps, sl=sl, h=h: e.matmul(ps[0:64, :], lhsT=wuqb[:, 96 * h:96 * h + 64], rhs=cqnb[:, sl], start=False, stop=True),
                     reads=[wuqb, cqnb.sub(tb)], writes=[ps])
                P.op('act', lambda e, ps=ps, sl=sl, h=h: e.copy(out=QnT[h][:, sl], in_=ps[0:64, :]), reads=[ps], writes=[QnT[h].sub(tb)])
                ps = psum[self._pn % 2]; self._pn += 1
                P.op('pe', lambda e, ps=ps, sl=sl, h=h: e.matmul(ps[0:64, :], lhsT=wukv[:, 128 * h:128 * h + 64], rhs=ckvn[:, sl], start=True, stop=True),
                     reads=[wukv, ckvn.sub(tb)], writes=[ps])
                P.op('act', lambda e, ps=ps, sl=sl, h=h: e.copy(out=KnT[h][:, sl], in_=ps[0:64, :]), reads=[ps], writes=[KnT[h].sub(tb)])
                ps = psum[self._pn % 2]; self._pn += 1
                P.op('pe', lambda e, ps=ps, sl=sl, h=h: e.matmul(ps[0:32, :], lhsT=wuqa[:, 96 * h + 64:96 * h + 96], rhs=cqna[:, sl], start=True, stop=False),
                     reads=[wuqa, cqna.sub(tb)], writes=[ps])
                P.op('pe', lambda e, ps=ps, sl=sl, h=h: e.matmul(ps[0:32, :], lhsT=wuqb[:, 96 * h + 64:96 * h + 96], rhs=cqnb[:, sl], start=False, stop=True),
                     reads=[wuqb, cqnb.sub(tb)], writes=[ps])
                P.op('dve', lambda e, ps=ps, sl=sl: e.tensor_tensor(out=qx[:], in0=ps[0:32, :], in1=cosT[:, sl], op=ALU.mult), reads=[ps, cosT], writes=[qx])
                ps = psum[self._pn % 2]; self._pn += 1
                P.op('pe', lambda e, ps=ps, sl=sl, h=h: e.matmul(ps[0:32, :], lhsT=wra[:, h, :], rhs=cqna[:, sl], start=True, stop=False),
                     reads=[wra, cqna.sub(tb)], writes=[ps])
                P.op('pe', lambda e, ps=ps, sl=sl, h=h: e.matmul(ps[0:32, :], lhsT=wrb[:, h, :], rhs=cqnb[:, sl], start=False, stop=True),
                     reads=[wrb, cqnb.sub(tb)], writes=[ps])
                P.op('dve', lambda e, ps=ps, sl=sl: e.tensor_tensor(out=qx2[:], in0=ps[0:32, :], in1=sinT[:, sl], op=ALU.mult), reads=[ps, sinT], writes=[qx2])
                P.op('dve', lambda e, sl=sl, h=h: e.tensor_tensor(out=QrT[h][:, sl], in0=qx[:], in1=qx2[:], op=ALU.add), reads=[qx, qx2], writes=[QrT[h].sub(tb)])
        for tt in range(16):
            ps = psum[self._pn % 2]; self._pn += 1
            P.op('pe', lambda e, ps=ps, tt=tt: e.matmul(ps[:, 0:256], lhsT=ckvn[:, 128 * tt:128 * tt + 128], rhs=wv[:].rearrange('p h c -> p (h c)'),
                                                        start=True, stop=True), reads=[ckvn.sub(tt // 4), wv], writes=[ps])
            P.op('act', lambda e, ps=ps, tt=tt: e.copy(out=Vtok[:, tt, :], in_=ps[:, 0:256]), reads=[ps], writes=[Vtok])
        cm = P.sb([128, 4, 512], BF16, 'cmB')
        P.op('dve', lambda e: e.tensor_copy(out=cm[:], in_=self.c('cmask').rearrange('p (r f) -> p r f', r=4)), reads=[self.cst], writes=[cm])
        pts = [P.sb([128, 512], BF16, 'ptB') for _ in range(4)]
        obs = [P.sb([64, 512], BF16, 'oB') for _ in range(2)]
        rec = P.sb([64, 512], F32, 'recB')
        blocks = []
        nq = 0
        for h in range(4):
            for qb in range(4):
                nkb = 4 * qb + 4
                for kb in range(nkb):
                    blocks.append((h, qb, kb, nkb, nq))
                nq += 1

        def stage1(i):
            (h, qb, kb, nkb, q_) = blocks[i]
            qsl = slice(512 * qb, 512 * qb + 512)
            ksl = slice(128 * kb, 128 * kb + 128)
            r = kb - 4 * qb
            st = psum[i % 4]; pt = pts[i % 4]
            P.op('pe', lambda e: e.matmul(st[:], lhsT=KnT[h][:, ksl], rhs=QnT[h][:, qsl], start=True, stop=False),
                 reads=[KnT[h].sub(kb // 4), QnT[h].sub(qb)], writes=[st])
            P.op('pe', lambda e: e.matmul(st[:], lhsT=KrT[:, ksl], rhs=QrT[h][:, qsl], start=False, stop=True),
                 reads=[KrT.sub(kb // 4), QrT[h].sub(qb)], writes=[st])
            P.op('act', lambda e: e.activation(out=pt[:], in_=st[:], func=AF.Exp, scale=SC), reads=[st], writes=[pt])
            if r >= 0:
                P.op('dve', lambda e: e.tensor_tensor(out=pt[:], in0=pt[:], in1=cm[:, r, :], op=ALU.mult), reads=[pt, cm], writes=[pt])

        def stage2(i):
            (h, qb, kb, nkb, q_) = blocks[i]
            qsl = slice(512 * qb, 512 * qb + 512)
            pt = pts[i % 4]
            pn = psum[4 + 2 * (q_ % 2)]; pd = psum[5 + 2 * (q_ % 2)]
            P.op('pe', lambda e: e.matmul(pn[0:64, :], lhsT=Vtok[:, kb, 64 * h:64 * h + 64], rhs=pt[:], start=(kb == 0), stop=(kb == nkb - 1)),
                 reads=[Vtok, pt], writes=[pn])
            P.op('pe', lambda e: e.matmul(pd[0:64, :], lhsT=ones[:, 0:64], rhs=pt[:], start=(kb == 0), stop=(kb == nkb - 1)),
                 reads=[ones, pt], writes=[pd])
            if kb == nkb - 1:
                ob = obs[q_ % 2]
                P.op('dve', lambda e: e.reciprocal(out=rec[:], in_=pd[0:64, :]), reads=[pd], writes=[rec])
                P.op('dve', lambda e: e.tensor_tensor(out=ob[:], in0=pn[0:64, :], in1=rec[:], op=ALU.mult), reads=[pn, rec], writes=[ob])
                P.dma('sp', self.ocat_d[4 + h, :, qsl], ob[:], reads=[ob], writes=[self.ocat_d])

        SK = 2
        for i in range(min(SK, len(blocks))):
            stage1(i)
        for i in range(len(blocks)):
            if i + SK < len(blocks):
                stage1(i + SK)
            stage2(i)

    def mixer_D(self, l):
        P = self.P
        xT = self.xT
        psum = self.psum
        self._pn = 0
        W = P.sb([128, 8, 768], BF16, 'wD')
        P.dma('pool', W[:], self.w_in_d[l, :, 2408:3176].rearrange('(k p) n -> p k n', p=128), writes=[W])
        ones = P.sb([128, 64], BF16, 'onesD')
        P.op('pool', lambda e: e.memset(ones[:], 1.0), writes=[ones])
        geoms = [(1, 0, 'tri_ge'), (1, 128, 'tri_le'), (4, 0, 'tri_ge'), (4, 128, 'tri_le'), (16, 0, 'tri_ge')]
        tabs = []
        tmpf = P.sb([128, 128], F32, 'tabtmp')
        for (dil, off, mk) in geoms:
            tab = P.sb([128, 4, 128], BF16, 'tabD')
            for h in range(4):
                slope = 2.0 ** (-8.0 * (h + 1) / 4)
                P.op('dve', lambda e, mk=mk: e.tensor_tensor(out=tmpf[:], in0=self.c('dist'), in1=self.c(mk), op=ALU.mult),
                     reads=[self.cst], writes=[tmpf])
                P.op('act', lambda e, dil=dil, off=off, slope=slope: e.activation(
                    out=tmpf[:], in_=tmpf[:], func=AF.Exp, scale=-slope * dil, bias=-slope * dil * off),
                    reads=[tmpf], writes=[tmpf])
                P.op('dve', lambda e, tab=tab, h=h, mk=mk: e.tensor_tensor(out=tab[:, h, :], in0=tmpf[:], in1=self.c(mk), op=ALU.mult),
                     reads=[tmpf, self.cst], writes=[tab])
            tabs.append(tab)
        QT = [P.sb([64, S], BF16, 'qTD') for _ in range(4)]
        KT = [P.sb([64, S], BF16, 'kTD') for _ in range(4)]
        n = 0
        for i in range(8):
            for tb in range(4):
                ps = psum[n % 2]; n += 1
                for kc in range(8):
                    P.op('pe', lambda e, ps=ps, kc=kc, i=i, tb=tb: e.matmul(
                        ps[0:64, :], lhsT=W[:, kc, 64 * i:64 * i + 64], rhs=xT[:, kc, 512 * tb:512 * tb + 512],
                        start=(kc == 0), stop=(kc == 7)), reads=[W, xT.sub(tb)], writes=[ps])
                dst = QT[i] if i < 4 else KT[i - 4]
                P.op('act', lambda e, ps=ps, dst=dst, tb=tb, i=i: e.mul(out=dst[:, 512 * tb:512 * tb + 512], in_=ps[0:64, :],
                                                                       mul=(0.125 if i < 4 else 1.0)), reads=[ps], writes=[dst])
        NUM = P.sb([64, 4, S], F32, 'numD')
        DEN = P.sb([64, 4, S], F32, 'denD')
        Vt = [P.sb([128, 256], BF16, 'vD') for _ in range(3)]
        PC = [P.sb([128, 4, 128], BF16, 'pcD') for _ in range(2)]
        PP = [P.sb([128, 4, 128], BF16, 'ppD') for _ in range(2)]
        units = []
        for nb in range(16):
            units.append((0, slice(128 * nb, 128 * nb + 128), slice(128 * (nb - 1), 128 * nb) if nb > 0 else None, 0, 1))
        for r in range(4):
            for b in range(4):
                units.append((1, slice(512 * b + r, 512 * (b + 1), 4), slice(512 * (b - 1) + r, 512 * b, 4) if b > 0 else None, 2, 3))
        for r in range(16):
            units.append((2, slice(r, S, 16), None, 4, None))
        nv = 0
        vts = {}

        def stage1(u):
            nonlocal n, nv
            (br, Tq, Tp, gc, gp) = units[u]
            vcur = Vt[nv % 3]; nv += 1
            vts[u] = vcur
            ps = psum[n % 2]; n += 1
            for kc in range(8):
                P.op('pe', lambda e, ps=ps, kc=kc, Tq=Tq: e.matmul(
                    ps[:, 0:256], lhsT=xT[:, kc, Tq], rhs=W[:, kc, 512:768], start=(kc == 0), stop=(kc == 7)),
                    reads=[W, xT.sub(0), xT.sub(1), xT.sub(2), xT.sub(3)], writes=[ps])
            P.op('act', lambda e, ps=ps, vcur=vcur: e.copy(out=vcur[:], in_=ps[:, 0:256]), reads=[ps], writes=[vcur])
            sc = psum[2 + 2 * (u % 2)]
            sp_ = psum[3 + 2 * (u % 2)]
            pc = PC[u % 2]; pp = PP[u % 2]
            for h in range(4):
                P.op('pe', lambda e, h=h, sc=sc, Tq=Tq: e.matmul(
                    sc[:, 128 * h:128 * h + 128], lhsT=KT[h][:, Tq], rhs=QT[h][:, Tq], start=True, stop=True),
                    reads=[KT[h], QT[h]], writes=[sc])
            P.op('act', lambda e, sc=sc, pc=pc: e.activation(out=pc[:].rearrange('p h q -> p (h q)'), in_=sc[:], func=AF.Exp),
                 reads=[sc], writes=[pc])
            P.op('dve', lambda e, pc=pc, gc=gc: e.tensor_tensor(out=pc[:], in0=pc[:], in1=tabs[gc][:], op=ALU.mult),
                 reads=[pc, tabs[gc]], writes=[pc])
            if Tp is not None:
                for h in range(4):
                    P.op('pe', lambda e, h=h, sp_=sp_, Tq=Tq, Tp=Tp: e.matmul(
                        sp_[:, 128 * h:128 * h + 128], lhsT=KT[h][:, Tp], rhs=QT[h][:, Tq], start=True, stop=True),
                        reads=[KT[h], QT[h]], writes=[sp_])
                P.op('act', lambda e, sp_=sp_, pp=pp: e.activation(out=pp[:].rearrange('p h q -> p (h q)'), in_=sp_[:], func=AF.Exp),
                     reads=[sp_], writes=[pp])
                P.op('dve', lambda e, pp=pp, gp=gp: e.tensor_tensor(out=pp[:], in0=pp[:], in1=tabs[gp][:], op=ALU.mult),
                     reads=[pp, tabs[gp]], writes=[pp])

        def stage2(u):
            (br, Tq, Tp, gc, gp) = units[u]
            vcur = vts[u]; vprev = vts.get(u - 1)
            pc = PC[u % 2]; pp = PP[u % 2]
            pn = psum[6]; pd = psum[7]
            for h in range(4):
                P.op('pe', lambda e, h=h, pc=pc, vcur=vcur, last=(Tp is None): e.matmul(
                    pn[0:64, 128 * h:128 * h + 128], lhsT=vcur[:, 64 * h:64 * h + 64], rhs=pc[:, h, :], start=True, stop=last),
                    reads=[vcur, pc], writes=[pn])
                if Tp is not None:
                    P.op('pe', lambda e, h=h, pp=pp, vprev=vprev: e.matmul(
                        pn[0:64, 128 * h:128 * h + 128], lhsT=vprev[:, 64 * h:64 * h + 64], rhs=pp[:, h, :], start=False, stop=True),
                        reads=[vprev, pp], writes=[pn])
                P.op('pe', lambda e, h=h, pc=pc, last=(Tp is None): e.matmul(
                    pd[0:64, 128 * h:128 * h + 128], lhsT=ones[:], rhs=pc[:, h, :], start=True, stop=last),
                    reads=[ones, pc], writes=[pd])
                if Tp is not None:
                    P.op('pe', lambda e, h=h, pp=pp: e.matmul(
                        pd[0:64, 128 * h:128 * h + 128], lhsT=ones[:], rhs=pp[:, h, :], start=False, stop=True),
                        reads=[ones, pp], writes=[pd])
            for (acc_t, pt) in ((NUM, pn), (DEN, pd)):
                src = pt[0:64, :].rearrange('p (h q) -> p h q', h=4)
                if br == 0:
                    P.op('act', lambda e, acc_t=acc_t, src=src, Tq=Tq: e.copy(out=acc_t[:, :, Tq], in_=src), reads=[pt], writes=[acc_t])
                else:
                    P.op('dve', lambda e, acc_t=acc_t, src=src, Tq=Tq: e.tensor_tensor(out=acc_t[:, :, Tq], in0=acc_t[:, :, Tq], in1=src, op=ALU.add),
                         reads=[pt, acc_t], writes=[acc_t])

        stage1(0)
        for u in range(len(units)):
            if u + 1 < len(units):
                stage1(u + 1)
            stage2(u)
        ob = [P.sb([64, S], BF16, 'oD') for _ in range(2)]
        for h in range(4):
            o = ob[h % 2]
            P.op('dve', lambda e, h=h: e.reciprocal(out=DEN[:, h, :], in_=DEN[:, h, :]), reads=[DEN], writes=[DEN])
            P.op('dve', lambda e, h=h, o=o: e.tensor_tensor(out=o[:], in0=NUM[:, h, :], in1=DEN[:, h, :], op=ALU.mult), reads=[NUM, DEN], writes=[o])
            P.dma('sp', self.ocat_d[12 + h], o[:], reads=[o], writes=[self.ocat_d])

    def mixer_stub(self, l):
        P = self.P
        W = P.sb([128, 8, 1024], BF16, 'wstub')
        P.dma('pool', W[:], self.w_in_d[l, :, 0:1024].rearrange('(k p) n -> p k n', p=128), writes=[W])
        ob = [P.sb([64, 512], BF16, 'ostub') for _ in range(2)]
        n = 0
        for c in range(16):
            for tb in range(4):
                ps = self.psum[n % 2]
                o = ob[n % 2]
                n += 1
                for kc in range(8):
                    P.op('pe', lambda e, ps=ps, kc=kc, c=c, tb=tb: e.matmul(
                        ps[0:64, :], lhsT=W[:, kc, 64 * c:64 * c + 64], rhs=self.xT[:, kc, 512 * tb:512 * tb + 512],
                        start=(kc == 0), stop=(kc == 7)), reads=[W, self.xT.sub(tb)], writes=[ps])
                P.op('act', lambda e, ps=ps, o=o: e.copy(out=o[:], in_=ps[0:64, :]), reads=[ps], writes=[o])
                P.dma('sp', self.ocat_d[c, :, 512 * tb:512 * tb + 512], o[:], reads=[o], writes=[self.ocat_d])

    def ln_tile(self, r, g_rep, b_rep, out, r_ap=None, r_res=None, slot=None):
        for _ in self.ln_tile_gen(r, g_rep, b_rep, out, r_ap, r_res, slot):
            pass

    def ln_tile_gen(self, r, g_rep, b_rep, out, r_ap=None, r_res=None, slot=None):
        P = self.P
        if r_ap is None:
            r_ap = r[:]
            r_res = r
        rl = list(r_res) if isinstance(r_res, (list, tuple)) else [r_res]
        if slot is None:
            self._lnk += 1
            slot = self._lnk % 2
        st, mv, sc = self._lntmp[slot]
        for h in range(2):
            P.op('dve', lambda e, h=h: e.bn_stats(out=st[:, h, :], in_=r_ap[:, 512 * h:512 * h + 512]), reads=rl, writes=[st])
        P.op('dve', lambda e: e.bn_aggr(out=mv[:], in_=st[:].rearrange('p a b -> p (a b)')), reads=[st], writes=[mv])
        yield
        P.op('act', lambda e: e.activation(out=sc[:, 0:1], in_=mv[:, 1:2], func=AF.Ln, bias=EPS), reads=[mv], writes=[sc])
        P.op('act', lambda e: e.activation(out=sc[:, 0:1], in_=sc[:, 0:1], func=AF.Exp, scale=-0.5), reads=[sc], writes=[sc])
        yield
        P.op('dve', lambda e: e.scalar_tensor_tensor(out=sc[:, 1:2], in0=mv[:, 0:1], scalar=-1.0, in1=sc[:, 0:1],
                                                     op0=ALU.mult, op1=ALU.mult), reads=[mv, sc], writes=[sc])
        yield
        P.op('act', lambda e: e.activation(out=out[:], in_=r_ap, func=AF.Identity, bias=sc[:, 1:2], scale=sc[:, 0:1]),
             reads=rl + [sc], writes=[out])
        yield
        P.op('dve', lambda e: e.tensor_tensor(out=out[:], in0=out[:], in1=g_rep[:], op=ALU.mult), reads=[out, g_rep], writes=[out])
        P.op('dve', lambda e: e.tensor_tensor(out=out[:], in0=out[:], in1=b_rep[:], op=ALU.add), reads=[out, b_rep], writes=[out])
        yield

    def dense(self, l, last):
        P = self.P
        xT = self.xT
        acc = P.sb([128, 16, D], F32, 'acc')
        self._lnk = 0
        self._lntmp = [(P.sb([128, 2, 6], F32, 'bnst'), P.sb([128, 2], F32, 'mv'), P.sb([128, 2], F32, 'lnsc')) for _ in range(2)]
        x1 = [P.sb([128, D], F32, 'x1') for _ in range(2)]
        x_src = self.x_d if l == 0 else self.xres_d
        with P.scope():
            g1 = P.sb([128, D], F32, 'g1'); b1 = P.sb([128, D], F32, 'b1')
            for t, d in ((g1, self.ln1_g_d), (b1, self.ln1_b_d)):
                P.dma('sp', t[:], d[l:l + 1, :].partition_broadcast(128), writes=[t])
            wout = P.sb([64, 16, D], BF16, 'wout')
            P.dma('pool', wout[:], self.w_out_d[l].rearrange('(c p) n -> p c n', p=64), writes=[wout])
            ocs = [P.sb([64, 16, 512], BF16, 'oc') for _ in range(2)]
            xr = [P.sb([128, D], F32, 'xr') for _ in range(2)]
            rr = [P.sb([128, D], F32, 'rr') for _ in range(2)]
            def ln1_gen(tt):
                tb = tt // 4
                oc = ocs[tb % 2]
                xrt = xr[tt % 2]; r = rr[tt % 2]; x1t = x1[tt % 2]
                P.dma('sp', xrt[:], x_src[128 * tt:128 * tt + 128, :], reads=[x_src], writes=[xrt])
                for half in range(2):
                    ps = self.psum[2 * (tt % 2) + half]
                    for c in range(16):
                        P.op('pe', lambda e, ps=ps, c=c, half=half: e.matmul(
                            ps[:], lhsT=oc[:, c, 128 * (tt % 4):128 * (tt % 4) + 128], rhs=wout[:, c, 512 * half:512 * half + 512],
                            start=(c == 0), stop=(c == 15)), reads=[oc, wout], writes=[ps])
                yield
                for half in range(2):
                    ps = self.psum[2 * (tt % 2) + half]
                    P.op('dve', lambda e, ps=ps, half=half: e.scalar_tensor_tensor(
                        out=r[:, 512 * half:512 * half + 512], in0=xrt[:, 512 * half:512 * half + 512], scalar=ALPHA, in1=ps[:],
                        op0=ALU.mult, op1=ALU.add), reads=[ps, xrt], writes=[r])
                yield
                yield from self.ln_tile_gen(r, g1, b1, x1t, slot=tt % 2)
                P.op('act', lambda e: e.mul(out=acc[:, tt, :], in_=x1t[:], mul=ALPHA), reads=[x1t], writes=[acc.sub((tt, 0)), acc.sub((tt, 1))])
                self.transpose_tile_to_xT(x1t, tt, pbase=4 + 2 * (tt % 2))
                yield

            for tp in range(8):
                if tp % 2 == 0:
                    tb = tp // 2
                    P.dma('sp', ocs[tb % 2][:], self.ocat_d[:, :, 512 * tb:512 * tb + 512].rearrange('c p t -> p c t'),
                          reads=[self.ocat_d], writes=[ocs[tb % 2]])
                lockstep([ln1_gen(2 * tp), ln1_gen(2 * tp + 1)])
        with P.scope():
            g2 = P.sb([128, D], F32, 'g2'); b2 = P.sb([128, D], F32, 'b2')
            for t, d in ((g2, self.ln2_g_d), (b2, self.ln2_b_d)):
                P.dma('sp', t[:], d[l:l + 1, :].partition_broadcast(128), writes=[t])
            HC = 1024
            nhc = D_FF // HC
            NM = HC // 128
            w1s = [P.sb([128, 8, HC], BF16, 'w1') for _ in range(2)]
            w2s = [P.sb([128, NM, D], BF16, 'w2') for _ in range(2)]
            fts = [P.sb([128, NM, 512], BF16, 'fT') for _ in range(2)]
            nf = 0
            n1 = 0
            n2 = 0
            for j in range(nhc):
                w1 = w1s[j % 2]; w2 = w2s[j % 2]
                P.dma('pool', w1[:], self.w_ff1_d[l, :, HC * j:HC * j + HC].rearrange('(k p) n -> p k n', p=128), writes=[w1])
                P.dma('pool', w2[:], self.w_ff2_d[l, HC * j:HC * j + HC, :].rearrange('(m p) n -> p m n', p=128), writes=[w2])
                for tb in range(4):
                    ft = fts[nf % 2]; nf += 1
                    for m in range(NM):
                        ps = self.psum[n1 % 3]; n1 += 1
                        for kc in range(8):
                            P.op('pe', lambda e, ps=ps, kc=kc, m=m, w1=w1, tb=tb: e.matmul(
                                ps[:], lhsT=w1[:, kc, 128 * m:128 * m + 128], rhs=xT[:, kc, 512 * tb:512 * tb + 512],
                                start=(kc == 0), stop=(kc == 7)), reads=[w1, xT.sub(tb)], writes=[ps])
                        P.op('act', lambda e, ps=ps, m=m, ft=ft: e.activation(out=ft[:, m, :], in_=ps[:], func=AF.Relu),
                             reads=[ps], writes=[ft.sub(m)])
                        P.op('act', lambda e, m=m, ft=ft: e.activation(out=ft[:, m, :], in_=ft[:, m, :], func=AF.Square),
                             reads=[ft.sub(m)], writes=[ft.sub(m)])
                    for t4 in range(4):
                        tt = 4 * tb + t4
                        for half in range(2):
                            ps = self.psum[3 + (n2 % 5)]; n2 += 1
                            for m in range(NM):
                                P.op('pe', lambda e, ps=ps, m=m, ft=ft, t4=t4, w2=w2, half=half: e.matmul(
                                    ps[:], lhsT=ft[:, m, 128 * t4:128 * t4 + 128], rhs=w2[:, m, 512 * half:512 * half + 512],
                                    start=(m == 0), stop=(m == NM - 1)), reads=[ft.sub(m), w2], writes=[ps])
                            P.op('dve', lambda e, ps=ps, tt=tt, half=half: e.tensor_tensor(
                                out=acc[:, tt, 512 * half:512 * half + 512], in0=acc[:, tt, 512 * half:512 * half + 512], in1=ps[:], op=ALU.add),
                                reads=[ps, acc.sub((tt, half))], writes=[acc.sub((tt, half))])
            def ln2_gen(tt):
                x2 = x1[tt % 2]
                yield from self.ln_tile_gen(None, g2, b2, x2, r_ap=acc[:, tt, :], r_res=[acc.sub((tt, 0)), acc.sub((tt, 1))], slot=tt % 2)
                if last:
                    P.dma('sp', self.y_d[128 * tt:128 * tt + 128, :], x2[:], reads=[x2], writes=[self.y_d])
                else:
                    P.dma('sp', self.xres_d[128 * tt:128 * tt + 128, :], x2[:], reads=[x2], writes=[self.xres_d])
                    self.transpose_tile_to_xT(x2, tt, pbase=4 + 2 * (tt % 2))
                yield

            for tp in range(8):
                lockstep([ln2_gen(2 * tp), ln2_gen(2 * tp + 1)])


def lockstep(gens):
    gens = list(gens)
    while gens:
        nxt = []
        for g in gens:
            try:
                next(g)
                nxt.append(g)
            except StopIteration:
                pass
        gens = nxt


def _rope_table():
    half = 16
    inv_freq = (np.float32(10000.0) ** (-np.arange(half, dtype=np.float32) / np.float32(half))).astype(np.float32)
    ang = (np.arange(S, dtype=np.float32)[None, :] * inv_freq[:, None]).astype(np.float32)
    ang = np.concatenate([ang, ang], axis=0)
    return np.stack([np.cos(ang), np.sin(ang)]).astype(np.float32)


ROPE_TAB = _rope_table()
RMASK = np.stack([(np.arange(S) % 16 != 0), (np.arange(S) % 64 != 0)]).astype(np.float32)
_CACHE = {}


def get_builder(stage='full', nlayers=DEPTH):
    key = (stage, nlayers)
    if key not in _CACHE:
        _CACHE[key] = Builder(stage, nlayers)
    return _CACHE[key]


def make_in_maps(b, inputs, cores):
    maps = []
    for ci in cores:
        m = {'x': np.ascontiguousarray(inputs['x'][ci]), 'cst': b.cst_np, 'rope': ROPE_TAB, 'rmask': RMASK}
        for k in ('w_in', 'w_out', 'w_ff1', 'w_ff2', 'ln1_g', 'ln1_b', 'ln2_g', 'ln2_b',
                  'mla_q_norm_g', 'mla_kv_norm_g', 'mla_w_uq', 'mla_w_ukv', 'hgrn_lb_logits', 'hgrn_norm_g',
                  'gdn_conv_w', 'gdn_a_log', 'gdn_dt_bias', 'gdn_norm_g'):
            m[k] = np.ascontiguousarray(inputs[k])
        maps.append(m)
    return maps


def kernel(**inputs):
    inputs = {k: np.asarray(v) for k, v in inputs.items()}
    b = get_builder('full')
    maps = make_in_maps(b, inputs, list(range(NCORES)))
    res = run_bass_kernel_spmd(b.nc, maps, core_ids=list(range(NCORES)))
    return np.stack([r['y'] for r in res.results], axis=0).astype(np.float32)
```

```python
import math
import numpy as np
import concourse.bass as bass
import concourse.mybir as mybir
from concourse.bass_utils import run_bass_kernel_spmd
from contextlib import ExitStack, contextmanager

F32 = mybir.dt.float32
BF16 = mybir.dt.bfloat16
AF = mybir.ActivationFunctionType
ALU = mybir.AluOpType
AX = mybir.AxisListType

COMPUTE = ('pe', 'act', 'dve', 'pool')
ENGS = ('pe', 'act', 'dve', 'pool', 'sp')
N_DMA_SEMS = 6

S = 2048
D = 1024
NCORES = 8
DEPTH = 2
D_IN = 3176
D_FF = 4096
ALPHA = (2 * DEPTH) ** 0.25
EPS = 1e-6


class Res:
    __slots__ = ('name', 'w', 'r')

    def __init__(self, name='', barrier=None):
        self.name = name
        self.w = None
        self.r = dict(barrier) if barrier else {}


class T:
    def __init__(self, handle, name, barrier=None):
        self.h = handle
        self.name = name
        self.res = Res(name, barrier)
        self.subs = {}
        self._barrier = barrier

    def __getitem__(self, k):
        return self.h[k]

    def sub(self, key):
        r = self.subs.get(key)
        if r is None:
            r = self.subs[key] = Res(f'{self.name}.{key}', self._barrier)
        return r

    def all_res(self):
        return [self.res] + list(self.subs.values())


def _res(x):
    return x.res if isinstance(x, T) else x


def ones_f(P, b):
    t = getattr(b, '_ones_f', None)
    if t is None:
        raise RuntimeError('ones_f not allocated')
    return t


def _mx(d, k, v):
    if v > d.get(k, -1):
        d[k] = v


class Prog:
    def __init__(self, nc):
        self.nc = nc
        self.stack = ExitStack()
        self.ins = {e: [] for e in ENGS}
        self.nalloc = 0
        self.barrier = {}
        self.scopes = []
        self.dma_k = {e: 0 for e in ENGS}
        self.dma_cnt = {}

    def _reg(self, t):
        if self.scopes:
            self.scopes[-1][1].append(t)
        return t

    def sb(self, shape, dt=F32, name=None):
        self.nalloc += 1
        name = f'{name or "t"}{self.nalloc}'
        st = self.scopes[-1][0] if self.scopes else self.stack
        h = st.enter_context(self.nc.sbuf_tensor(name, list(shape), dt))
        return self._reg(T(h, name, dict(self.barrier)))

    def ps(self, shape, dt=F32, name=None):
        self.nalloc += 1
        name = f'{name or "p"}{self.nalloc}'
        h = self.stack.enter_context(self.nc.psum_tensor(name, list(shape), dt))
        return T(h, name)

    def dram(self, shape, dt=F32, name=None, kind='Internal'):
        self.nalloc += 1
        name = name or f'd{self.nalloc}'
        h = self.nc.dram_tensor(name, list(shape), dt, kind=kind)
        return T(h.ap(), name)

    @contextmanager
    def scope(self):
        st = ExitStack()
        tiles = []
        self.scopes.append((st, tiles))
        try:
            yield
        finally:
            self.scopes.pop()
            st.close()
            b = self.barrier
            for t in tiles:
                for r in t.all_res():
                    if r.w is not None:
                        _mx(b, r.w[0], r.w[1])
                    for k, v in r.r.items():
                        _mx(b, k, v)

    def op(self, eng, fn, reads=(), writes=(), dma=False):
        idx = len(self.ins[eng])
        deps = {}
        for r in reads:
            r = _res(r)
            if r.w is not None:
                _mx(deps, r.w[0], r.w[1])
        for w in writes:
            w = _res(w)
            if w.w is not None:
                _mx(deps, w.w[0], w.w[1])
            for k, v in w.r.items():
                _mx(deps, k, v)
        rec = dict(fn=fn, deps=deps, dma=dma, inc=False)
        if dma:
            key = ('d', eng, self.dma_k[eng] % N_DMA_SEMS)
            self.dma_k[eng] += 1
            prev = self.dma_cnt.get(key, 0)
            if prev:
                _mx(deps, key, prev)
            self.dma_cnt[key] = prev + 16
            tok = (key, prev + 16)
            rec['dsem'] = key
        else:
            tok = (('c', eng), idx)
        self.ins[eng].append(rec)
        for r in reads:
            _mx(_res(r).r, tok[0], tok[1])
        for w in writes:
            w = _res(w)
            w.w = tok
            w.r = {}
        return tok

    def dma(self, eng, out, in_, reads=(), writes=(), **kw):
        return self.op(eng, lambda e: e.dma_start(out=out, in_=in_, **kw), reads=reads, writes=writes, dma=True)

    def emit(self):
        nc = self.nc
        ins = self.ins
        for E in ENGS:
            seen = {}
            for idx, rec in enumerate(ins[E]):
                waits = []
                for k, val in rec['deps'].items():
                    if k == ('c', 'pe') and E == 'pe':
                        continue
                    if k == ('c', E) and val >= idx:
                        continue
                    if val <= seen.get(k, -1):
                        continue
                    seen[k] = val
                    waits.append((k, val))
                    if k[0] == 'c':
                        ins[k[1]][val]['inc'] = True
                rec['waits'] = waits
        for e in ENGS:
            c = 0
            for rec in ins[e]:
                if rec['inc'] and not rec['dma']:
                    c += 1
                    rec['semval'] = c
        st = self.stack
        csem = {e: st.enter_context(nc.semaphore(f's_{e}')) for e in ENGS}
        dsem = {key: st.enter_context(nc.semaphore(f'd_{key[1]}{key[2]}')) for key in self.dma_cnt}
        block = st.enter_context(nc.Block())

        def replay(E, eng):
            for rec in ins[E]:
                for (k, val) in rec['waits']:
                    if k[0] == 'c':
                        eng.wait_ge(csem[k[1]], ins[k[1]][val]['semval'])
                    else:
                        eng.wait_ge(dsem[k], val)
                r = rec['fn'](eng)
                if rec['dma']:
                    r.then_inc(dsem[rec['dsem']], 16)
                elif rec['inc']:
                    r.then_inc(csem[E], 1)

        @block.tensor
        def _(e):
            replay('pe', e)

        @block.scalar
        def _(e):
            replay('act', e)

        @block.vector
        def _(e):
            replay('dve', e)

        @block.gpsimd
        def _(e):
            replay('pool', e)

        @block.sync
        def _(e):
            replay('sp', e)

    def close(self):
        self.stack.close()


def make_consts():
    c = {}
    c['ident'] = np.eye(128, dtype=np.float32)
    kj = np.arange(128)[:, None]
    qi = np.arange(128)[None, :]
    c['dist'] = (qi - kj).astype(np.float32)
    c['tri_ge'] = (qi >= kj).astype(np.float32)
    c['tri_le'] = (qi <= kj).astype(np.float32)
    c['mask16'] = ((qi >= kj) & (qi // 16 == kj // 16)).astype(np.float32)
    c['cm8'] = (np.arange(128)[:, None] // 16 == np.arange(8)[None, :]).astype(np.float32)
    f = np.arange(512)[None, None, :]
    p = np.arange(128)[:, None, None]
    r = np.arange(4)[None, :, None]
    c['cmask'] = (f >= 128 * r + p).astype(np.float32)
    return c


CONST_ORDER = ['ident', 'dist', 'tri_ge', 'tri_le', 'mask16', 'cm8', 'cmask']


def pack_consts():
    c = make_consts()
    cols = {}
    arrs = []
    off = 0
    for k in CONST_ORDER:
        a = c[k].reshape(128, -1).astype(np.float32)
        cols[k] = (off, a.shape[1])
        arrs.append(a)
        off += a.shape[1]
    return np.concatenate(arrs, axis=1), cols


class Builder:
    def __init__(self, stage='full', nlayers=DEPTH):
        self.stage = stage
        self.nlayers = nlayers
        nc = self.nc = bass.Bass("TRN2", target_bir_lowering=False)
        P = self.P = Prog(nc)
        self.cst_np, self.cst_cols = pack_consts()
        ein = lambda name, shape: P.dram(shape, F32, name, kind='ExternalInput')
        self.x_d = ein('x', [S, D])
        self.w_in_d = ein('w_in', [DEPTH, D, D_IN])
        self.w_out_d = ein('w_out', [DEPTH, D, D])
        self.w_ff1_d = ein('w_ff1', [DEPTH, D, D_FF])
        self.w_ff2_d = ein('w_ff2', [DEPTH, D_FF, D])
        self.ln1_g_d = ein('ln1_g', [DEPTH, D])
        self.ln1_b_d = ein('ln1_b', [DEPTH, D])
        self.ln2_g_d = ein('ln2_g', [DEPTH, D])
        self.ln2_b_d = ein('ln2_b', [DEPTH, D])
        self.cst_d = ein('cst', list(self.cst_np.shape))
        self.rope_d = ein('rope', [2, 32, S])
        self.rmask_d = ein('rmask', [2, S])
        self.gdn_conv_d = ein('gdn_conv_w', [DEPTH, 4, 768])
        self.gdn_alog_d = ein('gdn_a_log', [DEPTH, 4])
        self.gdn_dtb_d = ein('gdn_dt_bias', [DEPTH, 4])
        self.gdn_g_d = ein('gdn_norm_g', [DEPTH, 64])
        _k = 'ExternalOutput' if stage == 'A' else 'Internal'
        self.kkd = P.dram([128, 64, 64], F32, 'kk_scr', kind=_k)
        self.qkd = P.dram([128, 64, 64], F32, 'qk_scr', kind=_k)
        self.ttd = P.dram([128, 64, 64], F32, 'tt_scr', kind=_k)
        self.qktd = P.dram([128, 64, 64], F32, 'qkt_scr', kind=_k)
        self.tokd = P.dram([4, 3, 64, 32, 64], F32, 'tok_scr', kind=_k)
        self.qed = P.dram([4, 64, S], F32, 'qe_scr', kind=_k)
        self.gsd = P.dram([4, 64, S], F32, 'gs_scr', kind=_k)
        self.rows8d = P.dram([2, 8, S], F32, 'rows8_scr', kind=_k)
        self.qhd = P.dram([4, 64, S], F32, 'qh_scr')
        self.ohd = P.dram([4, 64, S], F32, 'oh_scr')
        self.hgrn_lb_d = ein('hgrn_lb_logits', [DEPTH, 256])
        self.hgrn_g_d = ein('hgrn_norm_g', [DEPTH, 64])
        self.mla_qg_d = ein('mla_q_norm_g', [DEPTH, 192])
        self.mla_kvg_d = ein('mla_kv_norm_g', [DEPTH, 128])
        self.mla_wuq_d = ein('mla_w_uq', [DEPTH, 192, 384])
        self.mla_wukv_d = ein('mla_w_ukv', [DEPTH, 128, 512])
        self.y_d = P.dram([S, D], F32, 'y', kind='ExternalOutput')
        self.ocat_d = P.dram([16, 64, S], BF16, 'ocat_scr', kind=('Internal' if stage in ('full', 'dense') else 'ExternalOutput'))
        self.xres_d = P.dram([S, D], F32, 'xres_scr')
        self.dbg = {}
        self.cst = P.sb(list(self.cst_np.shape), F32, 'cst')
        P.dma('sp', self.cst[:], self.cst_d[:], writes=[self.cst])
        self.xT = P.sb([128, 8, S], BF16, 'xT')
        self._ones_f = P.sb([32, 64], F32, 'ones_f')
        P.op('pool', lambda e: e.memset(self._ones_f[:], 1.0), writes=[self._ones_f])
        self.psum = [P.ps([128, 512], F32, f'ps{i}') for i in range(8)]
        self.build()
        outs = [self.y_d, self.ocat_d, self.kkd, self.qkd, self.ttd, self.qktd, self.tokd, self.qed, self.gsd, self.rows8d] + [t for t in self.dbg.values()]
        P.op('sp', lambda e: e.nop(), reads=outs)
        P.emit()
        P.close()

    def dump(self, name, t, ap, shape):
        if self.stage in ('full', 'dense'):
            return
        d = self.P.dram(list(shape), F32, 'dbg_' + name, kind='ExternalOutput')
        self.P.dma('sp', d[:], ap, reads=[t], writes=[d])
        self.dbg[name] = d

    def c(self, name):
        o, n = self.cst_cols[name]
        return self.cst[:, o:o + n]

    def build(self):
        P = self.P
        self.load_x_transposed()
        for l in range(self.nlayers):
            last = (l == self.nlayers - 1)
            with P.scope():
                self.mixers(l)
            if self.stage in ('full', 'dense'):
                with P.scope():
                    self.dense(l, last)

    def load_x_transposed(self):
        P = self.P
        xT = self.xT
        with P.scope():
            xt = [P.sb([128, D], F32, 'xin') for _ in range(2)]
            for tt in range(16):
                t = xt[tt % 2]
                P.dma('sp', t[:], self.x_d[128 * tt:128 * tt + 128, :], writes=[t])
                self.transpose_tile_to_xT(t, tt)

    def transpose_tile_to_xT(self, t, tt, pbase=6):
        P = self.P
        ident = self.c('ident')
        for half in range(2):
            ps = self.psum[pbase + half]
            for j in range(4):
                kc = half * 4 + j
                P.op('pe', lambda e, ps=ps, j=j, kc=kc: e.transpose(ps[:, 128 * j:128 * j + 128], t[:, 128 * kc:128 * kc + 128], ident),
                     reads=[t, self.cst], writes=[ps])
            P.op('act', lambda e, ps=ps, half=half: e.copy(
                out=self.xT[:, 4 * half:4 * half + 4, 128 * tt:128 * tt + 128],
                in_=ps[:].rearrange('p (k t) -> p k t', k=4)),
                reads=[ps], writes=[self.xT.sub(tt // 4)])

    def mixers(self, l):
        if self.stage == 'dense':
            self.mixer_stub(l)
            return
        if self.stage in ('full', 'D'):
            with self.P.scope():
                self.mixer_D(l)

        if self.stage in ('full', 'A'):
            with self.P.scope():
                self.mixer_A(l)
        if self.stage in ('full', 'B'):
            with self.P.scope():
                self.mixer_B(l)
        if self.stage in ('full', 'C'):
            with self.P.scope():
                self.mixer_C(l)

    def rms_gate_out(self, o, gate, gcol, ones, slot, tb, sq, rs, ob):
        P = self.P
        sl = slice(512 * tb, 512 * tb + 512)
        P.op('act', lambda e: e.activation(out=sq[:], in_=o[:], func=AF.Square), reads=[o], writes=[sq])
        ps = self.psum[self._pn % 2]; self._pn += 1
        P.op('pe', lambda e: e.matmul(ps[0:64, :], lhsT=ones[0:64, 0:64], rhs=sq[:], start=True, stop=True), reads=[ones, sq], writes=[ps])
        P.op('act', lambda e: e.activation(out=rs[:], in_=ps[0:64, :], func=AF.Ln, scale=1.0 / 64, bias=EPS), reads=[ps], writes=[rs])
        P.op('act', lambda e: e.activation(out=rs[:], in_=rs[:], func=AF.Exp, scale=-0.5), reads=[rs], writes=[rs])
        P.op('dve', lambda e: e.scalar_tensor_tensor(out=o[:], in0=o[:], scalar=gcol, in1=rs[:], op0=ALU.mult, op1=ALU.mult),
             reads=[o, rs], writes=[o])
        P.op('dve', lambda e: e.tensor_tensor(out=ob[:], in0=o[:], in1=gate[:, sl], op=ALU.mult), reads=[o, gate, gate.sub(tb)], writes=[ob])
        P.dma('sp', self.ocat_d[slot, :, sl], ob[:], reads=[ob], writes=[self.ocat_d])

    def mixer_A(self, l):
        P = self.P
        psum = self.psum
        xT = self.xT
        self._pn = 0
        ident = self.c('ident')
        ones = P.sb([128, 128], BF16, 'onesA')
        P.op('pool', lambda e: e.memset(ones[:], 1.0), writes=[ones])
        gn = P.sb([64, 1], F32, 'gnA')
        P.dma('sp', gn[:], self.gdn_g_d[l, :].rearrange('(p o) -> p o', o=1), writes=[gn])
        SC = {}
        for nm in ('beta', 'e', 'ed', 'egl'):
            SC[nm] = P.sb([64, 32, 8], F32, 'sc_' + nm)
        be = P.sb([64, 32, 4], F32, 'sc_be')
        with P.scope():
            WA = P.sb([128, 8, 1032], BF16, 'wA')
            P.dma('pool', WA[:], self.w_in_d[l, :, 0:1032].rearrange('(k p) n -> p k n', p=128), writes=[WA])
            cw = P.sb([64, 12, 4], F32, 'cwA')
            for j in range(4):
                for b_ in range(12):
                    P.dma('sp', cw[:, b_, j:j + 1], self.gdn_conv_d[l, j, 64 * b_:64 * b_ + 64].rearrange('(c o) -> c o', o=1), reads=[cw], writes=[cw])
            e8 = P.sb([32, S], F32, 'e8')
            self._A_rows(l, WA, e8, SC, be)
            self._A_rest(l, WA, e8, SC, be, ones, gn, cw)
        self._A_phase2()
        self._A_phase3(SC, ones, gn)

    def _A_rows(self, l, WA, e8, SC, be):
        P = self.P
        psum = self.psum
        ident = self.c('ident')
        with P.scope():
            self._A_rows_inner(l, WA, e8, SC, be)

    def _A_rows_inner(self, l, WA, e8, SC, be):
        P = self.P
        psum = self.psum
        ident = self.c('ident')
        rm8 = P.sb([32, S], F32, 'rm8')
        P.dma('sp', rm8[:], self.rmask_d[1:2, :].partition_broadcast(32), writes=[rm8])
        dtb = P.sb([32, 1], F32, 'dtb'); nA = P.sb([32, 1], F32, 'nA')
        P.op('pool', lambda e: e.memset(dtb[:], 0.0), writes=[dtb])
        P.op('pool', lambda e: e.memset(nA[:], 0.0), writes=[nA])
        P.dma('sp', dtb[0:4, :], self.gdn_dtb_d[l, :].rearrange('(p o) -> p o', o=1), reads=[dtb], writes=[dtb])
        P.dma('sp', nA[0:4, :], self.gdn_alog_d[l, :].rearrange('(p o) -> p o', o=1), reads=[nA], writes=[nA])
        P.op('act', lambda e: e.activation(out=nA[:], in_=nA[:], func=AF.Exp), reads=[nA], writes=[nA])
        P.op('dve', lambda e: e.tensor_scalar(out=nA[:], in0=nA[:], scalar1=-1.0, scalar2=None, op0=ALU.mult), reads=[nA], writes=[nA])
        ab = P.sb([32, S], F32, 'abA'); beta8 = P.sb([32, S], F32, 'beta8'); gc8 = P.sb([32, S], F32, 'gc8')
        self.proj_fm(WA, 768, 32, lambda ps, tb: P.op('act', lambda e: e.copy(out=ab[:, 512 * tb:512 * tb + 512], in_=ps[0:32, :]), reads=[ps], writes=[ab]))
        P.op('act', lambda e: e.activation(out=beta8[:], in_=ab[:], func=AF.Sigmoid), reads=[ab], writes=[beta8])
        P.op('act', lambda e: e.activation(out=ab[:], in_=ab[:], func=AF.Exp, bias=dtb[:, 0:1]), reads=[ab, dtb], writes=[ab])
        P.op('act', lambda e: e.activation(out=ab[:], in_=ab[:], func=AF.Ln, bias=1.0), reads=[ab], writes=[ab])
        P.op('dve', lambda e: e.tensor_scalar(out=ab[:], in0=ab[:], scalar1=nA[:, 0:1], scalar2=None, op0=ALU.mult), reads=[ab, nA], writes=[ab])
        P.op('dve', lambda e: e.tensor_tensor_scan(out=gc8[:], data0=rm8[:], data1=ab[:], initial=0.0, op0=ALU.mult, op1=ALU.add), reads=[rm8, ab], writes=[gc8])
        P.dma('sp', self.rows8d[0], gc8[0:8, :], reads=[gc8], writes=[self.rows8d])
        P.dma('sp', self.rows8d[1], beta8[0:8, :], reads=[beta8], writes=[self.rows8d])
        ed8 = P.sb([32, S], F32, 'ed8'); egl8 = P.sb([32, S], F32, 'egl8')
        gc3 = gc8[:].rearrange('p (n c) -> p n c', c=64)
        P.op('act', lambda e: e.activation(out=e8[:], in_=gc8[:], func=AF.Exp), reads=[gc8], writes=[e8])
        P.op('dve', lambda e: e.tensor_tensor(out=ed8[:].rearrange('p (n c) -> p n c', c=64), in0=gc3[:, :, 63:64].to_broadcast([32, 32, 64]), in1=gc3, op=ALU.subtract),
             reads=[gc8], writes=[ed8])
        P.op('act', lambda e: e.activation(out=ed8[:], in_=ed8[:], func=AF.Exp), reads=[ed8], writes=[ed8])
        P.op('dve', lambda e: e.tensor_copy(out=egl8[:].rearrange('p (n c) -> p n c', c=64), in_=gc3[:, :, 63:64].to_broadcast([32, 32, 64])), reads=[gc8], writes=[egl8])
        P.op('act', lambda e: e.activation(out=egl8[:], in_=egl8[:], func=AF.Exp), reads=[egl8], writes=[egl8])
        for qi_, (nm, src) in enumerate((('beta', beta8), ('e', e8), ('ed', ed8), ('egl', egl8))):
            t = SC[nm]
            for half in range(2):
                ps = psum[2 + half]
                for c in range(16):
                    n = 16 * half + c
                    P.op('pe', lambda e, ps=ps, c=c, n=n, src=src: e.transpose(ps[0:64, 32 * c:32 * c + 32], src[:, 64 * n:64 * n + 64], ident[0:32, 0:32]),
                         reads=[src, self.cst], writes=[ps])
                P.op('act', lambda e, ps=ps, t=t, half=half: e.copy(out=t[:, 16 * half:16 * half + 16, :], in_=ps[0:64, :].rearrange('p (n r) -> p n r', r=32)[:, :, 0:8]),
                     reads=[ps], writes=[t])
        P.op('dve', lambda e: e.tensor_tensor(out=be[:], in0=SC['beta'][:, :, 4:8], in1=SC['e'][:, :, 0:4], op=ALU.mult), reads=[SC['beta'], SC['e']], writes=[be])

    def _A_rest(self, l, WA, e8, SC, be, ones, gn, cw):
        P = self.P
        self.dump('e8', e8, e8[:], [32, S])
        for nm in SC:
            self.dump('sc_' + nm, SC[nm], SC[nm][:], [64, 32, 8])
        self.dump('be', be, be[:], [64, 32, 4])
        psum = self.psum
        xT = self.xT
        ident = self.c('ident')
        for h in range(4):
          with P.scope():
            xs = [P.sb([64, S], F32, 'xA') for _ in range(3)]
            ys = [P.sb([64, S], F32, 'yA') for _ in range(3)]
            gs = P.sb([64, S], F32, 'gsA')
            for i in range(3):
                self.proj_fm(WA, 256 * i + 64 * h, 64, lambda ps, tb, i=i: P.op('act', lambda e: e.copy(out=xs[i][:, 512 * tb:512 * tb + 512], in_=ps[0:64, :]),
                                                                               reads=[ps], writes=[xs[i]]))
            self.proj_fm(WA, 776 + 64 * h, 64, lambda ps, tb: P.op('act', lambda e: e.activation(out=gs[:, 512 * tb:512 * tb + 512], in_=ps[0:64, :], func=AF.Silu),
                                                                   reads=[ps], writes=[gs]))
            P.dma('sp', self.gsd[h], gs[:], reads=[gs], writes=[self.gsd])
            for i in range(3):
                x = xs[i]; y = ys[i]; blk = 4 * i + h
                P.op('dve', lambda e, x=x, y=y, blk=blk: e.tensor_scalar(out=y[:], in0=x[:], scalar1=cw[:, blk, 3:4], scalar2=None, op0=ALU.mult), reads=[x, cw], writes=[y])
                for sft in (1, 2, 3):
                    P.op('dve', lambda e, x=x, y=y, blk=blk, sft=sft: e.scalar_tensor_tensor(
                        out=y[:, sft:S], in0=x[:, 0:S - sft], scalar=cw[:, blk, 3 - sft:4 - sft], in1=y[:, sft:S], op0=ALU.mult, op1=ALU.add),
                        reads=[x, y, cw], writes=[y])
                P.op('act', lambda e, y=y: e.activation(out=y[:], in_=y[:], func=AF.Silu), reads=[y], writes=[y])
            sq = P.sb([64, 512], BF16, 'sqA'); rs = P.sb([64, 512], F32, 'rsA')
            for i in range(2):
                y = ys[i]
                for tb in range(4):
                    sl = slice(512 * tb, 512 * tb + 512)
                    P.op('act', lambda e, y=y, sl=sl: e.activation(out=sq[:], in_=y[:, sl], func=AF.Square), reads=[y], writes=[sq])
                    ps = psum[self._pn % 2]; self._pn += 1
                    P.op('pe', lambda e, ps=ps: e.matmul(ps[0:64, :], lhsT=ones[0:64, 0:64], rhs=sq[:], start=True, stop=True), reads=[ones, sq], writes=[ps])
                    P.op('act', lambda e, ps=ps: e.activation(out=rs[:], in_=ps[0:64, :], func=AF.Ln, bias=EPS), reads=[ps], writes=[rs])
                    P.op('act', lambda e: e.activation(out=rs[:], in_=rs[:], func=AF.Exp, scale=-0.5), reads=[rs], writes=[rs])
                    P.op('dve', lambda e, y=y, sl=sl, i=i: e.scalar_tensor_tensor(out=y[:, sl], in0=y[:, sl], scalar=(0.125 if i == 0 else 1.0), in1=rs[:],
                                                                                 op0=ALU.mult, op1=ALU.mult), reads=[y, rs], writes=[y])
            qn, kn, vv = ys
            stg = [P.sb([64, 32, 64], F32, 'stgA') for _ in range(2)]
            for gi, (lh, dst) in enumerate(((kn, self.kkd), (qn, self.qkd))):
                st = stg[gi]
                for g8 in range(4):
                    ps = psum[2 + (g8 % 2)]
                    for c in range(8):
                        n = 8 * g8 + c
                        csl = slice(64 * n, 64 * n + 64)
                        P.op('pe', lambda e, ps=ps, c=c, csl=csl, lh=lh: e.matmul(ps[0:64, 64 * c:64 * c + 64], lhsT=lh[:, csl], rhs=kn[:, csl], start=True, stop=True),
                             reads=[lh, kn], writes=[ps])
                    P.op('act', lambda e, ps=ps, st=st, g8=g8: e.copy(out=st[:, 8 * g8:8 * g8 + 8, :].rearrange('p n j -> p (n j)'), in_=ps[0:64, :]), reads=[ps], writes=[st])
                P.dma('sp', dst[32 * h:32 * h + 32].rearrange('n i j -> i n j'), st[:], reads=[st], writes=[dst])
            sel = P.sb([32, 64], F32, 'selA')
            P.op('pool', lambda e: e.memset(sel[:], 0.0), writes=[sel])
            P.op('pool', lambda e, h=h: e.affine_select(out=sel[:], in_=ones_f(P, self)[:], pattern=[[0, 64]], compare_op=ALU.is_equal, fill=0.0,
                                                        base=-h, channel_multiplier=1), reads=[sel], writes=[sel])
            qe = xs[0]
            for tb in range(4):
                sl = slice(512 * tb, 512 * tb + 512)
                ps = psum[self._pn % 2]; self._pn += 1
                P.op('pe', lambda e, ps=ps, sl=sl: e.matmul(ps[0:64, :], lhsT=sel[:], rhs=e8[:, sl], start=True, stop=True), reads=[sel, e8], writes=[ps])
                P.op('dve', lambda e, ps=ps, sl=sl: e.tensor_tensor(out=qe[:, sl], in0=qn[:, sl], in1=ps[0:64, :], op=ALU.mult), reads=[qn, ps], writes=[qe])
            P.dma('sp', self.qed[h], qe[:], reads=[qe], writes=[self.qed])
            tk = [P.sb([64, 32, 64], F32, 'tkA') for _ in range(3)]
            for g8 in range(4):
                for si, src in enumerate((kn, vv)):
                    ps = psum[4 + si]
                    for c in range(8):
                        n = 8 * g8 + c
                        P.op('pe', lambda e, ps=ps, c=c, n=n, src=src: e.transpose(ps[0:64, 64 * c:64 * c + 64], src[:, 64 * n:64 * n + 64], ident[0:64, 0:64]),
                             reads=[src, self.cst], writes=[ps])
                    p3 = ps[0:64, :].rearrange('p (n d) -> p n d', d=64)
                    nsl = slice(8 * g8, 8 * g8 + 8)
                    if si == 0:
                        P.op('dve', lambda e, p3=p3, nsl=nsl, h=h: e.tensor_tensor(out=tk[0][:, nsl, :], in0=p3, in1=be[:, nsl, h:h + 1].to_broadcast([64, 8, 64]), op=ALU.mult),
                             reads=[ps, be], writes=[tk[0]])
                        P.op('dve', lambda e, p3=p3, nsl=nsl, h=h: e.tensor_tensor(out=tk[1][:, nsl, :], in0=p3, in1=SC['ed'][:, nsl, h:h + 1].to_broadcast([64, 8, 64]), op=ALU.mult),
                             reads=[ps, SC['ed']], writes=[tk[1]])
                    else:
                        P.op('dve', lambda e, p3=p3, nsl=nsl, h=h: e.tensor_tensor(out=tk[2][:, nsl, :], in0=p3, in1=SC['beta'][:, nsl, 4 + h:5 + h].to_broadcast([64, 8, 64]), op=ALU.mult),
                             reads=[ps, SC['beta']], writes=[tk[2]])
            for i in range(3):
                P.dma('sp', self.tokd[h, i], tk[i][:], reads=[tk[i]], writes=[self.tokd])

    def _A_phase2(self):
        P = self.P
        with P.scope():
            KKs = P.sb([128, 64, 64], F32, 'KKs'); QKs = P.sb([128, 64, 64], F32, 'QKs')
            Dm = P.sb([128, 64, 64], F32, 'Dms'); X = P.sb([128, 64, 64], F32, 'Xs'); tmp = P.sb([128, 64, 64], F32, 'tmps')
            gcs = P.sb([128, 64], F32, 'gcs'); bts = P.sb([128, 64], F32, 'bts')
            P.dma('sp', KKs[:], self.kkd[:], reads=[self.kkd], writes=[KKs])
            P.dma('sp', QKs[:], self.qkd[:], reads=[self.qkd], writes=[QKs])
            P.dma('sp', gcs[:], self.rows8d[0, 0:4, :].rearrange('h (n c) -> (h n) c', c=64), reads=[self.rows8d], writes=[gcs])
            P.dma('sp', bts[:], self.rows8d[1, 4:8, :].rearrange('h (n c) -> (h n) c', c=64), reads=[self.rows8d], writes=[bts])
            P.op('dve', lambda e: e.tensor_tensor(out=Dm[:], in0=gcs[:].unsqueeze(2).to_broadcast([128, 64, 64]), in1=gcs[:].unsqueeze(1).to_broadcast([128, 64, 64]),
                                                  op=ALU.subtract), reads=[gcs], writes=[Dm])
            P.op('dve', lambda e: e.tensor_scalar(out=Dm[:], in0=Dm[:], scalar1=0.0, scalar2=None, op0=ALU.min), reads=[Dm], writes=[Dm])
            P.op('act', lambda e: e.activation(out=Dm[:], in_=Dm[:], func=AF.Exp), reads=[Dm], writes=[Dm])
            P.op('dve', lambda e: e.tensor_tensor(out=tmp[:].rearrange('p j i -> p i j'), in0=QKs[:], in1=Dm[:], op=ALU.mult), reads=[QKs, Dm], writes=[tmp])
            P.op('pool', lambda e: e.affine_select(out=tmp[:], in_=tmp[:], pattern=[[-1, 64], [1, 64]], compare_op=ALU.is_ge, fill=0.0, base=0, channel_multiplier=0),
                 reads=[tmp], writes=[tmp])
            P.dma('sp', self.qktd[:], tmp[:], reads=[tmp], writes=[self.qktd])
            P.op('dve', lambda e: e.tensor_tensor(out=KKs[:], in0=KKs[:], in1=Dm[:], op=ALU.mult), reads=[KKs, Dm], writes=[KKs])
            P.op('dve', lambda e: e.tensor_tensor(out=KKs[:], in0=KKs[:], in1=bts[:].unsqueeze(2).to_broadcast([128, 64, 64]), op=ALU.mult), reads=[KKs, bts], writes=[KKs])
            P.op('pool', lambda e: e.affine_select(out=KKs[:], in_=KKs[:], pattern=[[1, 64], [-1, 64]], compare_op=ALU.is_gt, fill=0.0, base=0, channel_multiplier=0),
                 reads=[KKs], writes=[KKs])
            P.op('pool', lambda e: e.memset(X[:], 1.0), writes=[X])
            P.op('pool', lambda e: e.affine_select(out=X[:], in_=X[:], pattern=[[1, 64], [-1, 64]], compare_op=ALU.is_equal, fill=0.0, base=0, channel_multiplier=0),
                 reads=[X], writes=[X])
            tmp2 = QKs
            for b in range(1, 64):
                P.op('dve', lambda e, b=b: e.tensor_tensor(out=tmp2[:, 0:b, 0:b], in0=X[:, 0:b, 0:b], in1=KKs[:, b:b + 1, 0:b].to_broadcast([128, b, b]), op=ALU.mult),
                     reads=[X, KKs, tmp], writes=[tmp2])
                P.op('dve', lambda e, b=b: e.tensor_reduce(out=X[:, 0:b, b:b + 1], in_=tmp2[:, 0:b, 0:b], axis=AX.X, op=ALU.add, negate=True), reads=[tmp2], writes=[X])
            P.dma('sp', self.ttd[:], X[:], reads=[X], writes=[self.ttd])

    def _A_phase3(self, SC, ones, gn):
        P = self.P
        psum = self.psum
        ident = self.c('ident')
        Sall = P.sb([64, 4, 33, 64], F32, 'SallA')
        for pair in range(2):
          with P.scope():
            ATa = P.sb([64, 2, 32, 64], F32, 'ATA'); Bna = P.sb([64, 2, 32, 64], F32, 'BnA')
            for hh in range(2):
                with P.scope():
                    self._A3a_head(2 * pair + hh, hh, SC, ATa, Bna)
            for hh in range(2):
                h = 2 * pair + hh
                P.op('pool', lambda e, h=h: e.memset(Sall[:, h, 0, :], 0.0), writes=[Sall.sub((h, 0))])
            for n in range(32):
                for hh in range(2):
                    h = 2 * pair + hh
                    ps = psum[4 * (n % 2) + hh]
                    P.op('pe', lambda e, ps=ps, n=n, h=h, hh=hh: e.matmul(ps[0:64, 0:64], lhsT=ATa[:, hh, n, :], rhs=Sall[:, h, n, :], start=True, stop=True),
                         reads=[ATa.sub(hh), Sall.sub((h, n))], writes=[ps])
                    P.op('dve', lambda e, ps=ps, n=n, h=h, hh=hh: e.tensor_tensor(out=Sall[:, h, n + 1, :], in0=Bna[:, hh, n, :], in1=ps[0:64, 0:64], op=ALU.add),
                         reads=[Bna.sub(hh), ps], writes=[Sall.sub((h, n + 1))])
        for h in range(4):
            with P.scope():
                self._A3c_head(h, Sall, ones, gn)

    def _A3a_head(self, h, hh, SC, ATa, Bna):
        P = self.P
        psum = self.psum
        ident = self.c('ident')
        TT = P.sb([64, 32, 64], F32, 'TTA'); QKT = P.sb([64, 32, 64], F32, 'QKTA')
        tk = [P.sb([64, 32, 64], F32, 'tk3A') for _ in range(3)]
        qe = P.sb([64, S], F32, 'qe3A')
        P.dma('sp', TT[:], self.ttd[32 * h:32 * h + 32].rearrange('n a b -> a n b'), reads=[self.ttd], writes=[TT])
        P.dma('sp', QKT[:], self.qktd[32 * h:32 * h + 32].rearrange('n a b -> a n b'), reads=[self.qktd], writes=[QKT])
        for i in range(3):
            P.dma('sp', tk[i][:], self.tokd[h, i], reads=[self.tokd], writes=[tk[i]])
        P.dma('sp', qe[:], self.qed[h], reads=[self.qed], writes=[qe])
        kbe, kd, vb = tk
        UW = P.sb([64, 32, 128], F32, 'UWA')
        for g4 in range(8):
            ps = psum[g4 % 2]
            for c in range(4):
                n = 4 * g4 + c
                P.op('pe', lambda e, ps=ps, c=c, n=n: e.matmul(ps[0:64, 128 * c:128 * c + 64], lhsT=TT[:, n, :], rhs=vb[:, n, :], start=True, stop=True),
                     reads=[TT, vb], writes=[ps])
                P.op('pe', lambda e, ps=ps, c=c, n=n: e.matmul(ps[0:64, 128 * c + 64:128 * c + 128], lhsT=TT[:, n, :], rhs=kbe[:, n, :], start=True, stop=True),
                     reads=[TT, kbe], writes=[ps])
            P.op('act', lambda e, ps=ps, g4=g4: e.copy(out=UW[:, 4 * g4:4 * g4 + 4, :].rearrange('p n d -> p (n d)'), in_=ps[0:64, :]), reads=[ps], writes=[UW])
        QhT = P.sb([64, S], F32, 'QhTA'); OhT = P.sb([64, S], F32, 'OhTA')
        id64 = P.sb([64, 64], F32, 'id64A')
        P.op('pool', lambda e: e.tensor_copy(out=id64[:], in_=ident[0:64, 0:64]), reads=[self.cst], writes=[id64])
        for g8 in range(4):
            sl = slice(512 * g8, 512 * g8 + 512)
            ps = psum[2]
            for c in range(8):
                n = 8 * g8 + c
                P.op('pe', lambda e, ps=ps, c=c, n=n: e.matmul(ps[0:64, 64 * c:64 * c + 64], lhsT=UW[:, n, 64:128], rhs=QKT[:, n, :], start=True, stop=True),
                     reads=[UW, QKT], writes=[ps])
            P.op('dve', lambda e, ps=ps, sl=sl: e.tensor_tensor(out=QhT[:, sl], in0=qe[:, sl], in1=ps[0:64, :], op=ALU.subtract), reads=[qe, ps], writes=[QhT])
            ps = psum[3]
            for c in range(8):
                n = 8 * g8 + c
                P.op('pe', lambda e, ps=ps, c=c, n=n: e.matmul(ps[0:64, 64 * c:64 * c + 64], lhsT=UW[:, n, 0:64], rhs=QKT[:, n, :], start=True, stop=True),
                     reads=[UW, QKT], writes=[ps])
            P.op('act', lambda e, ps=ps, sl=sl: e.copy(out=OhT[:, sl], in_=ps[0:64, :]), reads=[ps], writes=[OhT])
            ps = psum[4 + g8 % 2]
            for c in range(8):
                n = 8 * g8 + c
                P.op('pe', lambda e, ps=ps, c=c, n=n: e.matmul(ps[0:64, 64 * c:64 * c + 64], lhsT=UW[:, n, 64:128], rhs=kd[:, n, :], start=True, stop=True),
                     reads=[UW, kd], writes=[ps])
            for c in range(8):
                n = 8 * g8 + c
                P.op('dve', lambda e, ps=ps, c=c, n=n: e.scalar_tensor_tensor(out=ATa[:, hh, n, :], in0=id64[:], scalar=SC['egl'][:, n, h:h + 1], in1=ps[0:64, 64 * c:64 * c + 64],
                                                                         op0=ALU.mult, op1=ALU.subtract), reads=[id64, SC['egl'], ps], writes=[ATa.sub(hh)])
            ps = psum[6 + g8 % 2]
            for c in range(8):
                n = 8 * g8 + c
                P.op('pe', lambda e, ps=ps, c=c, n=n: e.matmul(ps[0:64, 64 * c:64 * c + 64], lhsT=kd[:, n, :], rhs=UW[:, n, 0:64], start=True, stop=True),
                     reads=[UW, kd], writes=[ps])
            P.op('act', lambda e, ps=ps, g8=g8: e.copy(out=Bna[:, hh, 8 * g8:8 * g8 + 8, :].rearrange('p n d -> p (n d)'), in_=ps[0:64, :]), reads=[ps], writes=[Bna.sub(hh)])
        P.dma('sp', self.qhd[h], QhT[:], reads=[QhT], writes=[self.qhd])
        P.dma('sp', self.ohd[h], OhT[:], reads=[OhT], writes=[self.ohd])

    def _A3c_head(self, h, Sall, ones, gn):
        P = self.P
        psum = self.psum
        QhT = P.sb([64, S], F32, 'QhTc'); OhT = P.sb([64, S], F32, 'OhTc'); gs = P.sb([64, S], F32, 'gs3A')
        P.dma('sp', QhT[:], self.qhd[h], reads=[self.qhd], writes=[QhT])
        P.dma('sp', OhT[:], self.ohd[h], reads=[self.ohd], writes=[OhT])
        P.dma('sp', gs[:], self.gsd[h], reads=[self.gsd], writes=[gs])
        oi = [P.sb([64, 512], F32, 'oiA') for _ in range(2)]
        sq = P.sb([64, 512], BF16, 'sq3A'); rs = P.sb([64, 512], F32, 'rs3A')
        ob = [P.sb([64, 512], BF16, 'obA') for _ in range(2)]
        for tb in range(4):
            sl = slice(512 * tb, 512 * tb + 512)
            ps = psum[4 + tb % 2]
            for c in range(8):
                n = 8 * tb + c
                P.op('pe', lambda e, ps=ps, c=c, n=n: e.matmul(ps[0:64, 64 * c:64 * c + 64], lhsT=Sall[:, h, n, :], rhs=QhT[:, 64 * n:64 * n + 64], start=True, stop=True),
                     reads=[Sall.sub((h, n)), QhT], writes=[ps])
            o = oi[tb % 2]
            P.op('dve', lambda e, o=o, ps=ps, sl=sl: e.tensor_tensor(out=o[:], in0=OhT[:, sl], in1=ps[0:64, :], op=ALU.add), reads=[ps, OhT], writes=[o])
            self.rms_gate_out(o, gs, gn[:, 0:1], ones, h, tb, sq, rs, ob[tb % 2])

    def mixer_C(self, l):
        P = self.P
        psum = self.psum
        xT = self.xT
        self._pn = 0
        WC = P.sb([128, 8, 1024], BF16, 'wC')
        P.dma('pool', WC[:], self.w_in_d[l, :, 1384:2408].rearrange('(k p) n -> p k n', p=128), writes=[WC])
        ones = P.sb([128, 128], BF16, 'onesC')
        P.op('pool', lambda e: e.memset(ones[:], 1.0), writes=[ones])
        zer = P.sb([64, 16], BF16, 'zerC')
        P.op('pool', lambda e: e.memset(zer[:], 0.0), writes=[zer])
        rmask = P.sb([64, S], F32, 'rmaskC')
        P.dma('sp', rmask[:], self.rmask_d[0:1, :].partition_broadcast(64), writes=[rmask])
        gn = P.sb([64, 1], F32, 'gnC')
        P.dma('sp', gn[:], self.hgrn_g_d[l, :].rearrange('(p o) -> p o', o=1), writes=[gn])
        lb = P.sb([64, 4], F32, 'lbC'); oml = P.sb([64, 4], F32, 'omlC'); noml = P.sb([64, 4], F32, 'nomlC')
        if l == 0:
            P.op('pool', lambda e: e.memset(lb[:], 0.0), writes=[lb])
        else:
            z = P.sb([64, 2, 4], F32, 'zC')
            for li in range(2):
                for hh in range(4):
                    P.dma('sp', z[:, li, hh:hh + 1], self.hgrn_lb_d[li, 64 * hh:64 * hh + 64].rearrange('(c o) -> c o', o=1), reads=[z], writes=[z])
            P.op('dve', lambda e: e.tensor_tensor(out=lb[:], in0=z[:, 1, :], in1=z[:, 0, :], op=ALU.subtract), reads=[z], writes=[lb])
            P.op('act', lambda e: e.activation(out=lb[:], in_=lb[:], func=AF.Sigmoid), reads=[lb], writes=[lb])
        P.op('dve', lambda e: e.tensor_scalar(out=oml[:], in0=lb[:], scalar1=-1.0, scalar2=1.0, op0=ALU.mult, op1=ALU.add), reads=[lb], writes=[oml])
        P.op('dve', lambda e: e.tensor_scalar(out=noml[:], in0=oml[:], scalar1=-1.0, scalar2=None, op0=ALU.mult), reads=[oml], writes=[noml])
        m16 = P.sb([128, 128], F32, 'm16')
        P.op('pool', lambda e: e.tensor_copy(out=m16[:], in_=self.c('mask16')), reads=[self.cst], writes=[m16])
        cm8 = P.sb([128, 8], BF16, 'cm8')
        P.op('pool', lambda e: e.tensor_copy(out=cm8[:], in_=self.c('cm8')), reads=[self.cst], writes=[cm8])
        Vtok = P.sb([128, 16, 256], BF16, 'VtokC')
        for tt in range(16):
            ps = psum[self._pn % 2]; self._pn += 1
            for kc in range(8):
                P.op('pe', lambda e, ps=ps, kc=kc, tt=tt: e.matmul(ps[:, 0:256], lhsT=xT[:, kc, 128 * tt:128 * tt + 128], rhs=WC[:, kc, 512:768],
                                                                   start=(kc == 0), stop=(kc == 7)), reads=[WC, xT.sub(tt // 4)], writes=[ps])
            P.op('act', lambda e, ps=ps, tt=tt: e.copy(out=Vtok[:, tt, :], in_=ps[:, 0:256]), reads=[ps], writes=[Vtok])
        gs = P.sb([64, S], F32, 'gsC')
        qt = P.sb([64, S], BF16, 'qtC'); kt = P.sb([64, S], BF16, 'ktC')
        adec = P.sb([64, 128], F32, 'adecC')
        Bst = P.sb([64, 64, 128], F32, 'BstC')
        kdt = [P.sb([128, 64], BF16, 'kdtC') for _ in range(2)]
        vex = [P.sb([128, 8, 64], BF16, 'vexC') for _ in range(2)]
        sm = [P.sb([128, 4, 128], BF16, 'smC') for _ in range(2)]
        oi = [P.sb([64, 512], F32, 'oiC') for _ in range(2)]
        sq = P.sb([64, 512], BF16, 'sqC'); rs = P.sb([64, 512], F32, 'rsC')
        ob = [P.sb([64, 512], BF16, 'obC') for _ in range(2)]
        ident = self.c('ident')
        for h in range(4):
          with P.scope():
            qT = P.sb([64, S], F32, 'qTC'); sg = P.sb([64, S], F32, 'sgC')
            kk = P.sb([64, S], F32, 'kkC'); cum = P.sb([64, S], F32, 'cumC'); ex = P.sb([64, S], F32, 'exC')
            kd = P.sb([64, S], F32, 'kdC')
            self.proj_fm(WC, 64 * h, 64, lambda ps, tb: P.op('act', lambda e: e.copy(out=qT[:, 512 * tb:512 * tb + 512], in_=ps[0:64, :]),
                                                             reads=[ps], writes=[qT.sub(tb)]))
            self.proj_fm(WC, 256 + 64 * h, 64, lambda ps, tb: P.op('act', lambda e: e.activation(out=sg[:, 512 * tb:512 * tb + 512], in_=ps[0:64, :], func=AF.Sigmoid),
                                                                   reads=[ps], writes=[sg]))
            self.proj_fm(WC, 768 + 64 * h, 64, lambda ps, tb: P.op('act', lambda e: e.activation(out=gs[:, 512 * tb:512 * tb + 512], in_=ps[0:64, :], func=AF.Silu),
                                                                   reads=[ps], writes=[gs.sub(tb)]))
            P.op('dve', lambda e, h=h: e.tensor_scalar(out=kk[:], in0=sg[:], scalar1=noml[:, h:h + 1], scalar2=oml[:, h:h + 1], op0=ALU.mult, op1=ALU.add),
                 reads=[sg, noml, oml], writes=[kk])
            P.op('dve', lambda e, h=h: e.tensor_scalar(out=sg[:], in0=sg[:], scalar1=oml[:, h:h + 1], scalar2=lb[:, h:h + 1], op0=ALU.mult, op1=ALU.add),
                 reads=[sg, oml, lb], writes=[sg])
            P.op('act', lambda e: e.activation(out=sg[:], in_=sg[:], func=AF.Ln), reads=[sg], writes=[sg])
            P.op('dve', lambda e: e.tensor_tensor_scan(out=cum[:], data0=rmask[:], data1=sg[:], initial=0.0, op0=ALU.mult, op1=ALU.add),
                 reads=[rmask, sg], writes=[cum])
            P.op('act', lambda e: e.activation(out=ex[:], in_=cum[:], func=AF.Exp), reads=[cum], writes=[ex])
            P.op('dve', lambda e: e.tensor_tensor(out=qt[:], in0=qT[:], in1=ex[:], op=ALU.mult), reads=[qT.sub(0), qT.sub(1), qT.sub(2), qT.sub(3), ex], writes=[qt])
            P.op('act', lambda e: e.activation(out=ex[:], in_=cum[:], func=AF.Exp, scale=-1.0), reads=[cum], writes=[ex])
            P.op('dve', lambda e: e.tensor_tensor(out=kt[:], in0=kk[:], in1=ex[:], op=ALU.mult), reads=[kk, ex], writes=[kt])
            cum3 = cum[:].rearrange('p (c j) -> p c j', j=16)
            cl = cum3[:, :, 15:16]
            P.op('dve', lambda e, cl=cl, cum3=cum3: e.tensor_tensor(out=ex[:].rearrange('p (c j) -> p c j', j=16), in0=cl.to_broadcast([64, 128, 16]), in1=cum3, op=ALU.subtract),
                 reads=[cum], writes=[ex])
            P.op('act', lambda e: e.activation(out=ex[:], in_=ex[:], func=AF.Exp), reads=[ex], writes=[ex])
            P.op('dve', lambda e: e.tensor_tensor(out=kd[:], in0=kk[:], in1=ex[:], op=ALU.mult), reads=[kk, ex], writes=[kd])
            P.op('act', lambda e, cl=cl: e.activation(out=adec[:].rearrange('p (c o) -> p c o', o=1), in_=cl, func=AF.Exp), reads=[cum], writes=[adec])
            P.op('pool', lambda e: e.memset(adec[:, 0:1], 0.0), reads=[adec], writes=[adec])
            for tt in range(16):
                tsl = slice(128 * tt, 128 * tt + 128)
                pt = psum[2 + (tt % 2)]
                kdtt = kdt[tt % 2]; vx = vex[tt % 2]
                P.op('pe', lambda e, pt=pt, tsl=tsl: e.transpose(pt[:, 0:64], kd[:, tsl], ident[0:64, 0:64]), reads=[kd, self.cst], writes=[pt])
                P.op('act', lambda e, pt=pt, kdtt=kdtt: e.copy(out=kdtt[:], in_=pt[:, 0:64]), reads=[pt], writes=[kdtt])
                P.op('dve', lambda e, vx=vx, tt=tt, h=h: e.tensor_tensor(
                    out=vx[:], in0=Vtok[:, tt, 64 * h:64 * h + 64].unsqueeze(1).to_broadcast([128, 8, 64]),
                    in1=cm8[:].unsqueeze(2).to_broadcast([128, 8, 64]), op=ALU.mult), reads=[Vtok, cm8], writes=[vx])
                pb = psum[4 + (tt % 2)]
                P.op('pe', lambda e, pb=pb, kdtt=kdtt, vx=vx: e.matmul(pb[0:64, :], lhsT=kdtt[:], rhs=vx[:].rearrange('p n v -> p (n v)'), start=True, stop=True),
                     reads=[kdtt, vx], writes=[pb])
                P.op('act', lambda e, pb=pb, tt=tt: e.copy(out=Bst[:, :, 8 * tt:8 * tt + 8], in_=pb[0:64, :].rearrange('p (n v) -> p v n', v=64)),
                     reads=[pb], writes=[Bst])
          with P.scope():
            Sst = P.sb([64, 64, 128], F32, 'SstC'); Sb = P.sb([64, 64, 128], BF16, 'SbC')
            for v in range(64):
                P.op('dve', lambda e, v=v: e.tensor_tensor_scan(out=Sst[:, v, :], data0=adec[:], data1=Bst[:, v, :], initial=0.0,
                                                                                         op0=ALU.mult, op1=ALU.add), reads=[adec, Bst], writes=[Sst.sub(v)])
            P.op('act', lambda e: e.copy(out=Sb[:], in_=Sst[:]), reads=[Sst.sub(v) for v in range(64)], writes=[Sb])
            for tb in range(4):
                pst = psum[2 + (tb % 2)]
                smt = sm[tb % 2]
                for t4 in range(4):
                    tsl = slice(512 * tb + 128 * t4, 512 * tb + 128 * t4 + 128)
                    P.op('pe', lambda e, pst=pst, t4=t4, tsl=tsl: e.matmul(pst[:, 128 * t4:128 * t4 + 128], lhsT=kt[:, tsl], rhs=qt[:, tsl], start=True, stop=True),
                         reads=[kt, qt], writes=[pst])
                P.op('dve', lambda e, pst=pst, smt=smt: e.tensor_tensor(out=smt[:], in0=pst[:].rearrange('p (a t) -> p a t', a=4),
                                                                       in1=m16[:].unsqueeze(1).to_broadcast([128, 4, 128]), op=ALU.mult),
                     reads=[pst, m16], writes=[smt])
                po = psum[6]; po2 = psum[7]
                for t4 in range(4):
                    tt = 4 * tb + t4
                    P.op('pe', lambda e, t4=t4, tt=tt, smt=smt, h=h: e.matmul(po[0:64, 128 * t4:128 * t4 + 128], lhsT=Vtok[:, tt, 64 * h:64 * h + 64], rhs=smt[:, t4, :],
                                                                              start=True, stop=True), reads=[Vtok, smt], writes=[po])
                for c in range(32):
                    n = 32 * tb + c
                    if n == 0:
                        P.op('pe', lambda e, c=c: e.matmul(po2[0:64, 16 * c:16 * c + 16], lhsT=Sb[:, :, 0], rhs=zer[:], start=True, stop=True),
                             reads=[Sb, zer], writes=[po2])
                    else:
                        P.op('pe', lambda e, c=c, n=n: e.matmul(po2[0:64, 16 * c:16 * c + 16], lhsT=Sb[:, :, n - 1], rhs=qt[:, 16 * n:16 * n + 16], start=True, stop=True),
                             reads=[Sb, qt], writes=[po2])
                o = oi[tb % 2]
                P.op('act', lambda e, o=o: e.copy(out=o[:], in_=po[0:64, :]), reads=[po], writes=[o])
                P.op('dve', lambda e, o=o: e.tensor_tensor(out=o[:], in0=o[:], in1=po2[0:64, :], op=ALU.add), reads=[o, po2], writes=[o])
                self.rms_gate_out(o, gs, gn[:, 0:1], ones, 8 + h, tb, sq, rs, ob[tb % 2])

    def proj_fm(self, W, c0, M, evac, xsrc=None, nk=8):
        P = self.P
        for tb in range(4):
            ps = self.psum[self._pn % 2]; self._pn += 1
            for kc in range(nk):
                P.op('pe', lambda e, ps=ps, kc=kc, tb=tb: e.matmul(
                    ps[0:M, :], lhsT=W[:, kc, c0:c0 + M], rhs=self.xT[:, kc, 512 * tb:512 * tb + 512],
                    start=(kc == 0), stop=(kc == nk - 1)), reads=[W, self.xT.sub(tb)], writes=[ps])
            evac(ps, tb)

    def mixer_B(self, l):
        P = self.P
        psum = self.psum
        self._pn = 0
        SC = 96 ** -0.5
        WB = P.sb([128, 8, 352], BF16, 'wB')
        P.dma('pool', WB[:], self.w_in_d[l, :, 1032:1384].rearrange('(k p) n -> p k n', p=128), writes=[WB])
        WBr = P.sb([128, 8, 32], BF16, 'wBr')
        P.op('act', lambda e: e.mul(out=WBr[:, :, 0:16], in_=WB[:, :, 336:352], mul=-1.0), reads=[WB], writes=[WBr])
        P.op('act', lambda e: e.copy(out=WBr[:, :, 16:32], in_=WB[:, :, 320:336]), reads=[WB, WBr], writes=[WBr])
        wuqa = P.sb([128, 384], BF16, 'wuqa'); wuqb = P.sb([64, 384], BF16, 'wuqb')
        P.dma('pool', wuqa[:], self.mla_wuq_d[l, 0:128, :], writes=[wuqa])
        P.dma('pool', wuqb[:], self.mla_wuq_d[l, 128:192, :], writes=[wuqb])
        wra = P.sb([128, 4, 32], BF16, 'wra'); wrb = P.sb([64, 4, 32], BF16, 'wrb')
        for (src, dst) in ((wuqa, wra), (wuqb, wrb)):
            v = src[:].rearrange('p (h c) -> p h c', c=96)
            P.op('act', lambda e, v=v, dst=dst: e.mul(out=dst[:, :, 0:16], in_=v[:, :, 80:96], mul=-1.0), reads=[src], writes=[dst])
            P.op('act', lambda e, v=v, dst=dst: e.copy(out=dst[:, :, 16:32], in_=v[:, :, 64:80]), reads=[src, dst], writes=[dst])
        wukv = P.sb([128, 512], BF16, 'wukv')
        P.dma('pool', wukv[:], self.mla_wukv_d[l], writes=[wukv])
        wv = P.sb([128, 4, 64], BF16, 'wv')
        P.op('act', lambda e: e.copy(out=wv[:], in_=wukv[:].rearrange('p (h c) -> p h c', c=128)[:, :, 64:128]), reads=[wukv], writes=[wv])
        gqa = P.sb([128, 1], F32, 'gqa'); gqb = P.sb([64, 1], F32, 'gqb'); gkv = P.sb([128, 1], F32, 'gkv')
        P.dma('sp', gqa[:], self.mla_qg_d[l, 0:128].rearrange('(p o) -> p o', o=1), writes=[gqa])
        P.dma('sp', gqb[:], self.mla_qg_d[l, 128:192].rearrange('(p o) -> p o', o=1), writes=[gqb])
        P.dma('sp', gkv[:], self.mla_kvg_d[l, :].rearrange('(p o) -> p o', o=1), writes=[gkv])
        ones = P.sb([128, 128], BF16, 'onesB')
        P.op('pool', lambda e: e.memset(ones[:], 1.0), writes=[ones])
        cosT = P.sb([32, S], F32, 'cosT'); sinT = P.sb([32, S], F32, 'sinT')
        P.dma('sp', cosT[:], self.rope_d[0], writes=[cosT])
        P.dma('sp', sinT[:], self.rope_d[1], writes=[sinT])
        cqna = P.sb([128, S], BF16, 'cqna'); cqnb = P.sb([64, S], BF16, 'cqnb'); ckvn = P.sb([128, S], BF16, 'ckvn')
        KrT = P.sb([32, S], BF16, 'KrT')
        with P.scope():
            cqa = P.sb([128, S], F32, 'cqa'); cqb = P.sb([64, S], F32, 'cqb'); ckv = P.sb([128, S], F32, 'ckv')
            sqa = P.sb([128, S], BF16, 'sqa'); sqb = P.sb([64, S], BF16, 'sqb'); sqk = P.sb([128, S], BF16, 'sqk')
            for (dst, sq, c0, M) in ((cqa, sqa, 0, 128), (cqb, sqb, 128, 64), (ckv, sqk, 192, 128)):
                def evac(ps, tb, dst=dst, sq=sq, M=M):
                    P.op('act', lambda e: e.copy(out=dst[:, 512 * tb:512 * tb + 512], in_=ps[0:M, :]), reads=[ps], writes=[dst.sub(tb)])
                    P.op('act', lambda e: e.activation(out=sq[:, 512 * tb:512 * tb + 512], in_=ps[0:M, :], func=AF.Square), reads=[ps], writes=[sq.sub(tb)])
                self.proj_fm(WB, c0, M, evac)
            kx = P.sb([32, S], F32, 'kx')
            self.proj_fm(WB, 320, 32, lambda ps, tb: P.op('dve', lambda e: e.tensor_tensor(
                out=kx[:, 512 * tb:512 * tb + 512], in0=ps[0:32, :], in1=cosT[:, 512 * tb:512 * tb + 512], op=ALU.mult),
                reads=[ps, cosT], writes=[kx.sub(tb)]))
            kx2 = P.sb([32, S], F32, 'kx2')
            self.proj_fm(WBr, 0, 32, lambda ps, tb: P.op('dve', lambda e: e.tensor_tensor(
                out=kx2[:, 512 * tb:512 * tb + 512], in0=ps[0:32, :], in1=sinT[:, 512 * tb:512 * tb + 512], op=ALU.mult),
                reads=[ps, sinT], writes=[kx2.sub(tb)]))
            for tb in range(4):
                P.op('dve', lambda e, tb=tb: e.tensor_tensor(out=KrT[:, 512 * tb:512 * tb + 512], in0=kx[:, 512 * tb:512 * tb + 512],
                                                              in1=kx2[:, 512 * tb:512 * tb + 512], op=ALU.add),
                     reads=[kx.sub(tb), kx2.sub(tb)], writes=[KrT.sub(tb)])
            rq = [P.sb([128, 512], F32, 'rq') for _ in range(2)]
            for tb in range(4):
                sl = slice(512 * tb, 512 * tb + 512)
                ps = psum[self._pn % 2]; self._pn += 1
                P.op('pe', lambda e, ps=ps, sl=sl: e.matmul(ps[:], lhsT=ones[:], rhs=sqa[:, sl], start=True, stop=False), reads=[ones, sqa.sub(tb)], writes=[ps])
                P.op('pe', lambda e, ps=ps, sl=sl: e.matmul(ps[:], lhsT=ones[0:64, :], rhs=sqb[:, sl], start=False, stop=True), reads=[ones, sqb.sub(tb)], writes=[ps])
                r = rq[0]
                P.op('act', lambda e, ps=ps, r=r: e.activation(out=r[:], in_=ps[:], func=AF.Ln, scale=1.0 / 192, bias=EPS), reads=[ps], writes=[r])
                P.op('act', lambda e, r=r: e.activation(out=r[:], in_=r[:], func=AF.Exp, scale=-0.5), reads=[r], writes=[r])
                P.op('dve', lambda e, r=r, sl=sl: e.scalar_tensor_tensor(out=cqna[:, sl], in0=cqa[:, sl], scalar=gqa[:, 0:1], in1=r[:], op0=ALU.mult, op1=ALU.mult),
                     reads=[cqa.sub(tb), gqa, r], writes=[cqna.sub(tb)])
                P.op('dve', lambda e, r=r, sl=sl: e.scalar_tensor_tensor(out=cqnb[:, sl], in0=cqb[:, sl], scalar=gqb[:, 0:1], in1=r[0:64, :], op0=ALU.mult, op1=ALU.mult),
                     reads=[cqb.sub(tb), gqb, r], writes=[cqnb.sub(tb)])
                ps = psum[self._pn % 2]; self._pn += 1
                P.op('pe', lambda e, ps=ps, sl=sl: e.matmul(ps[:], lhsT=ones[:], rhs=sqk[:, sl], start=True, stop=True), reads=[ones, sqk.sub(tb)], writes=[ps])
                r = rq[1]
                P.op('act', lambda e, ps=ps, r=r: e.activation(out=r[:], in_=ps[:], func=AF.Ln, scale=1.0 / 128, bias=EPS), reads=[ps], writes=[r])
                P.op('act', lambda e, r=r: e.activation(out=r[:], in_=r[:], func=AF.Exp, scale=-0.5), reads=[r], writes=[r])
                P.op('dve', lambda e, r=r, sl=sl: e.scalar_tensor_tensor(out=ckvn[:, sl], in0=ckv[:, sl], scalar=gkv[:, 0:1], in1=r[:], op0=ALU.mult, op1=ALU.mult),
                     reads=[ckv.sub(tb), gkv, r], writes=[ckvn.sub(tb)])
        QnT = [P.sb([64, S], BF16, 'QnT') for _ in range(4)]
        QrT = [P.sb([32, S], BF16, 'QrT') for _ in range(4)]
        KnT = [P.sb([64, S], BF16, 'KnT') for _ in range(4)]
        Vtok = P.sb([128, 16, 256], BF16, 'VtokB')
        qx = P.sb([32, 512], F32, 'qx'); qx2 = P.sb([32, 512], F32, 'qx2')
        for h in range(4):
            for tb in range(4):
                sl = slice(512 * tb, 512 * tb + 512)
                ps = psum[self._pn % 2]; self._pn += 1
                P.op('pe', lambda e, ps=ps, sl=sl, h=h: e.matmul(ps[0:64, :], lhsT=wuqa[:, 96 * h:96 * h + 64], rhs=cqna[:, sl], start=True, stop=False),
                     reads=[wuqa, cqna.sub(tb)], writes=[ps])
                P.op('pe', lambda e, ps=ps, sl=sl, h=h: e.matmul(ps[0:64, :], lhsT=wuqb[:, 96 * h:96 * h + 64], rhs=cqnb[:, sl], start=False, stop=True),
                     reads=[wuqb, cqnb.sub(tb)], writes=[ps])
                P.op('act', lambda e, ps=ps, sl=sl, h=h: e.copy(out=QnT[h][:, sl], in_=ps[0:64, :]), reads=[ps], writes=[QnT[h].sub(tb)])
                ps = psum[self._pn % 2]; self._pn += 1
                P.op('pe', lambda e, ps=ps, sl=sl, h=h: e.matmul(ps[0:64, :], lhsT=wukv[:, 128 * h:128 * h + 64], rhs=ckvn[:, sl], start=True, stop=True),
                     reads=[wukv, ckvn.sub(tb)], writes=[ps])
                P.op('act', lambda e, ps=ps, sl=sl, h=h: e.copy(out=KnT[h][:, sl], in_=ps[0:64, :]), reads=[ps], writes=[KnT[h].sub(tb)])
                ps = psum[self._pn % 2]; self._pn += 1
                P.op('pe', lambda e, ps=ps, sl=sl, h=h: e.matmul(ps[0:32, :], lhsT=wuqa[:, 96 * h + 64:96 * h + 96], rhs=cqna[:, sl], start=True, stop=False),
                     reads=[wuqa, cqna.sub(tb)], writes=[ps])
                P.op('pe', lambda e, ps=ps, sl=sl, h=h: e.matmul(ps[0:32, :], lhsT=wuqb[:, 96 * h + 64:96 * h + 96], rhs=cqnb[:, sl], start=False, stop=True),
                     reads=[wuqb, cqnb.sub(tb)], writes=[ps])
                P.op('dve', lambda e, ps=ps, sl=sl: e.tensor_tensor(out=qx[:], in0=ps[0:32, :], in1=cosT[:, sl], op=ALU.mult), reads=[ps, cosT], writes=[qx])
                ps = psum[self._pn % 2]; self._pn += 1
                P.op('pe', lambda e, ps=ps, sl=sl, h=h: e.matmul(ps[0:32, :], lhsT=wra[:, h, :], rhs=cqna[:, sl], start=True, stop=False),
                     reads=[wra, cqna.sub(tb)], writes=[ps])
                P.op('pe', lambda e, ps=ps, sl=sl, h=h: e.matmul(ps[0:32, :], lhsT=wrb[:, h, :], rhs=cqnb[:, sl], start=False, stop=True),
                     reads=[wrb, cqnb.sub(tb)], writes=[ps])
                P.op('dve', lambda e, ps=ps, sl=sl: e.tensor_tensor(out=qx2[:], in0=ps[0:32, :], in1=sinT[:, sl], op=ALU.mult), reads=[ps, sinT], writes=[qx2])
                P.op('dve', lambda e, sl=sl, h=h: e.tensor_tensor(out=QrT[h][:, sl], in0=qx[:], in1=qx2[:], op=ALU.add), reads=[qx, qx2], writes=[QrT[h].sub(tb)])
        for tt in range(16):
            ps = psum[self._pn % 2]; self._pn += 1
            P.op('pe', lambda e, ps=ps, tt=tt: e.matmul(ps[:, 0:256], lhsT=ckvn[:, 128 * tt:128 * tt + 128], rhs=wv[:].rearrange('p h c -> p (h c)'),
                                                        start=True, stop=True), reads=[ckvn.sub(tt // 4), wv], writes=[ps])
            P.op('act', lambda e, ps=ps, tt=tt: e.copy(out=Vtok[:, tt, :], in_=ps[:, 0:256]), reads=[ps], writes=[Vtok])
        cm = P.sb([128, 4, 512], BF16, 'cmB')
        P.op('dve', lambda e: e.tensor_copy(out=cm[:], in_=self.c('cmask').rearrange('p (r f) -> p r f', r=4)), reads=[self.cst], writes=[cm])
        pts = [P.sb([128, 512], BF16, 'ptB') for _ in range(4)]
        obs = [P.sb([64, 512], BF16, 'oB') for _ in range(2)]
        rec = P.sb([64, 512], F32, 'recB')
        blocks = []
        nq = 0
        for h in range(4):
            for qb in range(4):
                nkb = 4 * qb + 4
                for kb in range(nkb):
                    blocks.append((h, qb, kb, nkb, nq))
                nq += 1

        def stage1(i):
            (h, qb, kb, nkb, q_) = blocks[i]
            qsl = slice(512 * qb, 512 * qb + 512)
            ksl = slice(128 * kb, 128 * kb + 128)
            r = kb - 4 * qb
            st = psum[i % 4]; pt = pts[i % 4]
            P.op('pe', lambda e: e.matmul(st[:], lhsT=KnT[h][:, ksl], rhs=QnT[h][:, qsl], start=True, stop=False),
                 reads=[KnT[h].sub(kb // 4), QnT[h].sub(qb)], writes=[st])
            P.op('pe', lambda e: e.matmul(st[:], lhsT=KrT[:, ksl], rhs=QrT[h][:, qsl], start=False, stop=True),
                 reads=[KrT.sub(kb // 4), QrT[h].sub(qb)], writes=[st])
            P.op('act', lambda e: e.activation(out=pt[:], in_=st[:], func=AF.Exp, scale=SC), reads=[st], writes=[pt])
            if r >= 0:
                P.op('dve', lambda e: e.tensor_tensor(out=pt[:], in0=pt[:], in1=cm[:, r, :], op=ALU.mult), reads=[pt, cm], writes=[pt])

        def stage2(i):
            (h, qb, kb, nkb, q_) = blocks[i]
            qsl = slice(512 * qb, 512 * qb + 512)
            pt = pts[i % 4]
            pn = psum[4 + 2 * (q_ % 2)]; pd = psum[5 + 2 * (q_ % 2)]
            P.op('pe', lambda e: e.matmul(pn[0:64, :], lhsT=Vtok[:, kb, 64 * h:64 * h + 64], rhs=pt[:], start=(kb == 0), stop=(kb == nkb - 1)),
                 reads=[Vtok, pt], writes=[pn])
            P.op('pe', lambda e: e.matmul(pd[0:64, :], lhsT=ones[:, 0:64], rhs=pt[:], start=(kb == 0), stop=(kb == nkb - 1)),
                 reads=[ones, pt], writes=[pd])
            if kb == nkb - 1:
                ob = obs[q_ % 2]
                P.op('dve', lambda e: e.reciprocal(out=rec[:], in_=pd[0:64, :]), reads=[pd], writes=[rec])
                P.op('dve', lambda e: e.tensor_tensor(out=ob[:], in0=pn[0:64, :], in1=rec[:], op=ALU.mult), reads=[pn, rec], writes=[ob])
                P.dma('sp', self.ocat_d[4 + h, :, qsl], ob[:], reads=[ob], writes=[self.ocat_d])

        SK = 2
        for i in range(min(SK, len(blocks))):
            stage1(i)
        for i in range(len(blocks)):
            if i + SK < len(blocks):
                stage1(i + SK)
            stage2(i)

    def mixer_D(self, l):
        P = self.P
        xT = self.xT
        psum = self.psum
        self._pn = 0
        W = P.sb([128, 8, 768], BF16, 'wD')
        P.dma('pool', W[:], self.w_in_d[l, :, 2408:3176].rearrange('(k p) n -> p k n', p=128), writes=[W])
        ones = P.sb([128, 64], BF16, 'onesD')
        P.op('pool', lambda e: e.memset(ones[:], 1.0), writes=[ones])
        geoms = [(1, 0, 'tri_ge'), (1, 128, 'tri_le'), (4, 0, 'tri_ge'), (4, 128, 'tri_le'), (16, 0, 'tri_ge')]
        tabs = []
        tmpf = P.sb([128, 128], F32, 'tabtmp')
        for (dil, off, mk) in geoms:
            tab = P.sb([128, 4, 128], BF16, 'tabD')
            for h in range(4):
                slope = 2.0 ** (-8.0 * (h + 1) / 4)
                P.op('dve', lambda e, mk=mk: e.tensor_tensor(out=tmpf[:], in0=self.c('dist'), in1=self.c(mk), op=ALU.mult),
                     reads=[self.cst], writes=[tmpf])
                P.op('act', lambda e, dil=dil, off=off, slope=slope: e.activation(
                    out=tmpf[:], in_=tmpf[:], func=AF.Exp, scale=-slope * dil, bias=-slope * dil * off),
                    reads=[tmpf], writes=[tmpf])
                P.op('dve', lambda e, tab=tab, h=h, mk=mk: e.tensor_tensor(out=tab[:, h, :], in0=tmpf[:], in1=self.c(mk), op=ALU.mult),
                     reads=[tmpf, self.cst], writes=[tab])
            tabs.append(tab)
        QT = [P.sb([64, S], BF16, 'qTD') for _ in range(4)]
        KT = [P.sb([64, S], BF16, 'kTD') for _ in range(4)]
        n = 0
        for i in range(8):
            for tb in range(4):
                ps = psum[n % 2]; n += 1
                for kc in range(8):
                    P.op('pe', lambda e, ps=ps, kc=kc, i=i, tb=tb: e.matmul(
                        ps[0:64, :], lhsT=W[:, kc, 64 * i:64 * i + 64], rhs=xT[:, kc, 512 * tb:512 * tb + 512],
                        start=(kc == 0), stop=(kc == 7)), reads=[W, xT.sub(tb)], writes=[ps])
                dst = QT[i] if i < 4 else KT[i - 4]
                P.op('act', lambda e, ps=ps, dst=dst, tb=tb, i=i: e.mul(out=dst[:, 512 * tb:512 * tb + 512], in_=ps[0:64, :],
                                                                       mul=(0.125 if i < 4 else 1.0)), reads=[ps], writes=[dst])
        NUM = P.sb([64, 4, S], F32, 'numD')
        DEN = P.sb([64, 4, S], F32, 'denD')
        Vt = [P.sb([128, 256], BF16, 'vD') for _ in range(3)]
        PC = [P.sb([128, 4, 128], BF16, 'pcD') for _ in range(2)]
        PP = [P.sb([128, 4, 128], BF16, 'ppD') for _ in range(2)]
        units = []
        for nb in range(16):
            units.append((0, slice(128 * nb, 128 * nb + 128), slice(128 * (nb - 1), 128 * nb) if nb > 0 else None, 0, 1))
        for r in range(4):
            for b in range(4):
                units.append((1, slice(512 * b + r, 512 * (b + 1), 4), slice(512 * (b - 1) + r, 512 * b, 4) if b > 0 else None, 2, 3))
        for r in range(16):
            units.append((2, slice(r, S, 16), None, 4, None))
        nv = 0
        vts = {}

        def stage1(u):
            nonlocal n, nv
            (br, Tq, Tp, gc, gp) = units[u]
            vcur = Vt[nv % 3]; nv += 1
            vts[u] = vcur
            ps = psum[n % 2]; n += 1
            for kc in range(8):
                P.op('pe', lambda e, ps=ps, kc=kc, Tq=Tq: e.matmul(
                    ps[:, 0:256], lhsT=xT[:, kc, Tq], rhs=W[:, kc, 512:768], start=(kc == 0), stop=(kc == 7)),
                    reads=[W, xT.sub(0), xT.sub(1), xT.sub(2), xT.sub(3)], writes=[ps])
            P.op('act', lambda e, ps=ps, vcur=vcur: e.copy(out=vcur[:], in_=ps[:, 0:256]), reads=[ps], writes=[vcur])
            sc = psum[2 + 2 * (u % 2)]
            sp_ = psum[3 + 2 * (u % 2)]
            pc = PC[u % 2]; pp = PP[u % 2]
            for h in range(4):
                P.op('pe', lambda e, h=h, sc=sc, Tq=Tq: e.matmul(
                    sc[:, 128 * h:128 * h + 128], lhsT=KT[h][:, Tq], rhs=QT[h][:, Tq], start=True, stop=True),
                    reads=[KT[h], QT[h]], writes=[sc])
            P.op('act', lambda e, sc=sc, pc=pc: e.activation(out=pc[:].rearrange('p h q -> p (h q)'), in_=sc[:], func=AF.Exp),
                 reads=[sc], writes=[pc])
            P.op('dve', lambda e, pc=pc, gc=gc: e.tensor_tensor(out=pc[:], in0=pc[:], in1=tabs[gc][:], op=ALU.mult),
                 reads=[pc, tabs[gc]], writes=[pc])
            if Tp is not None:
                for h in range(4):
                    P.op('pe', lambda e, h=h, sp_=sp_, Tq=Tq, Tp=Tp: e.matmul(
                        sp_[:, 128 * h:128 * h + 128], lhsT=KT[h][:, Tp], rhs=QT[h][:, Tq], start=True, stop=True),
                        reads=[KT[h], QT[h]], writes=[sp_])
                P.op('act', lambda e, sp_=sp_, pp=pp: e.activation(out=pp[:].rearrange('p h q -> p (h q)'), in_=sp_[:], func=AF.Exp),
                     reads=[sp_], writes=[pp])
                P.op('dve', lambda e, pp=pp, gp=gp: e.tensor_tensor(out=pp[:], in0=pp[:], in1=tabs[gp][:], op=ALU.mult),
                     reads=[pp, tabs[gp]], writes=[pp])

        def stage2(u):
            (br, Tq, Tp, gc, gp) = units[u]
            vcur = vts[u]; vprev = vts.get(u - 1)
            pc = PC[u % 2]; pp = PP[u % 2]
            pn = psum[6]; pd = psum[7]
            for h in range(4):
                P.op('pe', lambda e, h=h, pc=pc, vcur=vcur, last=(Tp is None): e.matmul(
                    pn[0:64, 128 * h:128 * h + 128], lhsT=vcur[:, 64 * h:64 * h + 64], rhs=pc[:, h, :], start=True, stop=last),
                    reads=[vcur, pc], writes=[pn])
                if Tp is not None:
                    P.op('pe', lambda e, h=h, pp=pp, vprev=vprev: e.matmul(
                        pn[0:64, 128 * h:128 * h + 128], lhsT=vprev[:, 64 * h:64 * h + 64], rhs=pp[:, h, :], start=False, stop=True),
                        reads=[vprev, pp], writes=[pn])
                P.op('pe', lambda e, h=h, pc=pc, last=(Tp is None): e.matmul(
                    pd[0:64, 128 * h:128 * h + 128], lhsT=ones[:], rhs=pc[:, h, :], start=True, stop=last),
                    reads=[ones, pc], writes=[pd])
                if Tp is not None:
                    P.op('pe', lambda e, h=h, pp=pp: e.matmul(
                        pd[0:64, 128 * h:128 * h + 128], lhsT=ones[:], rhs=pp[:, h, :], start=False, stop=True),
                        reads=[ones, pp], writes=[pd])
            for (acc_t, pt) in ((NUM, pn), (DEN, pd)):
                src = pt[0:64, :].rearrange('p (h q) -> p h q', h=4)
                if br == 0:
                    P.op('act', lambda e, acc_t=acc_t, src=src, Tq=Tq: e.copy(out=acc_t[:, :, Tq], in_=src), reads=[pt], writes=[acc_t])
                else:
                    P.op('dve', lambda e, acc_t=acc_t, src=src, Tq=Tq: e.tensor_tensor(out=acc_t[:, :, Tq], in0=acc_t[:, :, Tq], in1=src, op=ALU.add),
                         reads=[pt, acc_t], writes=[acc_t])

        stage1(0)
        for u in range(len(units)):
            if u + 1 < len(units):
                stage1(u + 1)
            stage2(u)
        ob = [P.sb([64, S], BF16, 'oD') for _ in range(2)]
        for h in range(4):
            o = ob[h % 2]
            P.op('dve', lambda e, h=h: e.reciprocal(out=DEN[:, h, :], in_=DEN[:, h, :]), reads=[DEN], writes=[DEN])
            P.op('dve', lambda e, h=h, o=o: e.tensor_tensor(out=o[:], in0=NUM[:, h, :], in1=DEN[:, h, :], op=ALU.mult), reads=[NUM, DEN], writes=[o])
            P.dma('sp', self.ocat_d[12 + h], o[:], reads=[o], writes=[self.ocat_d])

    def mixer_stub(self, l):
        P = self.P
        W = P.sb([128, 8, 1024], BF16, 'wstub')
        P.dma('pool', W[:], self.w_in_d[l, :, 0:1024].rearrange('(k p) n -> p k n', p=128), writes=[W])
        ob = [P.sb([64, 512], BF16, 'ostub') for _ in range(2)]
        n = 0
        for c in range(16):
            for tb in range(4):
                ps = self.psum[n % 2]
                o = ob[n % 2]
                n += 1
                for kc in range(8):
                    P.op('pe', lambda e, ps=ps, kc=kc, c=c, tb=tb: e.matmul(
                        ps[0:64, :], lhsT=W[:, kc, 64 * c:64 * c + 64], rhs=self.xT[:, kc, 512 * tb:512 * tb + 512],
                        start=(kc == 0), stop=(kc == 7)), reads=[W, self.xT.sub(tb)], writes=[ps])
                P.op('act', lambda e, ps=ps, o=o: e.copy(out=o[:], in_=ps[0:64, :]), reads=[ps], writes=[o])
                P.dma('sp', self.ocat_d[c, :, 512 * tb:512 * tb + 512], o[:], reads=[o], writes=[self.ocat_d])

    def ln_tile(self, r, g_rep, b_rep, out, r_ap=None, r_res=None, slot=None):
        for _ in self.ln_tile_gen(r, g_rep, b_rep, out, r_ap, r_res, slot):
            pass

    def ln_tile_gen(self, r, g_rep, b_rep, out, r_ap=None, r_res=None, slot=None):
        P = self.P
        if r_ap is None:
            r_ap = r[:]
            r_res = r
        rl = list(r_res) if isinstance(r_res, (list, tuple)) else [r_res]
        if slot is None:
            self._lnk += 1
            slot = self._lnk % 2
        st, mv, sc = self._lntmp[slot]
        for h in range(2):
            P.op('dve', lambda e, h=h: e.bn_stats(out=st[:, h, :], in_=r_ap[:, 512 * h:512 * h + 512]), reads=rl, writes=[st])
        P.op('dve', lambda e: e.bn_aggr(out=mv[:], in_=st[:].rearrange('p a b -> p (a b)')), reads=[st], writes=[mv])
        yield
        P.op('act', lambda e: e.activation(out=sc[:, 0:1], in_=mv[:, 1:2], func=AF.Ln, bias=EPS), reads=[mv], writes=[sc])
        P.op('act', lambda e: e.activation(out=sc[:, 0:1], in_=sc[:, 0:1], func=AF.Exp, scale=-0.5), reads=[sc], writes=[sc])
        yield
        P.op('dve', lambda e: e.scalar_tensor_tensor(out=sc[:, 1:2], in0=mv[:, 0:1], scalar=-1.0, in1=sc[:, 0:1],
                                                     op0=ALU.mult, op1=ALU.mult), reads=[mv, sc], writes=[sc])
        yield
        P.op('act', lambda e: e.activation(out=out[:], in_=r_ap, func=AF.Identity, bias=sc[:, 1:2], scale=sc[:, 0:1]),
             reads=rl + [sc], writes=[out])
        yield
        P.op('dve', lambda e: e.tensor_tensor(out=out[:], in0=out[:], in1=g_rep[:], op=ALU.mult), reads=[out, g_rep], writes=[out])
        P.op('dve', lambda e: e.tensor_tensor(out=out[:], in0=out[:], in1=b_rep[:], op=ALU.add), reads=[out, b_rep], writes=[out])
        yield

    def dense(self, l, last):
        P = self.P
        xT = self.xT
        acc = P.sb([128, 16, D], F32, 'acc')
        self._lnk = 0
        self._lntmp = [(P.sb([128, 2, 6], F32, 'bnst'), P.sb([128, 2], F32, 'mv'), P.sb([128, 2], F32, 'lnsc')) for _ in range(2)]
        x1 = [P.sb([128, D], F32, 'x1') for _ in range(2)]
        x_src = self.x_d if l == 0 else self.xres_d
        with P.scope():
            g1 = P.sb([128, D], F32, 'g1'); b1 = P.sb([128, D], F32, 'b1')
            for t, d in ((g1, self.ln1_g_d), (b1, self.ln1_b_d)):
                P.dma('sp', t[:], d[l:l + 1, :].partition_broadcast(128), writes=[t])
            wout = P.sb([64, 16, D], BF16, 'wout')
            P.dma('pool', wout[:], self.w_out_d[l].rearrange('(c p) n -> p c n', p=64), writes=[wout])
            ocs = [P.sb([64, 16, 512], BF16, 'oc') for _ in range(2)]
            xr = [P.sb([128, D], F32, 'xr') for _ in range(2)]
            rr = [P.sb([128, D], F32, 'rr') for _ in range(2)]
            def ln1_gen(tt):
                tb = tt // 4
                oc = ocs[tb % 2]
                xrt = xr[tt % 2]; r = rr[tt % 2]; x1t = x1[tt % 2]
                P.dma('sp', xrt[:], x_src[128 * tt:128 * tt + 128, :], reads=[x_src], writes=[xrt])
                for half in range(2):
                    ps = self.psum[2 * (tt % 2) + half]
                    for c in range(16):
                        P.op('pe', lambda e, ps=ps, c=c, half=half: e.matmul(
                            ps[:], lhsT=oc[:, c, 128 * (tt % 4):128 * (tt % 4) + 128], rhs=wout[:, c, 512 * half:512 * half + 512],
                            start=(c == 0), stop=(c == 15)), reads=[oc, wout], writes=[ps])
                yield
                for half in range(2):
                    ps = self.psum[2 * (tt % 2) + half]
                    P.op('dve', lambda e, ps=ps, half=half: e.scalar_tensor_tensor(
                        out=r[:, 512 * half:512 * half + 512], in0=xrt[:, 512 * half:512 * half + 512], scalar=ALPHA, in1=ps[:],
                        op0=ALU.mult, op1=ALU.add), reads=[ps, xrt], writes=[r])
                yield
                yield from self.ln_tile_gen(r, g1, b1, x1t, slot=tt % 2)
                P.op('act', lambda e: e.mul(out=acc[:, tt, :], in_=x1t[:], mul=ALPHA), reads=[x1t], writes=[acc.sub((tt, 0)), acc.sub((tt, 1))])
                self.transpose_tile_to_xT(x1t, tt, pbase=4 + 2 * (tt % 2))
                yield

            for tp in range(8):
                if tp % 2 == 0:
                    tb = tp // 2
                    P.dma('sp', ocs[tb % 2][:], self.ocat_d[:, :, 512 * tb:512 * tb + 512].rearrange('c p t -> p c t'),
                          reads=[self.ocat_d], writes=[ocs[tb % 2]])
                lockstep([ln1_gen(2 * tp), ln1_gen(2 * tp + 1)])
        with P.scope():
            g2 = P.sb([128, D], F32, 'g2'); b2 = P.sb([128, D], F32, 'b2')
            for t, d in ((g2, self.ln2_g_d), (b2, self.ln2_b_d)):
                P.dma('sp', t[:], d[l:l + 1, :].partition_broadcast(128), writes=[t])
            HC = 1024
            nhc = D_FF // HC
            NM = HC // 128
            w1s = [P.sb([128, 8, HC], BF16, 'w1') for _ in range(2)]
            w2s = [P.sb([128, NM, D], BF16, 'w2') for _ in range(2)]
            fts = [P.sb([128, NM, 512], BF16, 'fT') for _ in range(2)]
            nf = 0
            n1 = 0
            n2 = 0
            for j in range(nhc):
                w1 = w1s[j % 2]; w2 = w2s[j % 2]
                P.dma('pool', w1[:], self.w_ff1_d[l, :, HC * j:HC * j + HC].rearrange('(k p) n -> p k n', p=128), writes=[w1])
                P.dma('pool', w2[:], self.w_ff2_d[l, HC * j:HC * j + HC, :].rearrange('(m p) n -> p m n', p=128), writes=[w2])
                for tb in range(4):
                    ft = fts[nf % 2]; nf += 1
                    for m in range(NM):
                        ps = self.psum[n1 % 3]; n1 += 1
                        for kc in range(8):
                            P.op('pe', lambda e, ps=ps, kc=kc, m=m, w1=w1, tb=tb: e.matmul(
                                ps[:], lhsT=w1[:, kc, 128 * m:128 * m + 128], rhs=xT[:, kc, 512 * tb:512 * tb + 512],
                                start=(kc == 0), stop=(kc == 7)), reads=[w1, xT.sub(tb)], writes=[ps])
                        P.op('act', lambda e, ps=ps, m=m, ft=ft: e.activation(out=ft[:, m, :], in_=ps[:], func=AF.Relu),
                             reads=[ps], writes=[ft.sub(m)])
                        P.op('act', lambda e, m=m, ft=ft: e.activation(out=ft[:, m, :], in_=ft[:, m, :], func=AF.Square),
                             reads=[ft.sub(m)], writes=[ft.sub(m)])
                    for t4 in range(4):
                        tt = 4 * tb + t4
                        for half in range(2):
                            ps = self.psum[3 + (n2 % 5)]; n2 += 1
                            for m in range(NM):
                                P.op('pe', lambda e, ps=ps, m=m, ft=ft, t4=t4, w2=w2, half=half: e.matmul(
                                    ps[:], lhsT=ft[:, m, 128 * t4:128 * t4 + 128], rhs=w2[:, m, 512 * half:512 * half + 512],
                                    start=(m == 0), stop=(m == NM - 1)), reads=[ft.sub(m), w2], writes=[ps])
                            P.op('dve', lambda e, ps=ps, tt=tt, half=half: e.tensor_tensor(
                                out=acc[:, tt, 512 * half:512 * half + 512], in0=acc[:, tt, 512 * half:512 * half + 512], in1=ps[:], op=ALU.add),
                                reads=[ps, acc.sub((tt, half))], writes=[acc.sub((tt, half))])
            def ln2_gen(tt):
                x2 = x1[tt % 2]
                yield from self.ln_tile_gen(None, g2, b2, x2, r_ap=acc[:, tt, :], r_res=[acc.sub((tt, 0)), acc.sub((tt, 1))], slot=tt % 2)
                if last:
                    P.dma('sp', self.y_d[128 * tt:128 * tt + 128, :], x2[:], reads=[x2], writes=[self.y_d])
                else:
                    P.dma('sp', self.xres_d[128 * tt:128 * tt + 128, :], x2[:], reads=[x2], writes=[self.xres_d])
                    self.transpose_tile_to_xT(x2, tt, pbase=4 + 2 * (tt % 2))
                yield

            for tp in range(8):
                lockstep([ln2_gen(2 * tp), ln2_gen(2 * tp + 1)])


def lockstep(gens):
    gens = list(gens)
    while gens:
        nxt = []
        for g in gens:
            try:
                next(g)
                nxt.append(g)
            except StopIteration:
                pass
        gens = nxt


def _rope_table():
    half = 16
    inv_freq = (np.float32(10000.0) ** (-np.arange(half, dtype=np.float32) / np.float32(half))).astype(np.float32)
    ang = (np.arange(S, dtype=np.float32)[None, :] * inv_freq[:, None]).astype(np.float32)
    ang = np.concatenate([ang, ang], axis=0)
    return np.stack([np.cos(ang), np.sin(ang)]).astype(np.float32)


ROPE_TAB = _rope_table()
RMASK = np.stack([(np.arange(S) % 16 != 0), (np.arange(S) % 64 != 0)]).astype(np.float32)
_CACHE = {}


def get_builder(stage='full', nlayers=DEPTH):
    key = (stage, nlayers)
    if key not in _CACHE:
        _CACHE[key] = Builder(stage, nlayers)
    return _CACHE[key]


def make_in_maps(b, inputs, cores):
    maps = []
    for ci in cores:
        m = {'x': np.ascontiguousarray(inputs['x'][ci]), 'cst': b.cst_np, 'rope': ROPE_TAB, 'rmask': RMASK}
        for k in ('w_in', 'w_out', 'w_ff1', 'w_ff2', 'ln1_g', 'ln1_b', 'ln2_g', 'ln2_b',
                  'mla_q_norm_g', 'mla_kv_norm_g', 'mla_w_uq', 'mla_w_ukv', 'hgrn_lb_logits', 'hgrn_norm_g',
                  'gdn_conv_w', 'gdn_a_log', 'gdn_dt_bias', 'gdn_norm_g'):
            m[k] = np.ascontiguousarray(inputs[k])
        maps.append(m)
    return maps


def kernel(**inputs):
    inputs = {k: np.asarray(v) for k, v in inputs.items()}
    b = get_builder('full')
    maps = make_in_maps(b, inputs, list(range(NCORES)))
    res = run_bass_kernel_spmd(b.nc, maps, core_ids=list(range(NCORES)))
    return np.stack([r['y'] for r in res.results], axis=0).astype(np.float32)
```

```python
import math
import numpy as np
import concourse.bass as bass
import concourse.mybir as mybir
from concourse.bass_utils import run_bass_kernel_spmd
from contextlib import ExitStack, contextmanager

F32 = mybir.dt.float32
BF16 = mybir.dt.bfloat16
AF = mybir.ActivationFunctionType
ALU = mybir.AluOpType
AX = mybir.AxisListType

COMPUTE = ('pe', 'act', 'dve', 'pool')
ENGS = ('pe', 'act', 'dve', 'pool', 'sp')
N_DMA_SEMS = 6

S = 2048
D = 1024
NCORES = 8
DEPTH = 2
D_IN = 3176
D_FF = 4096
ALPHA = (2 * DEPTH) ** 0.25
EPS = 1e-6


class Res:
    __slots__ = ('name', 'w', 'r')

    def __init__(self, name='', barrier=None):
        self.name = name
        self.w = None
        self.r = dict(barrier) if barrier else {}


class T:
    def __init__(self, handle, name, barrier=None):
        self.h = handle
        self.name = name
        self.res = Res(name, barrier)
        self.subs = {}
        self._barrier = barrier

    def __getitem__(self, k):
        return self.h[k]

    def sub(self, key):
        r = self.subs.get(key)
        if r is None:
            r = self.subs[key] = Res(f'{self.name}.{key}', self._barrier)
        return r

    def all_res(self):
        return [self.res] + list(self.subs.values())


def _res(x):
    return x.res if isinstance(x, T) else x


def ones_f(P, b):
    t = getattr(b, '_ones_f', None)
    if t is None:
        raise RuntimeError('ones_f not allocated')
    return t


def _mx(d, k, v):
    if v > d.get(k, -1):
        d[k] = v


class Prog:
    def __init__(self, nc):
        self.nc = nc
        self.stack = ExitStack()
        self.ins = {e: [] for e in ENGS}
        self.nalloc = 0
        self.barrier = {}
        self.scopes = []
        self.dma_k = {e: 0 for e in ENGS}
        self.dma_cnt = {}

    def _reg(self, t):
        if self.scopes:
            self.scopes[-1][1].append(t)
        return t

    def sb(self, shape, dt=F32, name=None):
        self.nalloc += 1
        name = f'{name or "t"}{self.nalloc}'
        st = self.scopes[-1][0] if self.scopes else self.stack
        h = st.enter_context(self.nc.sbuf_tensor(name, list(shape), dt))
        return self._reg(T(h, name, dict(self.barrier)))

    def ps(self, shape, dt=F32, name=None):
        self.nalloc += 1
        name = f'{name or "p"}{self.nalloc}'
        h = self.stack.enter_context(self.nc.psum_tensor(name, list(shape), dt))
        return T(h, name)

    def dram(self, shape, dt=F32, name=None, kind='Internal'):
        self.nalloc += 1
        name = name or f'd{self.nalloc}'
        h = self.nc.dram_tensor(name, list(shape), dt, kind=kind)
        return T(h.ap(), name)

    @contextmanager
    def scope(self):
        st = ExitStack()
        tiles = []
        self.scopes.append((st, tiles))
        try:
            yield
        finally:
            self.scopes.pop()
            st.close()
            b = self.barrier
            for t in tiles:
                for r in t.all_res():
                    if r.w is not None:
                        _mx(b, r.w[0], r.w[1])
                    for k, v in r.r.items():
                        _mx(b, k, v)

    def op(self, eng, fn, reads=(), writes=(), dma=False):
        idx = len(self.ins[eng])
        deps = {}
        for r in reads:
            r = _res(r)
            if r.w is not None:
                _mx(deps, r.w[0], r.w[1])
        for w in writes:
            w = _res(w)
            if w.w is not None:
                _mx(deps, w.w[0], w.w[1])
            for k, v in w.r.items():
                _mx(deps, k, v)
        rec = dict(fn=fn, deps=deps, dma=dma, inc=False)
        if dma:
            key = ('d', eng, self.dma_k[eng] % N_DMA_SEMS)
            self.dma_k[eng] += 1
            prev = self.dma_cnt.get(key, 0)
            if prev:
                _mx(deps, key, prev)
            self.dma_cnt[key] = prev + 16
            tok = (key, prev + 16)
            rec['dsem'] = key
        else:
            tok = (('c', eng), idx)
        self.ins[eng].append(rec)
        for r in reads:
            _mx(_res(r).r, tok[0], tok[1])
        for w in writes:
            w = _res(w)
            w.w = tok
            w.r = {}
        return tok

    def dma(self, eng, out, in_, reads=(), writes=(), **kw):
        return self.op(eng, lambda e: e.dma_start(out=out, in_=in_, **kw), reads=reads, writes=writes, dma=True)

    def emit(self):
        nc = self.nc
        ins = self.ins
        for E in ENGS:
            seen = {}
            for idx, rec in enumerate(ins[E]):
                waits = []
                for k, val in rec['deps'].items():
                    if k == ('c', 'pe') and E == 'pe':
                        continue
                    if k == ('c', E) and val >= idx:
                        continue
                    if val <= seen.get(k, -1):
                        continue
                    seen[k] = val
                    waits.append((k, val))
                    if k[0] == 'c':
                        ins[k[1]][val]['inc'] = True
                rec['waits'] = waits
        for e in ENGS:
            c = 0
            for rec in ins[e]:
                if rec['inc'] and not rec['dma']:
                    c += 1
                    rec['semval'] = c
        st = self.stack
        csem = {e: st.enter_context(nc.semaphore(f's_{e}')) for e in ENGS}
        dsem = {key: st.enter_context(nc.semaphore(f'd_{key[1]}{key[2]}')) for key in self.dma_cnt}
        block = st.enter_context(nc.Block())

        def replay(E, eng):
            for rec in ins[E]:
                for (k, val) in rec['waits']:
                    if k[0] == 'c':
                        eng.wait_ge(csem[k[1]], ins[k[1]][val]['semval'])
                    else:
                        eng.wait_ge(dsem[k], val)
                r = rec['fn'](eng)
                if rec['dma']:
                    r.then_inc(dsem[rec['dsem']], 16)
                elif rec['inc']:
                    r.then_inc(csem[E], 1)

        @block.tensor
        def _(e):
            replay('pe', e)

        @block.scalar
        def _(e):
            replay('act', e)

        @block.vector
        def _(e):
            replay('dve', e)

        @block.gpsimd
        def _(e):
            replay('pool', e)

        @block.sync
        def _(e):
            replay('sp', e)

    def close(self):
        self.stack.close()


def make_consts():
    c = {}
    c['ident'] = np.eye(128, dtype=np.float32)
    kj = np.arange(128)[:, None]
    qi = np.arange(128)[None, :]
    c['dist'] = (qi - kj).astype(np.float32)
    c['tri_ge'] = (qi >= kj).astype(np.float32)
    c['tri_le'] = (qi <= kj).astype(np.float32)
    c['mask16'] = ((qi >= kj) & (qi // 16 == kj // 16)).astype(np.float32)
    c['cm8'] = (np.arange(128)[:, None] // 16 == np.arange(8)[None, :]).astype(np.float32)
    f = np.arange(512)[None, None, :]
    p = np.arange(128)[:, None, None]
    r = np.arange(4)[None, :, None]
    c['cmask'] = (f >= 128 * r + p).astype(np.float32)
    return c


CONST_ORDER = ['ident', 'dist', 'tri_ge', 'tri_le', 'mask16', 'cm8', 'cmask']


def pack_consts():
    c = make_consts()
    cols = {}
    arrs = []
    off = 0
    for k in CONST_ORDER:
        a = c[k].reshape(128, -1).astype(np.float32)
        cols[k] = (off, a.shape[1])
        arrs.append(a)
        off += a.shape[1]
    return np.concatenate(arrs, axis=1), cols


class Builder:
    def __init__(self, stage='full', nlayers=DEPTH):
        self.stage = stage
        self.nlayers = nlayers
        nc = self.nc = bass.Bass("TRN2", target_bir_lowering=False)
        P = self.P = Prog(nc)
        self.cst_np, self.cst_cols = pack_consts()
        ein = lambda name, shape: P.dram(shape, F32, name, kind='ExternalInput')
        self.x_d = ein('x', [S, D])
        self.w_in_d = ein('w_in', [DEPTH, D, D_IN])
        self.w_out_d = ein('w_out', [DEPTH, D, D])
        self.w_ff1_d = ein('w_ff1', [DEPTH, D, D_FF])
        self.w_ff2_d = ein('w_ff2', [DEPTH, D_FF, D])
        self.ln1_g_d = ein('ln1_g', [DEPTH, D])
        self.ln1_b_d = ein('ln1_b', [DEPTH, D])
        self.ln2_g_d = ein('ln2_g', [DEPTH, D])
        self.ln2_b_d = ein('ln2_b', [DEPTH, D])
        self.cst_d = ein('cst', list(self.cst_np.shape))
        self.rope_d = ein('rope', [2, 32, S])
        self.rmask_d = ein('rmask', [2, S])
        self.gdn_conv_d = ein('gdn_conv_w', [DEPTH, 4, 768])
        self.gdn_alog_d = ein('gdn_a_log', [DEPTH, 4])
        self.gdn_dtb_d = ein('gdn_dt_bias', [DEPTH, 4])
        self.gdn_g_d = ein('gdn_norm_g', [DEPTH, 64])
        _k = 'ExternalOutput' if stage == 'A' else 'Internal'
        self.kkd = P.dram([128, 64, 64], F32, 'kk_scr', kind=_k)
        self.qkd = P.dram([128, 64, 64], F32, 'qk_scr', kind=_k)
        self.ttd = P.dram([128, 64, 64], F32, 'tt_scr', kind=_k)
        self.qktd = P.dram([128, 64, 64], F32, 'qkt_scr', kind=_k)
        self.tokd = P.dram([4, 3, 64, 32, 64], F32, 'tok_scr', kind=_k)
        self.qed = P.dram([4, 64, S], F32, 'qe_scr', kind=_k)
        self.gsd = P.dram([4, 64, S], F32, 'gs_scr', kind=_k)
        self.rows8d = P.dram([2, 8, S], F32, 'rows8_scr', kind=_k)
        self.qhd = P.dram([4, 64, S], F32, 'qh_scr')
        self.ohd = P.dram([4, 64, S], F32, 'oh_scr')
        self.hgrn_lb_d = ein('hgrn_lb_logits', [DEPTH, 256])
        self.hgrn_g_d = ein('hgrn_norm_g', [DEPTH, 64])
        self.mla_qg_d = ein('mla_q_norm_g', [DEPTH, 192])
        self.mla_kvg_d = ein('mla_kv_norm_g', [DEPTH, 128])
        self.mla_wuq_d = ein('mla_w_uq', [DEPTH, 192, 384])
        self.mla_wukv_d = ein('mla_w_ukv', [DEPTH, 128, 512])
        self.y_d = P.dram([S, D], F32, 'y', kind='ExternalOutput')
        self.ocat_d = P.dram([16, 64, S], BF16, 'ocat_scr', kind=('Internal' if stage in ('full', 'dense') else 'ExternalOutput'))
        self.xres_d = P.dram([S, D], F32, 'xres_scr')
        self.dbg = {}
        self.cst = P.sb(list(self.cst_np.shape), F32, 'cst')
        P.dma('sp', self.cst[:], self.cst_d[:], writes=[self.cst])
        self.xT = P.sb([128, 8, S], BF16, 'xT')
        self._ones_f = P.sb([32, 64], F32, 'ones_f')
        P.op('pool', lambda e: e.memset(self._ones_f[:], 1.0), writes=[self._ones_f])
        self.psum = [P.ps([128, 512], F32, f'ps{i}') for i in range(8)]
        self.build()
        outs = [self.y_d, self.ocat_d, self.kkd, self.qkd, self.ttd, self.qktd, self.tokd, self.qed, self.gsd, self.rows8d] + [t for t in self.dbg.values()]
        P.op('sp', lambda e: e.nop(), reads=outs)
        P.emit()
        P.close()

    def dump(self, name, t, ap, shape):
        if self.stage in ('full', 'dense'):
            return
        d = self.P.dram(list(shape), F32, 'dbg_' + name, kind='ExternalOutput')
        self.P.dma('sp', d[:], ap, reads=[t], writes=[d])
        self.dbg[name] = d

    def c(self, name):
        o, n = self.cst_cols[name]
        return self.cst[:, o:o + n]

    def build(self):
        P = self.P
        self.load_x_transposed()
        for l in range(self.nlayers):
            last = (l == self.nlayers - 1)
            with P.scope():
                self.mixers(l)
            if self.stage in ('full', 'dense'):
                with P.scope():
                    self.dense(l, last)

    def load_x_transposed(self):
        P = self.P
        xT = self.xT
        with P.scope():
            xt = [P.sb([128, D], F32, 'xin') for _ in range(2)]
            for tt in range(16):
                t = xt[tt % 2]
                P.dma('sp', t[:], self.x_d[128 * tt:128 * tt + 128, :], writes=[t])
                self.transpose_tile_to_xT(t, tt)

    def transpose_tile_to_xT(self, t, tt, pbase=6):
        P = self.P
        ident = self.c('ident')
        for half in range(2):
            ps = self.psum[pbase + half]
            for j in range(4):
                kc = half * 4 + j
                P.op('pe', lambda e, ps=ps, j=j, kc=kc: e.transpose(ps[:, 128 * j:128 * j + 128], t[:, 128 * kc:128 * kc + 128], ident),
                     reads=[t, self.cst], writes=[ps])
            P.op('act', lambda e, ps=ps, half=half: e.copy(
                out=self.xT[:, 4 * half:4 * half + 4, 128 * tt:128 * tt + 128],
                in_=ps[:].rearrange('p (k t) -> p k t', k=4)),
                reads=[ps], writes=[self.xT.sub(tt // 4)])

    def mixers(self, l):
        if self.stage == 'dense':
            self.mixer_stub(l)
            return
        if self.stage in ('full', 'D'):
            with self.P.scope():
                self.mixer_D(l)

        if self.stage in ('full', 'A'):
            with self.P.scope():
                self.mixer_A(l)
        if self.stage in ('full', 'B'):
            with self.P.scope():
                self.mixer_B(l)
        if self.stage in ('full', 'C'):
            with self.P.scope():
                self.mixer_C(l)

    def rms_gate_out(self, o, gate, gcol, ones, slot, tb, sq, rs, ob):
        for _ in self.rms_gate_out_gen(o, gate, gcol, ones, slot, tb, sq, rs, ob):
            pass

    def rms_gate_out_gen(self, o, gate, gcol, ones, slot, tb, sq, rs, ob):
        P = self.P
        sl = slice(512 * tb, 512 * tb + 512)
        P.op('act', lambda e: e.activation(out=sq[:], in_=o[:], func=AF.Square), reads=[o], writes=[sq])
        yield
        ps = self.psum[self._pn % 2]; self._pn += 1
        P.op('pe', lambda e: e.matmul(ps[0:64, :], lhsT=ones[0:64, 0:64], rhs=sq[:], start=True, stop=True), reads=[ones, sq], writes=[ps])
        yield
        P.op('act', lambda e: e.activation(out=rs[:], in_=ps[0:64, :], func=AF.Ln, scale=1.0 / 64, bias=EPS), reads=[ps], writes=[rs])
        P.op('act', lambda e: e.activation(out=rs[:], in_=rs[:], func=AF.Exp, scale=-0.5), reads=[rs], writes=[rs])
        yield
        P.op('dve', lambda e: e.scalar_tensor_tensor(out=o[:], in0=o[:], scalar=gcol, in1=rs[:], op0=ALU.mult, op1=ALU.mult),
             reads=[o, rs], writes=[o])
        P.op('dve', lambda e: e.tensor_tensor(out=ob[:], in0=o[:], in1=gate[:, sl], op=ALU.mult), reads=[o, gate, gate.sub(tb)], writes=[ob])
        yield
        P.dma('sp', self.ocat_d[slot, :, sl], ob[:], reads=[ob], writes=[self.ocat_d])
        yield

    def mixer_A(self, l):
        P = self.P
        psum = self.psum
        xT = self.xT
        self._pn = 0
        ident = self.c('ident')
        ones = P.sb([128, 128], BF16, 'onesA')
        P.op('pool', lambda e: e.memset(ones[:], 1.0), writes=[ones])
        gn = P.sb([64, 1], F32, 'gnA')
        P.dma('sp', gn[:], self.gdn_g_d[l, :].rearrange('(p o) -> p o', o=1), writes=[gn])
        SC = {}
        for nm in ('beta', 'e', 'ed', 'egl'):
            SC[nm] = P.sb([64, 32, 8], F32, 'sc_' + nm)
        be = P.sb([64, 32, 4], F32, 'sc_be')
        with P.scope():
            WA = P.sb([128, 8, 1032], BF16, 'wA')
            P.dma('pool', WA[:], self.w_in_d[l, :, 0:1032].rearrange('(k p) n -> p k n', p=128), writes=[WA])
            cw = P.sb([64, 12, 4], F32, 'cwA')
            for j in range(4):
                for b_ in range(12):
                    P.dma('sp', cw[:, b_, j:j + 1], self.gdn_conv_d[l, j, 64 * b_:64 * b_ + 64].rearrange('(c o) -> c o', o=1), reads=[cw], writes=[cw])
            e8 = P.sb([32, S], F32, 'e8')
            self._A_rows(l, WA, e8, SC, be)
            self._A_rest(l, WA, e8, SC, be, ones, gn, cw)
        self._A_phase2()
        self._A_phase3(SC, ones, gn)

    def _A_rows(self, l, WA, e8, SC, be):
        P = self.P
        psum = self.psum
        ident = self.c('ident')
        with P.scope():
            self._A_rows_inner(l, WA, e8, SC, be)

    def _A_rows_inner(self, l, WA, e8, SC, be):
        P = self.P
        psum = self.psum
        ident = self.c('ident')
        rm8 = P.sb([32, S], F32, 'rm8')
        P.dma('sp', rm8[:], self.rmask_d[1:2, :].partition_broadcast(32), writes=[rm8])
        dtb = P.sb([32, 1], F32, 'dtb'); nA = P.sb([32, 1], F32, 'nA')
        P.op('pool', lambda e: e.memset(dtb[:], 0.0), writes=[dtb])
        P.op('pool', lambda e: e.memset(nA[:], 0.0), writes=[nA])
        P.dma('sp', dtb[0:4, :], self.gdn_dtb_d[l, :].rearrange('(p o) -> p o', o=1), reads=[dtb], writes=[dtb])
        P.dma('sp', nA[0:4, :], self.gdn_alog_d[l, :].rearrange('(p o) -> p o', o=1), reads=[nA], writes=[nA])
        P.op('act', lambda e: e.activation(out=nA[:], in_=nA[:], func=AF.Exp), reads=[nA], writes=[nA])
        P.op('dve', lambda e: e.tensor_scalar(out=nA[:], in0=nA[:], scalar1=-1.0, scalar2=None, op0=ALU.mult), reads=[nA], writes=[nA])
        ab = P.sb([32, S], F32, 'abA'); beta8 = P.sb([32, S], F32, 'beta8'); gc8 = P.sb([32, S], F32, 'gc8')
        self.proj_fm(WA, 768, 32, lambda ps, tb: P.op('act', lambda e: e.copy(out=ab[:, 512 * tb:512 * tb + 512], in_=ps[0:32, :]), reads=[ps], writes=[ab]))
        P.op('act', lambda e: e.activation(out=beta8[:], in_=ab[:], func=AF.Sigmoid), reads=[ab], writes=[beta8])
        P.op('act', lambda e: e.activation(out=ab[:], in_=ab[:], func=AF.Exp, bias=dtb[:, 0:1]), reads=[ab, dtb], writes=[ab])
        P.op('act', lambda e: e.activation(out=ab[:], in_=ab[:], func=AF.Ln, bias=1.0), reads=[ab], writes=[ab])
        P.op('dve', lambda e: e.tensor_scalar(out=ab[:], in0=ab[:], scalar1=nA[:, 0:1], scalar2=None, op0=ALU.mult), reads=[ab, nA], writes=[ab])
        P.op('dve', lambda e: e.tensor_tensor_scan(out=gc8[:], data0=rm8[:], data1=ab[:], initial=0.0, op0=ALU.mult, op1=ALU.add), reads=[rm8, ab], writes=[gc8])
        P.dma('sp', self.rows8d[0], gc8[0:8, :], reads=[gc8], writes=[self.rows8d])
        P.dma('sp', self.rows8d[1], beta8[0:8, :], reads=[beta8], writes=[self.rows8d])
        ed8 = P.sb([32, S], F32, 'ed8'); egl8 = P.sb([32, S], F32, 'egl8')
        gc3 = gc8[:].rearrange('p (n c) -> p n c', c=64)
        P.op('act', lambda e: e.activation(out=e8[:], in_=gc8[:], func=AF.Exp), reads=[gc8], writes=[e8])
        P.op('dve', lambda e: e.tensor_tensor(out=ed8[:].rearrange('p (n c) -> p n c', c=64), in0=gc3[:, :, 63:64].to_broadcast([32, 32, 64]), in1=gc3, op=ALU.subtract),
             reads=[gc8], writes=[ed8])
        P.op('act', lambda e: e.activation(out=ed8[:], in_=ed8[:], func=AF.Exp), reads=[ed8], writes=[ed8])
        P.op('dve', lambda e: e.tensor_copy(out=egl8[:].rearrange('p (n c) -> p n c', c=64), in_=gc3[:, :, 63:64].to_broadcast([32, 32, 64])), reads=[gc8], writes=[egl8])
        P.op('act', lambda e: e.activation(out=egl8[:], in_=egl8[:], func=AF.Exp), reads=[egl8], writes=[egl8])
        for qi_, (nm, src) in enumerate((('beta', beta8), ('e', e8), ('ed', ed8), ('egl', egl8))):
            t = SC[nm]
            for half in range(2):
                ps = psum[2 + half]
                for c in range(16):
                    n = 16 * half + c
                    P.op('pe', lambda e, ps=ps, c=c, n=n, src=src: e.transpose(ps[0:64, 32 * c:32 * c + 32], src[:, 64 * n:64 * n + 64], ident[0:32, 0:32]),
                         reads=[src, self.cst], writes=[ps])
                P.op('act', lambda e, ps=ps, t=t, half=half: e.copy(out=t[:, 16 * half:16 * half + 16, :], in_=ps[0:64, :].rearrange('p (n r) -> p n r', r=32)[:, :, 0:8]),
                     reads=[ps], writes=[t])
        P.op('dve', lambda e: e.tensor_tensor(out=be[:], in0=SC['beta'][:, :, 4:8], in1=SC['e'][:, :, 0:4], op=ALU.mult), reads=[SC['beta'], SC['e']], writes=[be])

    def _A_rest(self, l, WA, e8, SC, be, ones, gn, cw):
        P = self.P
        self.dump('e8', e8, e8[:], [32, S])
        for nm in SC:
            self.dump('sc_' + nm, SC[nm], SC[nm][:], [64, 32, 8])
        self.dump('be', be, be[:], [64, 32, 4])
        psum = self.psum
        xT = self.xT
        ident = self.c('ident')
        for h in range(4):
          with P.scope():
            xs = [P.sb([64, S], F32, 'xA') for _ in range(3)]
            ys = [P.sb([64, S], F32, 'yA') for _ in range(3)]
            gs = P.sb([64, S], F32, 'gsA')
            for i in range(3):
                self.proj_fm(WA, 256 * i + 64 * h, 64, lambda ps, tb, i=i: P.op('act', lambda e: e.copy(out=xs[i][:, 512 * tb:512 * tb + 512], in_=ps[0:64, :]),
                                                                               reads=[ps], writes=[xs[i]]))
            self.proj_fm(WA, 776 + 64 * h, 64, lambda ps, tb: P.op('act', lambda e: e.activation(out=gs[:, 512 * tb:512 * tb + 512], in_=ps[0:64, :], func=AF.Silu),
                                                                   reads=[ps], writes=[gs]))
            P.dma('sp', self.gsd[h], gs[:], reads=[gs], writes=[self.gsd])
            for i in range(3):
                x = xs[i]; y = ys[i]; blk = 4 * i + h
                P.op('dve', lambda e, x=x, y=y, blk=blk: e.tensor_scalar(out=y[:], in0=x[:], scalar1=cw[:, blk, 3:4], scalar2=None, op0=ALU.mult), reads=[x, cw], writes=[y])
                for sft in (1, 2, 3):
                    P.op('dve', lambda e, x=x, y=y, blk=blk, sft=sft: e.scalar_tensor_tensor(
                        out=y[:, sft:S], in0=x[:, 0:S - sft], scalar=cw[:, blk, 3 - sft:4 - sft], in1=y[:, sft:S], op0=ALU.mult, op1=ALU.add),
                        reads=[x, y, cw], writes=[y])
                P.op('act', lambda e, y=y: e.activation(out=y[:], in_=y[:], func=AF.Silu), reads=[y], writes=[y])
            sq = P.sb([64, 512], BF16, 'sqA'); rs = P.sb([64, 512], F32, 'rsA')
            for i in range(2):
                y = ys[i]
                for tb in range(4):
                    sl = slice(512 * tb, 512 * tb + 512)
                    P.op('act', lambda e, y=y, sl=sl: e.activation(out=sq[:], in_=y[:, sl], func=AF.Square), reads=[y], writes=[sq])
                    ps = psum[self._pn % 2]; self._pn += 1
                    P.op('pe', lambda e, ps=ps: e.matmul(ps[0:64, :], lhsT=ones[0:64, 0:64], rhs=sq[:], start=True, stop=True), reads=[ones, sq], writes=[ps])
                    P.op('act', lambda e, ps=ps: e.activation(out=rs[:], in_=ps[0:64, :], func=AF.Ln, bias=EPS), reads=[ps], writes=[rs])
                    P.op('act', lambda e: e.activation(out=rs[:], in_=rs[:], func=AF.Exp, scale=-0.5), reads=[rs], writes=[rs])
                    P.op('dve', lambda e, y=y, sl=sl, i=i: e.scalar_tensor_tensor(out=y[:, sl], in0=y[:, sl], scalar=(0.125 if i == 0 else 1.0), in1=rs[:],
                                                                                 op0=ALU.mult, op1=ALU.mult), reads=[y, rs], writes=[y])
            qn, kn, vv = ys
            stg = [P.sb([64, 32, 64], F32, 'stgA') for _ in range(2)]
            for gi, (lh, dst) in enumerate(((kn, self.kkd), (qn, self.qkd))):
                st = stg[gi]
                for g8 in range(4):
                    ps = psum[2 + (g8 % 2)]
                    for c in range(8):
                        n = 8 * g8 + c
                        csl = slice(64 * n, 64 * n + 64)
                        P.op('pe', lambda e, ps=ps, c=c, csl=csl, lh=lh: e.matmul(ps[0:64, 64 * c:64 * c + 64], lhsT=lh[:, csl], rhs=kn[:, csl], start=True, stop=True),
                             reads=[lh, kn], writes=[ps])
                    P.op('act', lambda e, ps=ps, st=st, g8=g8: e.copy(out=st[:, 8 * g8:8 * g8 + 8, :].rearrange('p n j -> p (n j)'), in_=ps[0:64, :]), reads=[ps], writes=[st])
                P.dma('sp', dst[32 * h:32 * h + 32].rearrange('n i j -> i n j'), st[:], reads=[st], writes=[dst])
            sel = P.sb([32, 64], F32, 'selA')
            P.op('pool', lambda e: e.memset(sel[:], 0.0), writes=[sel])
            P.op('pool', lambda e, h=h: e.affine_select(out=sel[:], in_=ones_f(P, self)[:], pattern=[[0, 64]], compare_op=ALU.is_equal, fill=0.0,
                                                        base=-h, channel_multiplier=1), reads=[sel], writes=[sel])
            qe = xs[0]
            for tb in range(4):
                sl = slice(512 * tb, 512 * tb + 512)
                ps = psum[self._pn % 2]; self._pn += 1
                P.op('pe', lambda e, ps=ps, sl=sl: e.matmul(ps[0:64, :], lhsT=sel[:], rhs=e8[:, sl], start=True, stop=True), reads=[sel, e8], writes=[ps])
                P.op('dve', lambda e, ps=ps, sl=sl: e.tensor_tensor(out=qe[:, sl], in0=qn[:, sl], in1=ps[0:64, :], op=ALU.mult), reads=[qn, ps], writes=[qe])
            P.dma('sp', self.qed[h], qe[:], reads=[qe], writes=[self.qed])
            tk = [P.sb([64, 32, 64], F32, 'tkA') for _ in range(3)]
            for g8 in range(4):
                for si, src in enumerate((kn, vv)):
                    ps = psum[4 + si]
                    for c in range(8):
                        n = 8 * g8 + c
                        P.op('pe', lambda e, ps=ps, c=c, n=n, src=src: e.transpose(ps[0:64, 64 * c:64 * c + 64], src[:, 64 * n:64 * n + 64], ident[0:64, 0:64]),
                             reads=[src, self.cst], writes=[ps])
                    p3 = ps[0:64, :].rearrange('p (n d) -> p n d', d=64)
                    nsl = slice(8 * g8, 8 * g8 + 8)
                    if si == 0:
                        P.op('dve', lambda e, p3=p3, nsl=nsl, h=h: e.tensor_tensor(out=tk[0][:, nsl, :], in0=p3, in1=be[:, nsl, h:h + 1].to_broadcast([64, 8, 64]), op=ALU.mult),
                             reads=[ps, be], writes=[tk[0]])
                        P.op('dve', lambda e, p3=p3, nsl=nsl, h=h: e.tensor_tensor(out=tk[1][:, nsl, :], in0=p3, in1=SC['ed'][:, nsl, h:h + 1].to_broadcast([64, 8, 64]), op=ALU.mult),
                             reads=[ps, SC['ed']], writes=[tk[1]])
                    else:
                        P.op('dve', lambda e, p3=p3, nsl=nsl, h=h: e.tensor_tensor(out=tk[2][:, nsl, :], in0=p3, in1=SC['beta'][:, nsl, 4 + h:5 + h].to_broadcast([64, 8, 64]), op=ALU.mult),
                             reads=[ps, SC['beta']], writes=[tk[2]])
            for i in range(3):
                P.dma('sp', self.tokd[h, i], tk[i][:], reads=[tk[i]], writes=[self.tokd])

    def _A_phase2(self):
        P = self.P
        with P.scope():
            KKs = P.sb([128, 64, 64], F32, 'KKs'); QKs = P.sb([128, 64, 64], F32, 'QKs')
            Dm = P.sb([128, 64, 64], F32, 'Dms'); X = P.sb([128, 64, 64], F32, 'Xs'); tmp = P.sb([128, 64, 64], F32, 'tmps')
            gcs = P.sb([128, 64], F32, 'gcs'); bts = P.sb([128, 64], F32, 'bts')
            P.dma('sp', KKs[:], self.kkd[:], reads=[self.kkd], writes=[KKs])
            P.dma('sp', QKs[:], self.qkd[:], reads=[self.qkd], writes=[QKs])
            P.dma('sp', gcs[:], self.rows8d[0, 0:4, :].rearrange('h (n c) -> (h n) c', c=64), reads=[self.rows8d], writes=[gcs])
            P.dma('sp', bts[:], self.rows8d[1, 4:8, :].rearrange('h (n c) -> (h n) c', c=64), reads=[self.rows8d], writes=[bts])
            P.op('dve', lambda e: e.tensor_tensor(out=Dm[:], in0=gcs[:].unsqueeze(2).to_broadcast([128, 64, 64]), in1=gcs[:].unsqueeze(1).to_broadcast([128, 64, 64]),
                                                  op=ALU.subtract), reads=[gcs], writes=[Dm])
            P.op('dve', lambda e: e.tensor_scalar(out=Dm[:], in0=Dm[:], scalar1=0.0, scalar2=None, op0=ALU.min), reads=[Dm], writes=[Dm])
            P.op('act', lambda e: e.activation(out=Dm[:], in_=Dm[:], func=AF.Exp), reads=[Dm], writes=[Dm])
            P.op('dve', lambda e: e.tensor_tensor(out=tmp[:].rearrange('p j i -> p i j'), in0=QKs[:], in1=Dm[:], op=ALU.mult), reads=[QKs, Dm], writes=[tmp])
            P.op('pool', lambda e: e.affine_select(out=tmp[:], in_=tmp[:], pattern=[[-1, 64], [1, 64]], compare_op=ALU.is_ge, fill=0.0, base=0, channel_multiplier=0),
                 reads=[tmp], writes=[tmp])
            P.dma('sp', self.qktd[:], tmp[:], reads=[tmp], writes=[self.qktd])
            P.op('dve', lambda e: e.tensor_tensor(out=KKs[:], in0=KKs[:], in1=Dm[:], op=ALU.mult), reads=[KKs, Dm], writes=[KKs])
            P.op('dve', lambda e: e.tensor_tensor(out=KKs[:], in0=KKs[:], in1=bts[:].unsqueeze(2).to_broadcast([128, 64, 64]), op=ALU.mult), reads=[KKs, bts], writes=[KKs])
            P.op('pool', lambda e: e.affine_select(out=KKs[:], in_=KKs[:], pattern=[[1, 64], [-1, 64]], compare_op=ALU.is_gt, fill=0.0, base=0, channel_multiplier=0),
                 reads=[KKs], writes=[KKs])
            P.op('pool', lambda e: e.memset(X[:], 1.0), writes=[X])
            P.op('pool', lambda e: e.affine_select(out=X[:], in_=X[:], pattern=[[1, 64], [-1, 64]], compare_op=ALU.is_equal, fill=0.0, base=0, channel_multiplier=0),
                 reads=[X], writes=[X])
            tmp2 = QKs
            for b in range(1, 64):
                P.op('dve', lambda e, b=b: e.tensor_tensor(out=tmp2[:, 0:b, 0:b], in0=X[:, 0:b, 0:b], in1=KKs[:, b:b + 1, 0:b].to_broadcast([128, b, b]), op=ALU.mult),
                     reads=[X, KKs, tmp], writes=[tmp2])
                P.op('dve', lambda e, b=b: e.tensor_reduce(out=X[:, 0:b, b:b + 1], in_=tmp2[:, 0:b, 0:b], axis=AX.X, op=ALU.add, negate=True), reads=[tmp2], writes=[X])
            P.dma('sp', self.ttd[:], X[:], reads=[X], writes=[self.ttd])

    def _A_phase3(self, SC, ones, gn):
        P = self.P
        psum = self.psum
        ident = self.c('ident')
        Sall = P.sb([64, 4, 33, 64], F32, 'SallA')
        for pair in range(2):
          with P.scope():
            ATa = P.sb([64, 2, 32, 64], F32, 'ATA'); Bna = P.sb([64, 2, 32, 64], F32, 'BnA')
            for hh in range(2):
                with P.scope():
                    self._A3a_head(2 * pair + hh, hh, SC, ATa, Bna)
            for hh in range(2):
                h = 2 * pair + hh
                P.op('pool', lambda e, h=h: e.memset(Sall[:, h, 0, :], 0.0), writes=[Sall.sub((h, 0))])
            for n in range(32):
                for hh in range(2):
                    h = 2 * pair + hh
                    ps = psum[4 * (n % 2) + hh]
                    P.op('pe', lambda e, ps=ps, n=n, h=h, hh=hh: e.matmul(ps[0:64, 0:64], lhsT=ATa[:, hh, n, :], rhs=Sall[:, h, n, :], start=True, stop=True),
                         reads=[ATa.sub(hh), Sall.sub((h, n))], writes=[ps])
                    P.op('dve', lambda e, ps=ps, n=n, h=h, hh=hh: e.tensor_tensor(out=Sall[:, h, n + 1, :], in0=Bna[:, hh, n, :], in1=ps[0:64, 0:64], op=ALU.add),
                         reads=[Bna.sub(hh), ps], writes=[Sall.sub((h, n + 1))])
        for h in range(4):
            with P.scope():
                self._A3c_head(h, Sall, ones, gn)

    def _A3a_head(self, h, hh, SC, ATa, Bna):
        P = self.P
        psum = self.psum
        ident = self.c('ident')
        TT = P.sb([64, 32, 64], F32, 'TTA'); QKT = P.sb([64, 32, 64], F32, 'QKTA')
        tk = [P.sb([64, 32, 64], F32, 'tk3A') for _ in range(3)]
        qe = P.sb([64, S], F32, 'qe3A')
        P.dma('sp', TT[:], self.ttd[32 * h:32 * h + 32].rearrange('n a b -> a n b'), reads=[self.ttd], writes=[TT])
        P.dma('sp', QKT[:], self.qktd[32 * h:32 * h + 32].rearrange('n a b -> a n b'), reads=[self.qktd], writes=[QKT])
        for i in range(3):
            P.dma('sp', tk[i][:], self.tokd[h, i], reads=[self.tokd], writes=[tk[i]])
        P.dma('sp', qe[:], self.qed[h], reads=[self.qed], writes=[qe])
        kbe, kd, vb = tk
        UW = P.sb([64, 32, 128], F32, 'UWA')
        for g4 in range(8):
            ps = psum[g4 % 2]
            for c in range(4):
                n = 4 * g4 + c
                P.op('pe', lambda e, ps=ps, c=c, n=n: e.matmul(ps[0:64, 128 * c:128 * c + 64], lhsT=TT[:, n, :], rhs=vb[:, n, :], start=True, stop=True),
                     reads=[TT, vb], writes=[ps])
                P.op('pe', lambda e, ps=ps, c=c, n=n: e.matmul(ps[0:64, 128 * c + 64:128 * c + 128], lhsT=TT[:, n, :], rhs=kbe[:, n, :], start=True, stop=True),
                     reads=[TT, kbe], writes=[ps])
            P.op('act', lambda e, ps=ps, g4=g4: e.copy(out=UW[:, 4 * g4:4 * g4 + 4, :].rearrange('p n d -> p (n d)'), in_=ps[0:64, :]), reads=[ps], writes=[UW])
        QhT = P.sb([64, S], F32, 'QhTA'); OhT = P.sb([64, S], F32, 'OhTA')
        id64 = P.sb([64, 64], F32, 'id64A')
        P.op('pool', lambda e: e.tensor_copy(out=id64[:], in_=ident[0:64, 0:64]), reads=[self.cst], writes=[id64])
        for g8 in range(4):
            sl = slice(512 * g8, 512 * g8 + 512)
            ps = psum[2]
            for c in range(8):
                n = 8 * g8 + c
                P.op('pe', lambda e, ps=ps, c=c, n=n: e.matmul(ps[0:64, 64 * c:64 * c + 64], lhsT=UW[:, n, 64:128], rhs=QKT[:, n, :], start=True, stop=True),
                     reads=[UW, QKT], writes=[ps])
            P.op('dve', lambda e, ps=ps, sl=sl: e.tensor_tensor(out=QhT[:, sl], in0=qe[:, sl], in1=ps[0:64, :], op=ALU.subtract), reads=[qe, ps], writes=[QhT])
            ps = psum[3]
            for c in range(8):
                n = 8 * g8 + c
                P.op('pe', lambda e, ps=ps, c=c, n=n: e.matmul(ps[0:64, 64 * c:64 * c + 64], lhsT=UW[:, n, 0:64], rhs=QKT[:, n, :], start=True, stop=True),
                     reads=[UW, QKT], writes=[ps])
            P.op('act', lambda e, ps=ps, sl=sl: e.copy(out=OhT[:, sl], in_=ps[0:64, :]), reads=[ps], writes=[OhT])
            ps = psum[4 + g8 % 2]
            for c in range(8):
                n = 8 * g8 + c
                P.op('pe', lambda e, ps=ps, c=c, n=n: e.matmul(ps[0:64, 64 * c:64 * c + 64], lhsT=UW[:, n, 64:128], rhs=kd[:, n, :], start=True, stop=True),
                     reads=[UW, kd], writes=[ps])
            for c in range(8):
                n = 8 * g8 + c
                P.op('dve', lambda e, ps=ps, c=c, n=n: e.scalar_tensor_tensor(out=ATa[:, hh, n, :], in0=id64[:], scalar=SC['egl'][:, n, h:h + 1], in1=ps[0:64, 64 * c:64 * c + 64],
                                                                         op0=ALU.mult, op1=ALU.subtract), reads=[id64, SC['egl'], ps], writes=[ATa.sub(hh)])
            ps = psum[6 + g8 % 2]
            for c in range(8):
                n = 8 * g8 + c
                P.op('pe', lambda e, ps=ps, c=c, n=n: e.matmul(ps[0:64, 64 * c:64 * c + 64], lhsT=kd[:, n, :], rhs=UW[:, n, 0:64], start=True, stop=True),
                     reads=[UW, kd], writes=[ps])
            P.op('act', lambda e, ps=ps, g8=g8: e.copy(out=Bna[:, hh, 8 * g8:8 * g8 + 8, :].rearrange('p n d -> p (n d)'), in_=ps[0:64, :]), reads=[ps], writes=[Bna.sub(hh)])
        P.dma('sp', self.qhd[h], QhT[:], reads=[QhT], writes=[self.qhd])
        P.dma('sp', self.ohd[h], OhT[:], reads=[OhT], writes=[self.ohd])

    def _A3c_head(self, h, Sall, ones, gn):
        P = self.P
        psum = self.psum
        QhT = P.sb([64, S], F32, 'QhTc'); OhT = P.sb([64, S], F32, 'OhTc'); gs = P.sb([64, S], F32, 'gs3A')
        P.dma('sp', QhT[:], self.qhd[h], reads=[self.qhd], writes=[QhT])
        P.dma('sp', OhT[:], self.ohd[h], reads=[self.ohd], writes=[OhT])
        P.dma('sp', gs[:], self.gsd[h], reads=[self.gsd], writes=[gs])
        oi = [P.sb([64, 512], F32, 'oiA') for _ in range(2)]
        sqs = [P.sb([64, 512], BF16, 'sq3A') for _ in range(2)]; rss = [P.sb([64, 512], F32, 'rs3A') for _ in range(2)]
        ob = [P.sb([64, 512], BF16, 'obA') for _ in range(2)]
        def out_gen(tb):
            sl = slice(512 * tb, 512 * tb + 512)
            ps = psum[4 + tb % 2]
            for c in range(8):
                n = 8 * tb + c
                P.op('pe', lambda e, c=c, n=n: e.matmul(ps[0:64, 64 * c:64 * c + 64], lhsT=Sall[:, h, n, :], rhs=QhT[:, 64 * n:64 * n + 64], start=True, stop=True),
                     reads=[Sall.sub((h, n)), QhT], writes=[ps])
            yield
            o = oi[tb % 2]
            P.op('dve', lambda e: e.tensor_tensor(out=o[:], in0=OhT[:, sl], in1=ps[0:64, :], op=ALU.add), reads=[ps, OhT], writes=[o])
            yield
            yield from self.rms_gate_out_gen(o, gs, gn[:, 0:1], ones, h, tb, sqs[tb % 2], rss[tb % 2], ob[tb % 2])

        pipeline(out_gen, range(4), width=2, skew=2)

    def mixer_C(self, l):
        P = self.P
        psum = self.psum
        xT = self.xT
        self._pn = 0
        WC = P.sb([128, 8, 1024], BF16, 'wC')
        P.dma('pool', WC[:], self.w_in_d[l, :, 1384:2408].rearrange('(k p) n -> p k n', p=128), writes=[WC])
        ones = P.sb([128, 128], BF16, 'onesC')
        P.op('pool', lambda e: e.memset(ones[:], 1.0), writes=[ones])
        zer = P.sb([64, 16], BF16, 'zerC')
        P.op('pool', lambda e: e.memset(zer[:], 0.0), writes=[zer])
        rmask = P.sb([64, S], F32, 'rmaskC')
        P.dma('sp', rmask[:], self.rmask_d[0:1, :].partition_broadcast(64), writes=[rmask])
        gn = P.sb([64, 1], F32, 'gnC')
        P.dma('sp', gn[:], self.hgrn_g_d[l, :].rearrange('(p o) -> p o', o=1), writes=[gn])
        lb = P.sb([64, 4], F32, 'lbC'); oml = P.sb([64, 4], F32, 'omlC'); noml = P.sb([64, 4], F32, 'nomlC')
        if l == 0:
            P.op('pool', lambda e: e.memset(lb[:], 0.0), writes=[lb])
        else:
            z = P.sb([64, 2, 4], F32, 'zC')
            for li in range(2):
                for hh in range(4):
                    P.dma('sp', z[:, li, hh:hh + 1], self.hgrn_lb_d[li, 64 * hh:64 * hh + 64].rearrange('(c o) -> c o', o=1), reads=[z], writes=[z])
            P.op('dve', lambda e: e.tensor_tensor(out=lb[:], in0=z[:, 1, :], in1=z[:, 0, :], op=ALU.subtract), reads=[z], writes=[lb])
            P.op('act', lambda e: e.activation(out=lb[:], in_=lb[:], func=AF.Sigmoid), reads=[lb], writes=[lb])
        P.op('dve', lambda e: e.tensor_scalar(out=oml[:], in0=lb[:], scalar1=-1.0, scalar2=1.0, op0=ALU.mult, op1=ALU.add), reads=[lb], writes=[oml])
        P.op('dve', lambda e: e.tensor_scalar(out=noml[:], in0=oml[:], scalar1=-1.0, scalar2=None, op0=ALU.mult), reads=[oml], writes=[noml])
        m16 = P.sb([128, 128], F32, 'm16')
        P.op('pool', lambda e: e.tensor_copy(out=m16[:], in_=self.c('mask16')), reads=[self.cst], writes=[m16])
        cm8 = P.sb([128, 8], BF16, 'cm8')
        P.op('pool', lambda e: e.tensor_copy(out=cm8[:], in_=self.c('cm8')), reads=[self.cst], writes=[cm8])
        Vtok = P.sb([128, 16, 256], BF16, 'VtokC')
        for tt in range(16):
            ps = psum[self._pn % 2]; self._pn += 1
            for kc in range(8):
                P.op('pe', lambda e, ps=ps, kc=kc, tt=tt: e.matmul(ps[:, 0:256], lhsT=xT[:, kc, 128 * tt:128 * tt + 128], rhs=WC[:, kc, 512:768],
                                                                   start=(kc == 0), stop=(kc == 7)), reads=[WC, xT.sub(tt // 4)], writes=[ps])
            P.op('act', lambda e, ps=ps, tt=tt: e.copy(out=Vtok[:, tt, :], in_=ps[:, 0:256]), reads=[ps], writes=[Vtok])
        gs = P.sb([64, S], F32, 'gsC')
        qt = P.sb([64, S], BF16, 'qtC'); kt = P.sb([64, S], BF16, 'ktC')
        adec = P.sb([64, 128], F32, 'adecC')
        Bst = P.sb([64, 64, 128], F32, 'BstC')
        kdt = [P.sb([128, 64], BF16, 'kdtC') for _ in range(2)]
        vex = [P.sb([128, 8, 64], BF16, 'vexC') for _ in range(2)]
        sm = [P.sb([128, 4, 128], BF16, 'smC') for _ in range(2)]
        oi = [P.sb([64, 512], F32, 'oiC') for _ in range(2)]
        sqs = [P.sb([64, 512], BF16, 'sqC') for _ in range(2)]; rss = [P.sb([64, 512], F32, 'rsC') for _ in range(2)]
        ob = [P.sb([64, 512], BF16, 'obC') for _ in range(2)]
        ident = self.c('ident')
        for h in range(4):
          with P.scope():
            qT = P.sb([64, S], F32, 'qTC'); sg = P.sb([64, S], F32, 'sgC')
            kk = P.sb([64, S], F32, 'kkC'); cum = P.sb([64, S], F32, 'cumC'); ex = P.sb([64, S], F32, 'exC')
            kd = P.sb([64, S], F32, 'kdC')
            self.proj_fm(WC, 64 * h, 64, lambda ps, tb: P.op('act', lambda e: e.copy(out=qT[:, 512 * tb:512 * tb + 512], in_=ps[0:64, :]),
                                                             reads=[ps], writes=[qT.sub(tb)]))
            self.proj_fm(WC, 256 + 64 * h, 64, lambda ps, tb: P.op('act', lambda e: e.activation(out=sg[:, 512 * tb:512 * tb + 512], in_=ps[0:64, :], func=AF.Sigmoid),
                                                                   reads=[ps], writes=[sg]))
            self.proj_fm(WC, 768 + 64 * h, 64, lambda ps, tb: P.op('act', lambda e: e.activation(out=gs[:, 512 * tb:512 * tb + 512], in_=ps[0:64, :], func=AF.Silu),
                                                                   reads=[ps], writes=[gs.sub(tb)]))
            P.op('dve', lambda e, h=h: e.tensor_scalar(out=kk[:], in0=sg[:], scalar1=noml[:, h:h + 1], scalar2=oml[:, h:h + 1], op0=ALU.mult, op1=ALU.add),
                 reads=[sg, noml, oml], writes=[kk])
            P.op('dve', lambda e, h=h: e.tensor_scalar(out=sg[:], in0=sg[:], scalar1=oml[:, h:h + 1], scalar2=lb[:, h:h + 1], op0=ALU.mult, op1=ALU.add),
                 reads=[sg, oml, lb], writes=[sg])
            P.op('act', lambda e: e.activation(out=sg[:], in_=sg[:], func=AF.Ln), reads=[sg], writes=[sg])
            P.op('dve', lambda e: e.tensor_tensor_scan(out=cum[:], data0=rmask[:], data1=sg[:], initial=0.0, op0=ALU.mult, op1=ALU.add),
                 reads=[rmask, sg], writes=[cum])
            P.op('act', lambda e: e.activation(out=ex[:], in_=cum[:], func=AF.Exp), reads=[cum], writes=[ex])
            P.op('dve', lambda e: e.tensor_tensor(out=qt[:], in0=qT[:], in1=ex[:], op=ALU.mult), reads=[qT.sub(0), qT.sub(1), qT.sub(2), qT.sub(3), ex], writes=[qt])
            P.op('act', lambda e: e.activation(out=ex[:], in_=cum[:], func=AF.Exp, scale=-1.0), reads=[cum], writes=[ex])
            P.op('dve', lambda e: e.tensor_tensor(out=kt[:], in0=kk[:], in1=ex[:], op=ALU.mult), reads=[kk, ex], writes=[kt])
            cum3 = cum[:].rearrange('p (c j) -> p c j', j=16)
            cl = cum3[:, :, 15:16]
            P.op('dve', lambda e, cl=cl, cum3=cum3: e.tensor_tensor(out=ex[:].rearrange('p (c j) -> p c j', j=16), in0=cl.to_broadcast([64, 128, 16]), in1=cum3, op=ALU.subtract),
                 reads=[cum], writes=[ex])
            P.op('act', lambda e: e.activation(out=ex[:], in_=ex[:], func=AF.Exp), reads=[ex], writes=[ex])
            P.op('dve', lambda e: e.tensor_tensor(out=kd[:], in0=kk[:], in1=ex[:], op=ALU.mult), reads=[kk, ex], writes=[kd])
            P.op('act', lambda e, cl=cl: e.activation(out=adec[:].rearrange('p (c o) -> p c o', o=1), in_=cl, func=AF.Exp), reads=[cum], writes=[adec])
            P.op('pool', lambda e: e.memset(adec[:, 0:1], 0.0), reads=[adec], writes=[adec])
            def tile_gen(tt, h=h, kd=kd):
                tsl = slice(128 * tt, 128 * tt + 128)
                pt = psum[2 + (tt % 2)]
                kdtt = kdt[tt % 2]; vx = vex[tt % 2]
                P.op('pe', lambda e: e.transpose(pt[:, 0:64], kd[:, tsl], ident[0:64, 0:64]), reads=[kd, self.cst], writes=[pt])
                P.op('dve', lambda e: e.tensor_tensor(
                    out=vx[:], in0=Vtok[:, tt, 64 * h:64 * h + 64].unsqueeze(1).to_broadcast([128, 8, 64]),
                    in1=cm8[:].unsqueeze(2).to_broadcast([128, 8, 64]), op=ALU.mult), reads=[Vtok, cm8], writes=[vx])
                yield
                P.op('act', lambda e: e.copy(out=kdtt[:], in_=pt[:, 0:64]), reads=[pt], writes=[kdtt])
                yield
                pb = psum[4 + (tt % 2)]
                P.op('pe', lambda e: e.matmul(pb[0:64, :], lhsT=kdtt[:], rhs=vx[:].rearrange('p n v -> p (n v)'), start=True, stop=True),
                     reads=[kdtt, vx], writes=[pb])
                yield
                P.op('act', lambda e: e.copy(out=Bst[:, :, 8 * tt:8 * tt + 8], in_=pb[0:64, :].rearrange('p (n v) -> p v n', v=64)),
                     reads=[pb], writes=[Bst])
                yield

            pipeline(tile_gen, range(16), width=2, skew=2)
          with P.scope():
            Sst = P.sb([64, 64, 128], F32, 'SstC'); Sb = P.sb([64, 64, 128], BF16, 'SbC')
            for v in range(64):
                P.op('dve', lambda e, v=v: e.tensor_tensor_scan(out=Sst[:, v, :], data0=adec[:], data1=Bst[:, v, :], initial=0.0,
                                                                                         op0=ALU.mult, op1=ALU.add), reads=[adec, Bst], writes=[Sst.sub(v)])
            P.op('act', lambda e: e.copy(out=Sb[:], in_=Sst[:]), reads=[Sst.sub(v) for v in range(64)], writes=[Sb])
            def out_gen(tb, h=h, Sb=Sb):
                pst = psum[2 + (tb % 2)]
                smt = sm[tb % 2]
                for t4 in range(4):
                    tsl = slice(512 * tb + 128 * t4, 512 * tb + 128 * t4 + 128)
                    P.op('pe', lambda e, t4=t4, tsl=tsl: e.matmul(pst[:, 128 * t4:128 * t4 + 128], lhsT=kt[:, tsl], rhs=qt[:, tsl], start=True, stop=True),
                         reads=[kt, qt], writes=[pst])
                po = psum[4 + (tb % 2)]; po2 = psum[6 + (tb % 2)]
                for c in range(32):
                    n = 32 * tb + c
                    if n == 0:
                        P.op('pe', lambda e, c=c: e.matmul(po2[0:64, 16 * c:16 * c + 16], lhsT=Sb[:, :, 0], rhs=zer[:], start=True, stop=True),
                             reads=[Sb, zer], writes=[po2])
                    else:
                        P.op('pe', lambda e, c=c, n=n: e.matmul(po2[0:64, 16 * c:16 * c + 16], lhsT=Sb[:, :, n - 1], rhs=qt[:, 16 * n:16 * n + 16], start=True, stop=True),
                             reads=[Sb, qt], writes=[po2])
                yield
                P.op('dve', lambda e: e.tensor_tensor(out=smt[:], in0=pst[:].rearrange('p (a t) -> p a t', a=4),
                                                      in1=m16[:].unsqueeze(1).to_broadcast([128, 4, 128]), op=ALU.mult),
                     reads=[pst, m16], writes=[smt])
                yield
                for t4 in range(4):
                    tt = 4 * tb + t4
                    P.op('pe', lambda e, t4=t4, tt=tt: e.matmul(po[0:64, 128 * t4:128 * t4 + 128], lhsT=Vtok[:, tt, 64 * h:64 * h + 64], rhs=smt[:, t4, :],
                                                                start=True, stop=True), reads=[Vtok, smt], writes=[po])
                yield
                o = oi[tb % 2]
                P.op('act', lambda e: e.copy(out=o[:], in_=po[0:64, :]), reads=[po], writes=[o])
                yield
                P.op('dve', lambda e: e.tensor_tensor(out=o[:], in0=o[:], in1=po2[0:64, :], op=ALU.add), reads=[o, po2], writes=[o])
                yield
                yield from self.rms_gate_out_gen(o, gs, gn[:, 0:1], ones, 8 + h, tb, sqs[tb % 2], rss[tb % 2], ob[tb % 2])

            pipeline(out_gen, range(4), width=2, skew=3)

    def proj_fm(self, W, c0, M, evac, xsrc=None, nk=8):
        P = self.P
        for tb in range(4):
            ps = self.psum[self._pn % 2]; self._pn += 1
            for kc in range(nk):
                P.op('pe', lambda e, ps=ps, kc=kc, tb=tb: e.matmul(
                    ps[0:M, :], lhsT=W[:, kc, c0:c0 + M], rhs=self.xT[:, kc, 512 * tb:512 * tb + 512],
                    start=(kc == 0), stop=(kc == nk - 1)), reads=[W, self.xT.sub(tb)], writes=[ps])
            evac(ps, tb)

    def mixer_B(self, l):
        P = self.P
        psum = self.psum
        self._pn = 0
        SC = 96 ** -0.5
        WB = P.sb([128, 8, 352], BF16, 'wB')
        P.dma('pool', WB[:], self.w_in_d[l, :, 1032:1384].rearrange('(k p) n -> p k n', p=128), writes=[WB])
        WBr = P.sb([128, 8, 32], BF16, 'wBr')
        P.op('act', lambda e: e.mul(out=WBr[:, :, 0:16], in_=WB[:, :, 336:352], mul=-1.0), reads=[WB], writes=[WBr])
        P.op('act', lambda e: e.copy(out=WBr[:, :, 16:32], in_=WB[:, :, 320:336]), reads=[WB, WBr], writes=[WBr])
        wuqa = P.sb([128, 384], BF16, 'wuqa'); wuqb = P.sb([64, 384], BF16, 'wuqb')
        P.dma('pool', wuqa[:], self.mla_wuq_d[l, 0:128, :], writes=[wuqa])
        P.dma('pool', wuqb[:], self.mla_wuq_d[l, 128:192, :], writes=[wuqb])
        wra = P.sb([128, 4, 32], BF16, 'wra'); wrb = P.sb([64, 4, 32], BF16, 'wrb')
        for (src, dst) in ((wuqa, wra), (wuqb, wrb)):
            v = src[:].rearrange('p (h c) -> p h c', c=96)
            P.op('act', lambda e, v=v, dst=dst: e.mul(out=dst[:, :, 0:16], in_=v[:, :, 80:96], mul=-1.0), reads=[src], writes=[dst])
            P.op('act', lambda e, v=v, dst=dst: e.copy(out=dst[:, :, 16:32], in_=v[:, :, 64:80]), reads=[src, dst], writes=[dst])
        wukv = P.sb([128, 512], BF16, 'wukv')
        P.dma('pool', wukv[:], self.mla_wukv_d[l], writes=[wukv])
        wv = P.sb([128, 4, 64], BF16, 'wv')
        P.op('act', lambda e: e.copy(out=wv[:], in_=wukv[:].rearrange('p (h c) -> p h c', c=128)[:, :, 64:128]), reads=[wukv], writes=[wv])
        gqa = P.sb([128, 1], F32, 'gqa'); gqb = P.sb([64, 1], F32, 'gqb'); gkv = P.sb([128, 1], F32, 'gkv')
        P.dma('sp', gqa[:], self.mla_qg_d[l, 0:128].rearrange('(p o) -> p o', o=1), writes=[gqa])
        P.dma('sp', gqb[:], self.mla_qg_d[l, 128:192].rearrange('(p o) -> p o', o=1), writes=[gqb])
        P.dma('sp', gkv[:], self.mla_kvg_d[l, :].rearrange('(p o) -> p o', o=1), writes=[gkv])
        ones = P.sb([128, 128], BF16, 'onesB')
        P.op('pool', lambda e: e.memset(ones[:], 1.0), writes=[ones])
        cosT = P.sb([32, S], F32, 'cosT'); sinT = P.sb([32, S], F32, 'sinT')
        P.dma('sp', cosT[:], self.rope_d[0], writes=[cosT])
        P.dma('sp', sinT[:], self.rope_d[1], writes=[sinT])
        cqna = P.sb([128, S], BF16, 'cqna'); cqnb = P.sb([64, S], BF16, 'cqnb'); ckvn = P.sb([128, S], BF16, 'ckvn')
        KrT = P.sb([32, S], BF16, 'KrT')
        with P.scope():
            cqa = P.sb([128, S], F32, 'cqa'); cqb = P.sb([64, S], F32, 'cqb'); ckv = P.sb([128, S], F32, 'ckv')
            sqa = P.sb([128, S], BF16, 'sqa'); sqb = P.sb([64, S], BF16, 'sqb'); sqk = P.sb([128, S], BF16, 'sqk')
            for (dst, sq, c0, M) in ((cqa, sqa, 0, 128), (cqb, sqb, 128, 64), (ckv, sqk, 192, 128)):
                def evac(ps, tb, dst=dst, sq=sq, M=M):
                    P.op('act', lambda e: e.copy(out=dst[:, 512 * tb:512 * tb + 512], in_=ps[0:M, :]), reads=[ps], writes=[dst.sub(tb)])
                    P.op('act', lambda e: e.activation(out=sq[:, 512 * tb:512 * tb + 512], in_=ps[0:M, :], func=AF.Square), reads=[ps], writes=[sq.sub(tb)])
                self.proj_fm(WB, c0, M, evac)
            kx = P.sb([32, S], F32, 'kx')
            self.proj_fm(WB, 320, 32, lambda ps, tb: P.op('dve', lambda e: e.tensor_tensor(
                out=kx[:, 512 * tb:512 * tb + 512], in0=ps[0:32, :], in1=cosT[:, 512 * tb:512 * tb + 512], op=ALU.mult),
                reads=[ps, cosT], writes=[kx.sub(tb)]))
            kx2 = P.sb([32, S], F32, 'kx2')
            self.proj_fm(WBr, 0, 32, lambda ps, tb: P.op('dve', lambda e: e.tensor_tensor(
                out=kx2[:, 512 * tb:512 * tb + 512], in0=ps[0:32, :], in1=sinT[:, 512 * tb:512 * tb + 512], op=ALU.mult),
                reads=[ps, sinT], writes=[kx2.sub(tb)]))
            for tb in range(4):
                P.op('dve', lambda e, tb=tb: e.tensor_tensor(out=KrT[:, 512 * tb:512 * tb + 512], in0=kx[:, 512 * tb:512 * tb + 512],
                                                              in1=kx2[:, 512 * tb:512 * tb + 512], op=ALU.add),
                     reads=[kx.sub(tb), kx2.sub(tb)], writes=[KrT.sub(tb)])
            rq = [P.sb([128, 512], F32, 'rq') for _ in range(2)]
            for tb in range(4):
                sl = slice(512 * tb, 512 * tb + 512)
                ps = psum[self._pn % 2]; self._pn += 1
                P.op('pe', lambda e, ps=ps, sl=sl: e.matmul(ps[:], lhsT=ones[:], rhs=sqa[:, sl], start=True, stop=False), reads=[ones, sqa.sub(tb)], writes=[ps])
                P.op('pe', lambda e, ps=ps, sl=sl: e.matmul(ps[:], lhsT=ones[0:64, :], rhs=sqb[:, sl], start=False, stop=True), reads=[ones, sqb.sub(tb)], writes=[ps])
                r = rq[0]
                P.op('act', lambda e, ps=ps, r=r: e.activation(out=r[:], in_=ps[:], func=AF.Ln, scale=1.0 / 192, bias=EPS), reads=[ps], writes=[r])
                P.op('act', lambda e, r=r: e.activation(out=r[:], in_=r[:], func=AF.Exp, scale=-0.5), reads=[r], writes=[r])
                P.op('dve', lambda e, r=r, sl=sl: e.scalar_tensor_tensor(out=cqna[:, sl], in0=cqa[:, sl], scalar=gqa[:, 0:1], in1=r[:], op0=ALU.mult, op1=ALU.mult),
                     reads=[cqa.sub(tb), gqa, r], writes=[cqna.sub(tb)])
                P.op('dve', lambda e, r=r, sl=sl: e.scalar_tensor_tensor(out=cqnb[:, sl], in0=cqb[:, sl], scalar=gqb[:, 0:1], in1=r[0:64, :], op0=ALU.mult, op1=ALU.mult),
                     reads=[cqb.sub(tb), gqb, r], writes=[cqnb.sub(tb)])
                ps = psum[self._pn % 2]; self._pn += 1
                P.op('pe', lambda e, ps=ps, sl=sl: e.matmul(ps[:], lhsT=ones[:], rhs=sqk[:, sl], start=True, stop=True), reads=[ones, sqk.sub(tb)], writes=[ps])
                r = rq[1]
                P.op('act', lambda e, ps=ps, r=r: e.activation(out=r[:], in_=ps[:], func=AF.Ln, scale=1.0 / 128, bias=EPS), reads=[ps], writes=[r])
                P.op('act', lambda e, r=r: e.activation(out=r[:], in_=r[:], func=AF.Exp, scale=-0.5), reads=[r], writes=[r])
                P.op('dve', lambda e, r=r, sl=sl: e.scalar_tensor_tensor(out=ckvn[:, sl], in0=ckv[:, sl], scalar=gkv[:, 0:1], in1=r[:], op0=ALU.mult, op1=ALU.mult),
                     reads=[ckv.sub(tb), gkv, r], writes=[ckvn.sub(tb)])
        QnT = [P.sb([64, S], BF16, 'QnT') for _ in range(4)]
        QrT = [P.sb([32, S], BF16, 'QrT') for _ in range(4)]
        KnT = [P.sb([64, S], BF16, 'KnT') for _ in range(4)]
        Vtok = P.sb([128, 16, 256], BF16, 'VtokB')
        qx = P.sb([32, 512], F32, 'qx'); qx2 = P.sb([32, 512], F32, 'qx2')
        for h in range(4):
            for tb in range(4):
                sl = slice(512 * tb, 512 * tb + 512)
                ps = psum[self._pn % 2]; self._pn += 1
                P.op('pe', lambda e, ps=ps, sl=sl, h=h: e.matmul(ps[0:64, :], lhsT=wuqa[:, 96 * h:96 * h + 64], rhs=cqna[:, sl], start=True, stop=False),
                     reads=[wuqa, cqna.sub(tb)], writes=[ps])
                P.op('pe', lambda e, ps=ps, sl=sl, h=h: e.matmul(ps[0:64, :], lhsT=wuqb[:, 96 * h:96 * h + 64], rhs=cqnb[:, sl], start=False, stop=True),
                     reads=[wuqb, cqnb.sub(tb)], writes=[ps])
                P.op('act', lambda e, ps=ps, sl=sl, h=h: e.copy(out=QnT[h][:, sl], in_=ps[0:64, :]), reads=[ps], writes=[QnT[h].sub(tb)])
                ps = psum[self._pn % 2]; self._pn += 1
                P.op('pe', lambda e, ps=ps, sl=sl, h=h: e.matmul(ps[0:64, :], lhsT=wukv[:, 128 * h:128 * h + 64], rhs=ckvn[:, sl], start=True, stop=True),
                     reads=[wukv, ckvn.sub(tb)], writes=[ps])
                P.op('act', lambda e, ps=ps, sl=sl, h=h: e.copy(out=KnT[h][:, sl], in_=ps[0:64, :]), reads=[ps], writes=[KnT[h].sub(tb)])
                ps = psum[self._pn % 2]; self._pn += 1
                P.op('pe', lambda e, ps=ps, sl=sl, h=h: e.matmul(ps[0:32, :], lhsT=wuqa[:, 96 * h + 64:96 * h + 96], rhs=cqna[:, sl], start=True, stop=False),
                     reads=[wuqa, cqna.sub(tb)], writes=[ps])
                P.op('pe', lambda e, ps=ps, sl=sl, h=h: e.matmul(ps[0:32, :], lhsT=wuqb[:, 96 * h + 64:96 * h + 96], rhs=cqnb[:, sl], start=False, stop=True),
                     reads=[wuqb, cqnb.sub(tb)], writes=[ps])
                P.op('dve', lambda e, ps=ps, sl=sl: e.tensor_tensor(out=qx[:], in0=ps[0:32, :], in1=cosT[:, sl], op=ALU.mult), reads=[ps, cosT], writes=[qx])
                ps = psum[self._pn % 2]; self._pn += 1
                P.op('pe', lambda e, ps=ps, sl=sl, h=h: e.matmul(ps[0:32, :], lhsT=wra[:, h, :], rhs=cqna[:, sl], start=True, stop=False),
                     reads=[wra, cqna.sub(tb)], writes=[ps])
                P.op('pe', lambda e, ps=ps, sl=sl, h=h: e.matmul(ps[0:32, :], lhsT=wrb[:, h, :], rhs=cqnb[:, sl], start=False, stop=True),
                     reads=[wrb, cqnb.sub(tb)], writes=[ps])
                P.op('dve', lambda e, ps=ps, sl=sl: e.tensor_tensor(out=qx2[:], in0=ps[0:32, :], in1=sinT[:, sl], op=ALU.mult), reads=[ps, sinT], writes=[qx2])
                P.op('dve', lambda e, sl=sl, h=h: e.tensor_tensor(out=QrT[h][:, sl], in0=qx[:], in1=qx2[:], op=ALU.add), reads=[qx, qx2], writes=[QrT[h].sub(tb)])
        for tt in range(16):
            ps = psum[self._pn % 2]; self._pn += 1
            P.op('pe', lambda e, ps=ps, tt=tt: e.matmul(ps[:, 0:256], lhsT=ckvn[:, 128 * tt:128 * tt + 128], rhs=wv[:].rearrange('p h c -> p (h c)'),
                                                        start=True, stop=True), reads=[ckvn.sub(tt // 4), wv], writes=[ps])
            P.op('act', lambda e, ps=ps, tt=tt: e.copy(out=Vtok[:, tt, :], in_=ps[:, 0:256]), reads=[ps], writes=[Vtok])
        cm = P.sb([128, 4, 512], BF16, 'cmB')
        P.op('dve', lambda e: e.tensor_copy(out=cm[:], in_=self.c('cmask').rearrange('p (r f) -> p r f', r=4)), reads=[self.cst], writes=[cm])
        pts = [P.sb([128, 512], BF16, 'ptB') for _ in range(4)]
        obs = [P.sb([64, 512], BF16, 'oB') for _ in range(2)]
        rec = P.sb([64, 512], F32, 'recB')
        blocks = []
        nq = 0
        for h in range(4):
            for qb in range(4):
                nkb = 4 * qb + 4
                for kb in range(nkb):
                    blocks.append((h, qb, kb, nkb, nq))
                nq += 1

        def stage1(i):
            (h, qb, kb, nkb, q_) = blocks[i]
            qsl = slice(512 * qb, 512 * qb + 512)
            ksl = slice(128 * kb, 128 * kb + 128)
            r = kb - 4 * qb
            st = psum[i % 4]; pt = pts[i % 4]
            P.op('pe', lambda e: e.matmul(st[:], lhsT=KnT[h][:, ksl], rhs=QnT[h][:, qsl], start=True, stop=False),
                 reads=[KnT[h].sub(kb // 4), QnT[h].sub(qb)], writes=[st])
            P.op('pe', lambda e: e.matmul(st[:], lhsT=KrT[:, ksl], rhs=QrT[h][:, qsl], start=False, stop=True),
                 reads=[KrT.sub(kb // 4), QrT[h].sub(qb)], writes=[st])
            P.op('act', lambda e: e.activation(out=pt[:], in_=st[:], func=AF.Exp, scale=SC), reads=[st], writes=[pt])
            if r >= 0:
                P.op('dve', lambda e: e.tensor_tensor(out=pt[:], in0=pt[:], in1=cm[:, r, :], op=ALU.mult), reads=[pt, cm], writes=[pt])

        def stage2(i):
            (h, qb, kb, nkb, q_) = blocks[i]
            qsl = slice(512 * qb, 512 * qb + 512)
            pt = pts[i % 4]
            pn = psum[4 + 2 * (q_ % 2)]; pd = psum[5 + 2 * (q_ % 2)]
            P.op('pe', lambda e: e.matmul(pn[0:64, :], lhsT=Vtok[:, kb, 64 * h:64 * h + 64], rhs=pt[:], start=(kb == 0), stop=(kb == nkb - 1)),
                 reads=[Vtok, pt], writes=[pn])
            P.op('pe', lambda e: e.matmul(pd[0:64, :], lhsT=ones[:, 0:64], rhs=pt[:], start=(kb == 0), stop=(kb == nkb - 1)),
                 reads=[ones, pt], writes=[pd])
            if kb == nkb - 1:
                ob = obs[q_ % 2]
                P.op('dve', lambda e: e.reciprocal(out=rec[:], in_=pd[0:64, :]), reads=[pd], writes=[rec])
                P.op('dve', lambda e: e.tensor_tensor(out=ob[:], in0=pn[0:64, :], in1=rec[:], op=ALU.mult), reads=[pn, rec], writes=[ob])
                P.dma('sp', self.ocat_d[4 + h, :, qsl], ob[:], reads=[ob], writes=[self.ocat_d])

        SK = 2
        for i in range(min(SK, len(blocks))):
            stage1(i)
        for i in range(len(blocks)):
            if i + SK < len(blocks):
                stage1(i + SK)
            stage2(i)

    def mixer_D(self, l):
        P = self.P
        xT = self.xT
        psum = self.psum
        self._pn = 0
        W = P.sb([128, 8, 768], BF16, 'wD')
        P.dma('pool', W[:], self.w_in_d[l, :, 2408:3176].rearrange('(k p) n -> p k n', p=128), writes=[W])
        ones = P.sb([128, 64], BF16, 'onesD')
        P.op('pool', lambda e: e.memset(ones[:], 1.0), writes=[ones])
        geoms = [(1, 0, 'tri_ge'), (1, 128, 'tri_le'), (4, 0, 'tri_ge'), (4, 128, 'tri_le'), (16, 0, 'tri_ge')]
        tabs = []
        tmpf = P.sb([128, 128], F32, 'tabtmp')
        for (dil, off, mk) in geoms:
            tab = P.sb([128, 4, 128], BF16, 'tabD')
            for h in range(4):
                slope = 2.0 ** (-8.0 * (h + 1) / 4)
                P.op('dve', lambda e, mk=mk: e.tensor_tensor(out=tmpf[:], in0=self.c('dist'), in1=self.c(mk), op=ALU.mult),
                     reads=[self.cst], writes=[tmpf])
                P.op('act', lambda e, dil=dil, off=off, slope=slope: e.activation(
                    out=tmpf[:], in_=tmpf[:], func=AF.Exp, scale=-slope * dil, bias=-slope * dil * off),
                    reads=[tmpf], writes=[tmpf])
                P.op('dve', lambda e, tab=tab, h=h, mk=mk: e.tensor_tensor(out=tab[:, h, :], in0=tmpf[:], in1=self.c(mk), op=ALU.mult),
                     reads=[tmpf, self.cst], writes=[tab])
            tabs.append(tab)
        QT = [P.sb([64, S], BF16, 'qTD') for _ in range(4)]
        KT = [P.sb([64, S], BF16, 'kTD') for _ in range(4)]
        n = 0
        for i in range(8):
            for tb in range(4):
                ps = psum[n % 2]; n += 1
                for kc in range(8):
                    P.op('pe', lambda e, ps=ps, kc=kc, i=i, tb=tb: e.matmul(
                        ps[0:64, :], lhsT=W[:, kc, 64 * i:64 * i + 64], rhs=xT[:, kc, 512 * tb:512 * tb + 512],
                        start=(kc == 0), stop=(kc == 7)), reads=[W, xT.sub(tb)], writes=[ps])
                dst = QT[i] if i < 4 else KT[i - 4]
                P.op('act', lambda e, ps=ps, dst=dst, tb=tb, i=i: e.mul(out=dst[:, 512 * tb:512 * tb + 512], in_=ps[0:64, :],
                                                                       mul=(0.125 if i < 4 else 1.0)), reads=[ps], writes=[dst])
        NUM = P.sb([64, 4, S], F32, 'numD')
        DEN = P.sb([64, 4, S], F32, 'denD')
        Vt = [P.sb([128, 256], BF16, 'vD') for _ in range(3)]
        PC = [P.sb([128, 4, 128], BF16, 'pcD') for _ in range(2)]
        PP = [P.sb([128, 4, 128], BF16, 'ppD') for _ in range(2)]
        units = []
        for nb in range(16):
            units.append((0, slice(128 * nb, 128 * nb + 128), slice(128 * (nb - 1), 128 * nb) if nb > 0 else None, 0, 1))
        for r in range(4):
            for b in range(4):
                units.append((1, slice(512 * b + r, 512 * (b + 1), 4), slice(512 * (b - 1) + r, 512 * b, 4) if b > 0 else None, 2, 3))
        for r in range(16):
            units.append((2, slice(r, S, 16), None, 4, None))
        nv = 0
        vts = {}

        def stage1(u):
            nonlocal n, nv
            (br, Tq, Tp, gc, gp) = units[u]
            vcur = Vt[nv % 3]; nv += 1
            vts[u] = vcur
            ps = psum[n % 2]; n += 1
            for kc in range(8):
                P.op('pe', lambda e, ps=ps, kc=kc, Tq=Tq: e.matmul(
                    ps[:, 0:256], lhsT=xT[:, kc, Tq], rhs=W[:, kc, 512:768], start=(kc == 0), stop=(kc == 7)),
                    reads=[W, xT.sub(0), xT.sub(1), xT.sub(2), xT.sub(3)], writes=[ps])
            P.op('act', lambda e, ps=ps, vcur=vcur: e.copy(out=vcur[:], in_=ps[:, 0:256]), reads=[ps], writes=[vcur])
            sc = psum[2 + 2 * (u % 2)]
            sp_ = psum[3 + 2 * (u % 2)]
            pc = PC[u % 2]; pp = PP[u % 2]
            for h in range(4):
                P.op('pe', lambda e, h=h, sc=sc, Tq=Tq: e.matmul(
                    sc[:, 128 * h:128 * h + 128], lhsT=KT[h][:, Tq], rhs=QT[h][:, Tq], start=True, stop=True),
                    reads=[KT[h], QT[h]], writes=[sc])
            P.op('act', lambda e, sc=sc, pc=pc: e.activation(out=pc[:].rearrange('p h q -> p (h q)'), in_=sc[:], func=AF.Exp),
                 reads=[sc], writes=[pc])
            P.op('dve', lambda e, pc=pc, gc=gc: e.tensor_tensor(out=pc[:], in0=pc[:], in1=tabs[gc][:], op=ALU.mult),
                 reads=[pc, tabs[gc]], writes=[pc])
            if Tp is not None:
                for h in range(4):
                    P.op('pe', lambda e, h=h, sp_=sp_, Tq=Tq, Tp=Tp: e.matmul(
                        sp_[:, 128 * h:128 * h + 128], lhsT=KT[h][:, Tp], rhs=QT[h][:, Tq], start=True, stop=True),
                        reads=[KT[h], QT[h]], writes=[sp_])
                P.op('act', lambda e, sp_=sp_, pp=pp: e.activation(out=pp[:].rearrange('p h q -> p (h q)'), in_=sp_[:], func=AF.Exp),
                     reads=[sp_], writes=[pp])
                P.op('dve', lambda e, pp=pp, gp=gp: e.tensor_tensor(out=pp[:], in0=pp[:], in1=tabs[gp][:], op=ALU.mult),
                     reads=[pp, tabs[gp]], writes=[pp])

        def stage2(u):
            (br, Tq, Tp, gc, gp) = units[u]
            vcur = vts[u]; vprev = vts.get(u - 1)
            pc = PC[u % 2]; pp = PP[u % 2]
            pn = psum[6]; pd = psum[7]
            for h in range(4):
                P.op('pe', lambda e, h=h, pc=pc, vcur=vcur, last=(Tp is None): e.matmul(
                    pn[0:64, 128 * h:128 * h + 128], lhsT=vcur[:, 64 * h:64 * h + 64], rhs=pc[:, h, :], start=True, stop=last),
                    reads=[vcur, pc], writes=[pn])
                if Tp is not None:
                    P.op('pe', lambda e, h=h, pp=pp, vprev=vprev: e.matmul(
                        pn[0:64, 128 * h:128 * h + 128], lhsT=vprev[:, 64 * h:64 * h + 64], rhs=pp[:, h, :], start=False, stop=True),
                        reads=[vprev, pp], writes=[pn])
                P.op('pe', lambda e, h=h, pc=pc, last=(Tp is None): e.matmul(
                    pd[0:64, 128 * h:128 * h + 128], lhsT=ones[:], rhs=pc[:, h, :], start=True, stop=last),
                    reads=[ones, pc], writes=[pd])
                if Tp is not None:
                    P.op('pe', lambda e, h=h, pp=pp: e.matmul(
                        pd[0:64, 128 * h:128 * h + 128], lhsT=ones[:], rhs=pp[:, h, :], start=False, stop=True),
                        reads=[ones, pp], writes=[pd])
            for (acc_t, pt) in ((NUM, pn), (DEN, pd)):
                src = pt[0:64, :].rearrange('p (h q) -> p h q', h=4)
                if br == 0:
                    P.op('act', lambda e, acc_t=acc_t, src=src, Tq=Tq: e.copy(out=acc_t[:, :, Tq], in_=src), reads=[pt], writes=[acc_t])
                else:
                    P.op('dve', lambda e, acc_t=acc_t, src=src, Tq=Tq: e.tensor_tensor(out=acc_t[:, :, Tq], in0=acc_t[:, :, Tq], in1=src, op=ALU.add),
                         reads=[pt, acc_t], writes=[acc_t])

        stage1(0)
        for u in range(len(units)):
            if u + 1 < len(units):
                stage1(u + 1)
            stage2(u)
        ob = [P.sb([64, S], BF16, 'oD') for _ in range(2)]
        for h in range(4):
            o = ob[h % 2]
            P.op('dve', lambda e, h=h: e.reciprocal(out=DEN[:, h, :], in_=DEN[:, h, :]), reads=[DEN], writes=[DEN])
            P.op('dve', lambda e, h=h, o=o: e.tensor_tensor(out=o[:], in0=NUM[:, h, :], in1=DEN[:, h, :], op=ALU.mult), reads=[NUM, DEN], writes=[o])
            P.dma('sp', self.ocat_d[12 + h], o[:], reads=[o], writes=[self.ocat_d])

    def mixer_stub(self, l):
        P = self.P
        W = P.sb([128, 8, 1024], BF16, 'wstub')
        P.dma('pool', W[:], self.w_in_d[l, :, 0:1024].rearrange('(k p) n -> p k n', p=128), writes=[W])
        ob = [P.sb([64, 512], BF16, 'ostub') for _ in range(2)]
        n = 0
        for c in range(16):
            for tb in range(4):
                ps = self.psum[n % 2]
                o = ob[n % 2]
                n += 1
                for kc in range(8):
                    P.op('pe', lambda e, ps=ps, kc=kc, c=c, tb=tb: e.matmul(
                        ps[0:64, :], lhsT=W[:, kc, 64 * c:64 * c + 64], rhs=self.xT[:, kc, 512 * tb:512 * tb + 512],
                        start=(kc == 0), stop=(kc == 7)), reads=[W, self.xT.sub(tb)], writes=[ps])
                P.op('act', lambda e, ps=ps, o=o: e.copy(out=o[:], in_=ps[0:64, :]), reads=[ps], writes=[o])
                P.dma('sp', self.ocat_d[c, :, 512 * tb:512 * tb + 512], o[:], reads=[o], writes=[self.ocat_d])

    def ln_tile(self, r, g_rep, b_rep, out, r_ap=None, r_res=None, slot=None):
        for _ in self.ln_tile_gen(r, g_rep, b_rep, out, r_ap, r_res, slot):
            pass

    def ln_tile_gen(self, r, g_rep, b_rep, out, r_ap=None, r_res=None, slot=None):
        P = self.P
        if r_ap is None:
            r_ap = r[:]
            r_res = r
        rl = list(r_res) if isinstance(r_res, (list, tuple)) else [r_res]
        if slot is None:
            self._lnk += 1
            slot = self._lnk % 2
        st, mv, sc = self._lntmp[slot]
        for h in range(2):
            P.op('dve', lambda e, h=h: e.bn_stats(out=st[:, h, :], in_=r_ap[:, 512 * h:512 * h + 512]), reads=rl, writes=[st])
        P.op('dve', lambda e: e.bn_aggr(out=mv[:], in_=st[:].rearrange('p a b -> p (a b)')), reads=[st], writes=[mv])
        yield
        P.op('act', lambda e: e.activation(out=sc[:, 0:1], in_=mv[:, 1:2], func=AF.Ln, bias=EPS), reads=[mv], writes=[sc])
        P.op('act', lambda e: e.activation(out=sc[:, 0:1], in_=sc[:, 0:1], func=AF.Exp, scale=-0.5), reads=[sc], writes=[sc])
        yield
        P.op('dve', lambda e: e.scalar_tensor_tensor(out=sc[:, 1:2], in0=mv[:, 0:1], scalar=-1.0, in1=sc[:, 0:1],
                                                     op0=ALU.mult, op1=ALU.mult), reads=[mv, sc], writes=[sc])
        yield
        P.op('act', lambda e: e.activation(out=out[:], in_=r_ap, func=AF.Identity, bias=sc[:, 1:2], scale=sc[:, 0:1]),
             reads=rl + [sc], writes=[out])
        yield
        P.op('dve', lambda e: e.tensor_tensor(out=out[:], in0=out[:], in1=g_rep[:], op=ALU.mult), reads=[out, g_rep], writes=[out])
        P.op('dve', lambda e: e.tensor_tensor(out=out[:], in0=out[:], in1=b_rep[:], op=ALU.add), reads=[out, b_rep], writes=[out])
        yield

    def dense(self, l, last):
        P = self.P
        xT = self.xT
        acc = P.sb([128, 16, D], F32, 'acc')
        self._lnk = 0
        self._lntmp = [(P.sb([128, 2, 6], F32, 'bnst'), P.sb([128, 2], F32, 'mv'), P.sb([128, 2], F32, 'lnsc')) for _ in range(2)]
        x1 = [P.sb([128, D], F32, 'x1') for _ in range(2)]
        x_src = self.x_d if l == 0 else self.xres_d
        with P.scope():
            g1 = P.sb([128, D], F32, 'g1'); b1 = P.sb([128, D], F32, 'b1')
            for t, d in ((g1, self.ln1_g_d), (b1, self.ln1_b_d)):
                P.dma('sp', t[:], d[l:l + 1, :].partition_broadcast(128), writes=[t])
            wout = P.sb([64, 16, D], BF16, 'wout')
            P.dma('pool', wout[:], self.w_out_d[l].rearrange('(c p) n -> p c n', p=64), writes=[wout])
            ocs = [P.sb([64, 16, 512], BF16, 'oc') for _ in range(2)]
            xr = [P.sb([128, D], F32, 'xr') for _ in range(2)]
            rr = [P.sb([128, D], F32, 'rr') for _ in range(2)]
            def ln1_gen(tt):
                tb = tt // 4
                oc = ocs[tb % 2]
                if tt % 4 == 0:
                    P.dma('sp', oc[:], self.ocat_d[:, :, 512 * tb:512 * tb + 512].rearrange('c p t -> p c t'),
                          reads=[self.ocat_d], writes=[oc])
                xrt = xr[tt % 2]; r = rr[tt % 2]; x1t = x1[tt % 2]
                P.dma('sp', xrt[:], x_src[128 * tt:128 * tt + 128, :], reads=[x_src], writes=[xrt])
                for half in range(2):
                    ps = self.psum[2 * (tt % 2) + half]
                    for c in range(16):
                        P.op('pe', lambda e, ps=ps, c=c, half=half: e.matmul(
                            ps[:], lhsT=oc[:, c, 128 * (tt % 4):128 * (tt % 4) + 128], rhs=wout[:, c, 512 * half:512 * half + 512],
                            start=(c == 0), stop=(c == 15)), reads=[oc, wout], writes=[ps])
                yield
                for half in range(2):
                    ps = self.psum[2 * (tt % 2) + half]
                    P.op('dve', lambda e, ps=ps, half=half: e.scalar_tensor_tensor(
                        out=r[:, 512 * half:512 * half + 512], in0=xrt[:, 512 * half:512 * half + 512], scalar=ALPHA, in1=ps[:],
                        op0=ALU.mult, op1=ALU.add), reads=[ps, xrt], writes=[r])
                yield
                yield from self.ln_tile_gen(r, g1, b1, x1t, slot=tt % 2)
                P.op('act', lambda e: e.mul(out=acc[:, tt, :], in_=x1t[:], mul=ALPHA), reads=[x1t], writes=[acc.sub((tt, 0)), acc.sub((tt, 1))])
                self.transpose_tile_to_xT(x1t, tt, pbase=4 + 2 * (tt % 2))
                yield

            pipeline(ln1_gen, range(16), width=2, skew=4)
        with P.scope():
            g2 = P.sb([128, D], F32, 'g2'); b2 = P.sb([128, D], F32, 'b2')
            for t, d in ((g2, self.ln2_g_d), (b2, self.ln2_b_d)):
                P.dma('sp', t[:], d[l:l + 1, :].partition_broadcast(128), writes=[t])
            HC = 1024
            nhc = D_FF // HC
            NM = HC // 128
            w1s = [P.sb([128, 8, HC], BF16, 'w1') for _ in range(2)]
            w2s = [P.sb([128, NM, D], BF16, 'w2') for _ in range(2)]
            fts = [P.sb([128, NM, 512], BF16, 'fT') for _ in range(2)]
            nf = 0
            n1 = 0
            n2 = 0
            for j in range(nhc):
                w1 = w1s[j % 2]; w2 = w2s[j % 2]
                P.dma('pool', w1[:], self.w_ff1_d[l, :, HC * j:HC * j + HC].rearrange('(k p) n -> p k n', p=128), writes=[w1])
                P.dma('pool', w2[:], self.w_ff2_d[l, HC * j:HC * j + HC, :].rearrange('(m p) n -> p m n', p=128), writes=[w2])
                for tb in range(4):
                    ft = fts[nf % 2]; nf += 1
                    for m in range(NM):
                        ps = self.psum[n1 % 3]; n1 += 1
                        for kc in range(8):
                            P.op('pe', lambda e, ps=ps, kc=kc, m=m, w1=w1, tb=tb: e.matmul(
                                ps[:], lhsT=w1[:, kc, 128 * m:128 * m + 128], rhs=xT[:, kc, 512 * tb:512 * tb + 512],
                                start=(kc == 0), stop=(kc == 7)), reads=[w1, xT.sub(tb)], writes=[ps])
                        P.op('act', lambda e, ps=ps, m=m, ft=ft: e.activation(out=ft[:, m, :], in_=ps[:], func=AF.Relu),
                             reads=[ps], writes=[ft.sub(m)])
                        P.op('act', lambda e, m=m, ft=ft: e.activation(out=ft[:, m, :], in_=ft[:, m, :], func=AF.Square),
                             reads=[ft.sub(m)], writes=[ft.sub(m)])
                    for t4 in range(4):
                        tt = 4 * tb + t4
                        for half in range(2):
                            ps = self.psum[3 + (n2 % 5)]; n2 += 1
                            for m in range(NM):
                                P.op('pe', lambda e, ps=ps, m=m, ft=ft, t4=t4, w2=w2, half=half: e.matmul(
                                    ps[:], lhsT=ft[:, m, 128 * t4:128 * t4 + 128], rhs=w2[:, m, 512 * half:512 * half + 512],
                                    start=(m == 0), stop=(m == NM - 1)), reads=[ft.sub(m), w2], writes=[ps])
                            P.op('dve', lambda e, ps=ps, tt=tt, half=half: e.tensor_tensor(
                                out=acc[:, tt, 512 * half:512 * half + 512], in0=acc[:, tt, 512 * half:512 * half + 512], in1=ps[:], op=ALU.add),
                                reads=[ps, acc.sub((tt, half))], writes=[acc.sub((tt, half))])
            def ln2_gen(tt):
                x2 = x1[tt % 2]
                yield from self.ln_tile_gen(None, g2, b2, x2, r_ap=acc[:, tt, :], r_res=[acc.sub((tt, 0)), acc.sub((tt, 1))], slot=tt % 2)
                if last:
                    P.dma('sp', self.y_d[128 * tt:128 * tt + 128, :], x2[:], reads=[x2], writes=[self.y_d])
                else:
                    P.dma('sp', self.xres_d[128 * tt:128 * tt + 128, :], x2[:], reads=[x2], writes=[self.xres_d])
                    self.transpose_tile_to_xT(x2, tt, pbase=4 + 2 * (tt % 2))
                yield

            pipeline(ln2_gen, range(16), width=2, skew=3)


def pipeline(make_gen, items, width=2, skew=3):
    items = list(items)
    active = []
    nxt = 0
    while nxt < len(items) or active:
        if nxt < len(items) and len(active) < width and (not active or active[-1][1] >= skew):
            active.append([make_gen(items[nxt]), 0])
            nxt += 1
        keep = []
        for a in active:
            try:
                next(a[0])
                a[1] += 1
                keep.append(a)
            except StopIteration:
                pass
        active = keep


def lockstep(gens):
    gens = list(gens)
    while gens:
        nxt = []
        for g in gens:
            try:
                next(g)
                nxt.append(g)
            except StopIteration:
                pass
        gens = nxt


def _rope_table():
    half = 16
    inv_freq = (np.float32(10000.0) ** (-np.arange(half, dtype=np.float32) / np.float32(half))).astype(np.float32)
    ang = (np.arange(S, dtype=np.float32)[None, :] * inv_freq[:, None]).astype(np.float32)
    ang = np.concatenate([ang, ang], axis=0)
    return np.stack([np.cos(ang), np.sin(ang)]).astype(np.float32)


ROPE_TAB = _rope_table()
RMASK = np.stack([(np.arange(S) % 16 != 0), (np.arange(S) % 64 != 0)]).astype(np.float32)
_CACHE = {}


def get_builder(stage='full', nlayers=DEPTH):
    key = (stage, nlayers)
    if key not in _CACHE:
        _CACHE[key] = Builder(stage, nlayers)
    return _CACHE[key]


def make_in_maps(b, inputs, cores):
    maps = []
    for ci in cores:
        m = {'x': np.ascontiguousarray(inputs['x'][ci]), 'cst': b.cst_np, 'rope': ROPE_TAB, 'rmask': RMASK}
        for k in ('w_in', 'w_out', 'w_ff1', 'w_ff2', 'ln1_g', 'ln1_b', 'ln2_g', 'ln2_b',
                  'mla_q_norm_g', 'mla_kv_norm_g', 'mla_w_uq', 'mla_w_ukv', 'hgrn_lb_logits', 'hgrn_norm_g',
                  'gdn_conv_w', 'gdn_a_log', 'gdn_dt_bias', 'gdn_norm_g'):
            m[k] = np.ascontiguousarray(inputs[k])
        maps.append(m)
    return maps


def kernel(**inputs):
    inputs = {k: np.asarray(v) for k, v in inputs.items()}
    b = get_builder('full')
    maps = make_in_maps(b, inputs, list(range(NCORES)))
    res = run_bass_kernel_spmd(b.nc, maps, core_ids=list(range(NCORES)))
    return np.stack([r['y'] for r in res.results], axis=0).astype(np.float32)
```

```python
import math
import numpy as np
import concourse.bass as bass
import concourse.mybir as mybir
from concourse.bass_utils import run_bass_kernel_spmd
from contextlib import ExitStack, contextmanager

F32 = mybir.dt.float32
BF16 = mybir.dt.bfloat16
AF = mybir.ActivationFunctionType
ALU = mybir.AluOpType
AX = mybir.AxisListType

COMPUTE = ('pe', 'act', 'dve', 'pool')
ENGS = ('pe', 'act', 'dve', 'pool', 'sp')
N_DMA_SEMS = 6

S = 2048
D = 1024
NCORES = 8
DEPTH = 2
D_IN = 3176
D_FF = 4096
ALPHA = (2 * DEPTH) ** 0.25
EPS = 1e-6


class Res:
    __slots__ = ('name', 'w', 'r')

    def __init__(self, name='', barrier=None):
        self.name = name
        self.w = None
        self.r = dict(barrier) if barrier else {}


class T:
    def __init__(self, handle, name, barrier=None):
        self.h = handle
        self.name = name
        self.res = Res(name, barrier)
        self.subs = {}
        self._barrier = barrier

    def __getitem__(self, k):
        return self.h[k]

    def sub(self, key):
        r = self.subs.get(key)
        if r is None:
            r = self.subs[key] = Res(f'{self.name}.{key}', self._barrier)
        return r

    def all_res(self):
        return [self.res] + list(self.subs.values())


def _res(x):
    return x.res if isinstance(x, T) else x


def ones_f(P, b):
    t = getattr(b, '_ones_f', None)
    if t is None:
        raise RuntimeError('ones_f not allocated')
    return t


def _mx(d, k, v):
    if v > d.get(k, -1):
        d[k] = v


class Prog:
    def __init__(self, nc):
        self.nc = nc
        self.stack = ExitStack()
        self.ins = {e: [] for e in ENGS}
        self.nalloc = 0
        self.barrier = {}
        self.scopes = []
        self.dma_k = {e: 0 for e in ENGS}
        self.dma_cnt = {}

    def _reg(self, t):
        if self.scopes:
            self.scopes[-1][1].append(t)
        return t

    def sb(self, shape, dt=F32, name=None):
        self.nalloc += 1
        name = f'{name or "t"}{self.nalloc}'
        st = self.scopes[-1][0] if self.scopes else self.stack
        h = st.enter_context(self.nc.sbuf_tensor(name, list(shape), dt))
        return self._reg(T(h, name, dict(self.barrier)))

    def ps(self, shape, dt=F32, name=None):
        self.nalloc += 1
        name = f'{name or "p"}{self.nalloc}'
        h = self.stack.enter_context(self.nc.psum_tensor(name, list(shape), dt))
        return T(h, name)

    def dram(self, shape, dt=F32, name=None, kind='Internal'):
        self.nalloc += 1
        name = name or f'd{self.nalloc}'
        h = self.nc.dram_tensor(name, list(shape), dt, kind=kind)
        return T(h.ap(), name)

    @contextmanager
    def scope(self):
        st = ExitStack()
        tiles = []
        self.scopes.append((st, tiles))
        try:
            yield
        finally:
            self.scopes.pop()
            st.close()
            b = self.barrier
            for t in tiles:
                for r in t.all_res():
                    if r.w is not None:
                        _mx(b, r.w[0], r.w[1])
                    for k, v in r.r.items():
                        _mx(b, k, v)

    def op(self, eng, fn, reads=(), writes=(), dma=False):
        idx = len(self.ins[eng])
        deps = {}
        for r in reads:
            r = _res(r)
            if r.w is not None:
                _mx(deps, r.w[0], r.w[1])
        for w in writes:
            w = _res(w)
            if w.w is not None:
                _mx(deps, w.w[0], w.w[1])
            for k, v in w.r.items():
                _mx(deps, k, v)
        rec = dict(fn=fn, deps=deps, dma=dma, inc=False)
        if dma:
            key = ('d', eng, self.dma_k[eng] % N_DMA_SEMS)
            self.dma_k[eng] += 1
            prev = self.dma_cnt.get(key, 0)
            if prev:
                _mx(deps, key, prev)
            self.dma_cnt[key] = prev + 16
            tok = (key, prev + 16)
            rec['dsem'] = key
        else:
            tok = (('c', eng), idx)
        self.ins[eng].append(rec)
        for r in reads:
            _mx(_res(r).r, tok[0], tok[1])
        for w in writes:
            w = _res(w)
            w.w = tok
            w.r = {}
        return tok

    def dma(self, eng, out, in_, reads=(), writes=(), **kw):
        return self.op(eng, lambda e: e.dma_start(out=out, in_=in_, **kw), reads=reads, writes=writes, dma=True)

    def emit(self):
        nc = self.nc
        ins = self.ins
        for E in ENGS:
            seen = {}
            for idx, rec in enumerate(ins[E]):
                waits = []
                for k, val in rec['deps'].items():
                    if k == ('c', 'pe') and E == 'pe':
                        continue
                    if k == ('c', E) and val >= idx:
                        continue
                    if val <= seen.get(k, -1):
                        continue
                    seen[k] = val
                    waits.append((k, val))
                    if k[0] == 'c':
                        ins[k[1]][val]['inc'] = True
                rec['waits'] = waits
        for e in ENGS:
            c = 0
            for rec in ins[e]:
                if rec['inc'] and not rec['dma']:
                    c += 1
                    rec['semval'] = c
        st = self.stack
        csem = {e: st.enter_context(nc.semaphore(f's_{e}')) for e in ENGS}
        dsem = {key: st.enter_context(nc.semaphore(f'd_{key[1]}{key[2]}')) for key in self.dma_cnt}
        block = st.enter_context(nc.Block())

        def replay(E, eng):
            for rec in ins[E]:
                for (k, val) in rec['waits']:
                    if k[0] == 'c':
                        eng.wait_ge(csem[k[1]], ins[k[1]][val]['semval'])
                    else:
                        eng.wait_ge(dsem[k], val)
                r = rec['fn'](eng)
                if rec['dma']:
                    r.then_inc(dsem[rec['dsem']], 16)
                elif rec['inc']:
                    r.then_inc(csem[E], 1)

        @block.tensor
        def _(e):
            replay('pe', e)

        @block.scalar
        def _(e):
            replay('act', e)

        @block.vector
        def _(e):
            replay('dve', e)

        @block.gpsimd
        def _(e):
            replay('pool', e)

        @block.sync
        def _(e):
            replay('sp', e)

    def close(self):
        self.stack.close()


def make_consts():
    c = {}
    c['ident'] = np.eye(128, dtype=np.float32)
    kj = np.arange(128)[:, None]
    qi = np.arange(128)[None, :]
    c['dist'] = (qi - kj).astype(np.float32)
    c['tri_ge'] = (qi >= kj).astype(np.float32)
    c['tri_le'] = (qi <= kj).astype(np.float32)
    c['mask16'] = ((qi >= kj) & (qi // 16 == kj // 16)).astype(np.float32)
    c['cm8'] = (np.arange(128)[:, None] // 16 == np.arange(8)[None, :]).astype(np.float32)
    f = np.arange(512)[None, None, :]
    p = np.arange(128)[:, None, None]
    r = np.arange(4)[None, :, None]
    c['cmask'] = (f >= 128 * r + p).astype(np.float32)
    return c


CONST_ORDER = ['ident', 'dist', 'tri_ge', 'tri_le', 'mask16', 'cm8', 'cmask']


def pack_consts():
    c = make_consts()
    cols = {}
    arrs = []
    off = 0
    for k in CONST_ORDER:
        a = c[k].reshape(128, -1).astype(np.float32)
        cols[k] = (off, a.shape[1])
        arrs.append(a)
        off += a.shape[1]
    return np.concatenate(arrs, axis=1), cols


class Builder:
    def __init__(self, stage='full', nlayers=DEPTH):
        self.stage = stage
        self.nlayers = nlayers
        nc = self.nc = bass.Bass("TRN2", target_bir_lowering=False)
        P = self.P = Prog(nc)
        self.cst_np, self.cst_cols = pack_consts()
        ein = lambda name, shape: P.dram(shape, F32, name, kind='ExternalInput')
        self.x_d = ein('x', [S, D])
        self.w_in_d = ein('w_in', [DEPTH, D, D_IN])
        self.w_out_d = ein('w_out', [DEPTH, D, D])
        self.w_ff1_d = ein('w_ff1', [DEPTH, D, D_FF])
        self.w_ff2_d = ein('w_ff2', [DEPTH, D_FF, D])
        self.ln1_g_d = ein('ln1_g', [DEPTH, D])
        self.ln1_b_d = ein('ln1_b', [DEPTH, D])
        self.ln2_g_d = ein('ln2_g', [DEPTH, D])
        self.ln2_b_d = ein('ln2_b', [DEPTH, D])
        self.cst_d = ein('cst', list(self.cst_np.shape))
        self.rope_d = ein('rope', [2, 32, S])
        self.rmask_d = ein('rmask', [2, S])
        self.gdn_conv_d = ein('gdn_conv_w', [DEPTH, 4, 768])
        self.gdn_alog_d = ein('gdn_a_log', [DEPTH, 4])
        self.gdn_dtb_d = ein('gdn_dt_bias', [DEPTH, 4])
        self.gdn_g_d = ein('gdn_norm_g', [DEPTH, 64])
        _k = 'ExternalOutput' if stage == 'A' else 'Internal'
        self.kkd = P.dram([128, 64, 64], F32, 'kk_scr', kind=_k)
        self.qkd = P.dram([128, 64, 64], F32, 'qk_scr', kind=_k)
        self.ttd = P.dram([128, 64, 64], F32, 'tt_scr', kind=_k)
        self.qktd = P.dram([128, 64, 64], F32, 'qkt_scr', kind=_k)
        self.tokd = P.dram([4, 3, 64, 32, 64], F32, 'tok_scr', kind=_k)
        self.qed = P.dram([4, 64, S], F32, 'qe_scr', kind=_k)
        self.gsd = P.dram([4, 64, S], F32, 'gs_scr', kind=_k)
        self.rows8d = P.dram([2, 8, S], F32, 'rows8_scr', kind=_k)
        self.qhd = P.dram([4, 64, S], F32, 'qh_scr')
        self.ohd = P.dram([4, 64, S], F32, 'oh_scr')
        self.hgrn_lb_d = ein('hgrn_lb_logits', [DEPTH, 256])
        self.hgrn_g_d = ein('hgrn_norm_g', [DEPTH, 64])
        self.mla_qg_d = ein('mla_q_norm_g', [DEPTH, 192])
        self.mla_kvg_d = ein('mla_kv_norm_g', [DEPTH, 128])
        self.mla_wuq_d = ein('mla_w_uq', [DEPTH, 192, 384])
        self.mla_wukv_d = ein('mla_w_ukv', [DEPTH, 128, 512])
        self.y_d = P.dram([S, D], F32, 'y', kind='ExternalOutput')
        self.ocat_d = P.dram([16, 64, S], BF16, 'ocat_scr', kind=('Internal' if stage in ('full', 'dense') else 'ExternalOutput'))
        self.xres_d = P.dram([S, D], F32, 'xres_scr')
        self.dbg = {}
        self.cst = P.sb(list(self.cst_np.shape), F32, 'cst')
        P.dma('sp', self.cst[:], self.cst_d[:], writes=[self.cst])
        self.xT = P.sb([128, 8, S], BF16, 'xT')
        self._ones_f = P.sb([32, 64], F32, 'ones_f')
        P.op('pool', lambda e: e.memset(self._ones_f[:], 1.0), writes=[self._ones_f])
        self.psum = [P.ps([128, 512], F32, f'ps{i}') for i in range(8)]
        self.build()
        outs = [self.y_d, self.ocat_d, self.kkd, self.qkd, self.ttd, self.qktd, self.tokd, self.qed, self.gsd, self.rows8d] + [t for t in self.dbg.values()]
        P.op('sp', lambda e: e.nop(), reads=outs)
        P.emit()
        P.close()

    def dump(self, name, t, ap, shape):
        if self.stage in ('full', 'dense'):
            return
        d = self.P.dram(list(shape), F32, 'dbg_' + name, kind='ExternalOutput')
        self.P.dma('sp', d[:], ap, reads=[t], writes=[d])
        self.dbg[name] = d

    def c(self, name):
        o, n = self.cst_cols[name]
        return self.cst[:, o:o + n]

    def build(self):
        P = self.P
        self.load_x_transposed()
        for l in range(self.nlayers):
            last = (l == self.nlayers - 1)
            with P.scope():
                self.mixers(l)
            if self.stage in ('full', 'dense'):
                with P.scope():
                    self.dense(l, last)

    def load_x_transposed(self):
        P = self.P
        xT = self.xT
        with P.scope():
            xt = [P.sb([128, D], F32, 'xin') for _ in range(2)]
            for tt in range(16):
                t = xt[tt % 2]
                P.dma('sp', t[:], self.x_d[128 * tt:128 * tt + 128, :], writes=[t])
                self.transpose_tile_to_xT(t, tt)

    def transpose_tile_to_xT(self, t, tt, pbase=6):
        P = self.P
        ident = self.c('ident')
        for half in range(2):
            ps = self.psum[pbase + half]
            for j in range(4):
                kc = half * 4 + j
                P.op('pe', lambda e, ps=ps, j=j, kc=kc: e.transpose(ps[:, 128 * j:128 * j + 128], t[:, 128 * kc:128 * kc + 128], ident),
                     reads=[t, self.cst], writes=[ps])
            P.op('act', lambda e, ps=ps, half=half: e.copy(
                out=self.xT[:, 4 * half:4 * half + 4, 128 * tt:128 * tt + 128],
                in_=ps[:].rearrange('p (k t) -> p k t', k=4)),
                reads=[ps], writes=[self.xT.sub(tt // 4)])

    def mixers(self, l):
        if self.stage == 'dense':
            self.mixer_stub(l)
            return
        if self.stage in ('full', 'D'):
            with self.P.scope():
                self.mixer_D(l)

        if self.stage in ('full', 'A'):
            with self.P.scope():
                self.mixer_A(l)
        if self.stage in ('full', 'B'):
            with self.P.scope():
                self.mixer_B(l)
        if self.stage in ('full', 'C'):
            with self.P.scope():
                self.mixer_C(l)

    def rms_gate_out(self, o, gate, gcol, ones, slot, tb, sq, rs, ob):
        for _ in self.rms_gate_out_gen(o, gate, gcol, ones, slot, tb, sq, rs, ob):
            pass

    def rms_gate_out_gen(self, o, gate, gcol, ones, slot, tb, sq, rs, ob):
        P = self.P
        sl = slice(512 * tb, 512 * tb + 512)
        P.op('act', lambda e: e.activation(out=sq[:], in_=o[:], func=AF.Square), reads=[o], writes=[sq])
        yield
        ps = self.psum[self._pn % 2]; self._pn += 1
        P.op('pe', lambda e: e.matmul(ps[0:64, :], lhsT=ones[0:64, 0:64], rhs=sq[:], start=True, stop=True), reads=[ones, sq], writes=[ps])
        yield
        P.op('act', lambda e: e.activation(out=rs[:], in_=ps[0:64, :], func=AF.Ln, scale=1.0 / 64, bias=EPS), reads=[ps], writes=[rs])
        P.op('act', lambda e: e.activation(out=rs[:], in_=rs[:], func=AF.Exp, scale=-0.5), reads=[rs], writes=[rs])
        yield
        P.op('dve', lambda e: e.scalar_tensor_tensor(out=o[:], in0=o[:], scalar=gcol, in1=rs[:], op0=ALU.mult, op1=ALU.mult),
             reads=[o, rs], writes=[o])
        P.op('dve', lambda e: e.tensor_tensor(out=ob[:], in0=o[:], in1=gate[:, sl], op=ALU.mult), reads=[o, gate, gate.sub(tb)], writes=[ob])
        yield
        P.dma('sp', self.ocat_d[slot, :, sl], ob[:], reads=[ob], writes=[self.ocat_d])
        yield

    def mixer_A(self, l):
        P = self.P
        psum = self.psum
        xT = self.xT
        self._pn = 0
        ident = self.c('ident')
        ones = P.sb([128, 128], BF16, 'onesA')
        P.op('pool', lambda e: e.memset(ones[:], 1.0), writes=[ones])
        gn = P.sb([64, 1], F32, 'gnA')
        P.dma('sp', gn[:], self.gdn_g_d[l, :].rearrange('(p o) -> p o', o=1), writes=[gn])
        SC = {}
        for nm in ('beta', 'e', 'ed', 'egl'):
            SC[nm] = P.sb([64, 32, 8], F32, 'sc_' + nm)
        be = P.sb([64, 32, 4], F32, 'sc_be')
        with P.scope():
            WA = P.sb([128, 8, 1032], BF16, 'wA')
            P.dma('pool', WA[:], self.w_in_d[l, :, 0:1032].rearrange('(k p) n -> p k n', p=128), writes=[WA])
            cw = P.sb([128, 6, 4], F32, 'cwA')
            for j in range(4):
                for b_ in range(6):
                    P.dma('sp', cw[:, b_, j:j + 1], self.gdn_conv_d[l, j, 128 * b_:128 * b_ + 128].rearrange('(c o) -> c o', o=1), reads=[cw], writes=[cw])
            e8 = P.sb([32, S], F32, 'e8')
            self._A_rows(l, WA, e8, SC, be)
            self._A_rest(l, WA, e8, SC, be, ones, gn, cw)
        self._A_phase2()
        self._A_phase3(SC, ones, gn)

    def _A_rows(self, l, WA, e8, SC, be):
        P = self.P
        psum = self.psum
        ident = self.c('ident')
        with P.scope():
            self._A_rows_inner(l, WA, e8, SC, be)

    def _A_rows_inner(self, l, WA, e8, SC, be):
        P = self.P
        psum = self.psum
        ident = self.c('ident')
        rm8 = P.sb([32, S], F32, 'rm8')
        P.dma('sp', rm8[:], self.rmask_d[1:2, :].partition_broadcast(32), writes=[rm8])
        dtb = P.sb([32, 1], F32, 'dtb'); nA = P.sb([32, 1], F32, 'nA')
        P.op('pool', lambda e: e.memset(dtb[:], 0.0), writes=[dtb])
        P.op('pool', lambda e: e.memset(nA[:], 0.0), writes=[nA])
        P.dma('sp', dtb[0:4, :], self.gdn_dtb_d[l, :].rearrange('(p o) -> p o', o=1), reads=[dtb], writes=[dtb])
        P.dma('sp', nA[0:4, :], self.gdn_alog_d[l, :].rearrange('(p o) -> p o', o=1), reads=[nA], writes=[nA])
        P.op('act', lambda e: e.activation(out=nA[:], in_=nA[:], func=AF.Exp), reads=[nA], writes=[nA])
        P.op('dve', lambda e: e.tensor_scalar(out=nA[:], in0=nA[:], scalar1=-1.0, scalar2=None, op0=ALU.mult), reads=[nA], writes=[nA])
        ab = P.sb([32, S], F32, 'abA'); beta8 = P.sb([32, S], F32, 'beta8'); gc8 = P.sb([32, S], F32, 'gc8')
        self.proj_fm(WA, 768, 32, lambda ps, tb: P.op('act', lambda e: e.copy(out=ab[:, 512 * tb:512 * tb + 512], in_=ps[0:32, :]), reads=[ps], writes=[ab]))
        P.op('act', lambda e: e.activation(out=beta8[:], in_=ab[:], func=AF.Sigmoid), reads=[ab], writes=[beta8])
        P.op('act', lambda e: e.activation(out=ab[:], in_=ab[:], func=AF.Exp, bias=dtb[:, 0:1]), reads=[ab, dtb], writes=[ab])
        P.op('act', lambda e: e.activation(out=ab[:], in_=ab[:], func=AF.Ln, bias=1.0), reads=[ab], writes=[ab])
        P.op('dve', lambda e: e.tensor_scalar(out=ab[:], in0=ab[:], scalar1=nA[:, 0:1], scalar2=None, op0=ALU.mult), reads=[ab, nA], writes=[ab])
        P.op('dve', lambda e: e.tensor_tensor_scan(out=gc8[:], data0=rm8[:], data1=ab[:], initial=0.0, op0=ALU.mult, op1=ALU.add), reads=[rm8, ab], writes=[gc8])
        P.dma('sp', self.rows8d[0], gc8[0:8, :], reads=[gc8], writes=[self.rows8d])
        P.dma('sp', self.rows8d[1], beta8[0:8, :], reads=[beta8], writes=[self.rows8d])
        ed8 = P.sb([32, S], F32, 'ed8'); egl8 = P.sb([32, S], F32, 'egl8')
        gc3 = gc8[:].rearrange('p (n c) -> p n c', c=64)
        P.op('act', lambda e: e.activation(out=e8[:], in_=gc8[:], func=AF.Exp), reads=[gc8], writes=[e8])
        P.op('dve', lambda e: e.tensor_tensor(out=ed8[:].rearrange('p (n c) -> p n c', c=64), in0=gc3[:, :, 63:64].to_broadcast([32, 32, 64]), in1=gc3, op=ALU.subtract),
             reads=[gc8], writes=[ed8])
        P.op('act', lambda e: e.activation(out=ed8[:], in_=ed8[:], func=AF.Exp), reads=[ed8], writes=[ed8])
        P.op('dve', lambda e: e.tensor_copy(out=egl8[:].rearrange('p (n c) -> p n c', c=64), in_=gc3[:, :, 63:64].to_broadcast([32, 32, 64])), reads=[gc8], writes=[egl8])
        P.op('act', lambda e: e.activation(out=egl8[:], in_=egl8[:], func=AF.Exp), reads=[egl8], writes=[egl8])
        for qi_, (nm, src) in enumerate((('beta', beta8), ('e', e8), ('ed', ed8), ('egl', egl8))):
            t = SC[nm]
            for half in range(2):
                ps = psum[2 + half]
                for c in range(16):
                    n = 16 * half + c
                    P.op('pe', lambda e, ps=ps, c=c, n=n, src=src: e.transpose(ps[0:64, 32 * c:32 * c + 32], src[:, 64 * n:64 * n + 64], ident[0:32, 0:32]),
                         reads=[src, self.cst], writes=[ps])
                P.op('act', lambda e, ps=ps, t=t, half=half: e.copy(out=t[:, 16 * half:16 * half + 16, :], in_=ps[0:64, :].rearrange('p (n r) -> p n r', r=32)[:, :, 0:8]),
                     reads=[ps], writes=[t])
        P.op('dve', lambda e: e.tensor_tensor(out=be[:], in0=SC['beta'][:, :, 4:8], in1=SC['e'][:, :, 0:4], op=ALU.mult), reads=[SC['beta'], SC['e']], writes=[be])

    def _A_rest(self, l, WA, e8, SC, be, ones, gn, cw):
        P = self.P
        self.dump('e8', e8, e8[:], [32, S])
        for nm in SC:
            self.dump('sc_' + nm, SC[nm], SC[nm][:], [64, 32, 8])
        self.dump('be', be, be[:], [64, 32, 4])
        psum = self.psum
        xT = self.xT
        ident = self.c('ident')
        bones = P.sb([128, 128], BF16, 'bonesA')
        P.op('pool', lambda e: e.memset(bones[:], 0.0), writes=[bones])
        P.op('pool', lambda e: e.memset(bones[0:64, 0:64], 1.0), reads=[bones], writes=[bones])
        P.op('pool', lambda e: e.memset(bones[64:128, 64:128], 1.0), reads=[bones], writes=[bones])
        for pr in range(2):
          with P.scope():
            xs = [P.sb([128, S], F32, 'xA') for _ in range(3)]
            ys = [P.sb([128, S], F32, 'yA') for _ in range(3)]
            gs = P.sb([128, S], F32, 'gsA')
            for i in range(3):
                self.proj_fm(WA, 256 * i + 128 * pr, 128, lambda ps, tb, i=i: P.op('act', lambda e: e.copy(out=xs[i][:, 512 * tb:512 * tb + 512], in_=ps[:, :]),
                                                                                  reads=[ps], writes=[xs[i]]))
            self.proj_fm(WA, 776 + 128 * pr, 128, lambda ps, tb: P.op('act', lambda e: e.activation(out=gs[:, 512 * tb:512 * tb + 512], in_=ps[:, :], func=AF.Silu),
                                                                      reads=[ps], writes=[gs]))
            P.dma('sp', self.gsd[2 * pr:2 * pr + 2].rearrange('h p s -> (h p) s'), gs[:], reads=[gs], writes=[self.gsd])
            for i in range(3):
                x = xs[i]; y = ys[i]; blk = 2 * i + pr
                P.op('dve', lambda e, x=x, y=y, blk=blk: e.tensor_scalar(out=y[:], in0=x[:], scalar1=cw[:, blk, 3:4], scalar2=None, op0=ALU.mult), reads=[x, cw], writes=[y])
                for sft in (1, 2, 3):
                    P.op('dve', lambda e, x=x, y=y, blk=blk, sft=sft: e.scalar_tensor_tensor(
                        out=y[:, sft:S], in0=x[:, 0:S - sft], scalar=cw[:, blk, 3 - sft:4 - sft], in1=y[:, sft:S], op0=ALU.mult, op1=ALU.add),
                        reads=[x, y, cw], writes=[y])
                P.op('act', lambda e, y=y: e.activation(out=y[:], in_=y[:], func=AF.Silu), reads=[y], writes=[y])
            sqs = [P.sb([128, 512], BF16, 'sqA') for _ in range(2)]; rss = [P.sb([128, 512], F32, 'rsA') for _ in range(2)]

            def norm_gen(it):
                i, tb = it
                y = ys[i]
                sq = sqs[tb % 2]; rs = rss[tb % 2]
                sl = slice(512 * tb, 512 * tb + 512)
                P.op('act', lambda e: e.activation(out=sq[:], in_=y[:, sl], func=AF.Square), reads=[y], writes=[sq])
                yield
                ps = psum[tb % 2]
                P.op('pe', lambda e: e.matmul(ps[:, :], lhsT=bones[:], rhs=sq[:], start=True, stop=True), reads=[bones, sq], writes=[ps])
                yield
                P.op('act', lambda e: e.activation(out=rs[:], in_=ps[:, :], func=AF.Ln, bias=EPS), reads=[ps], writes=[rs])
                P.op('act', lambda e: e.activation(out=rs[:], in_=rs[:], func=AF.Exp, scale=-0.5), reads=[rs], writes=[rs])
                yield
                P.op('dve', lambda e: e.scalar_tensor_tensor(out=y[:, sl], in0=y[:, sl], scalar=(0.125 if i == 0 else 1.0), in1=rs[:],
                                                             op0=ALU.mult, op1=ALU.mult), reads=[y, rs], writes=[y])
                yield

            pipeline(norm_gen, [(i, tb) for i in range(2) for tb in range(4)], width=2, skew=2)
            qn, kn, vv = ys
            sel = P.sb([32, 128], F32, 'selA')
            P.op('pool', lambda e: e.memset(sel[:], 0.0), writes=[sel])
            for hh in range(2):
                P.op('pool', lambda e, hh=hh, pr=pr: e.affine_select(out=sel[:, 64 * hh:64 * hh + 64], in_=ones_f(P, self)[:], pattern=[[0, 64]], compare_op=ALU.is_equal, fill=0.0,
                                                              base=-(2 * pr + hh), channel_multiplier=1), reads=[sel], writes=[sel])
            qe = xs[0]
            for tb in range(4):
                sl = slice(512 * tb, 512 * tb + 512)
                ps = psum[tb % 2]
                P.op('pe', lambda e, ps=ps, sl=sl: e.matmul(ps[:, :], lhsT=sel[:], rhs=e8[:, sl], start=True, stop=True), reads=[sel, e8], writes=[ps])
                P.op('dve', lambda e, ps=ps, sl=sl: e.tensor_tensor(out=qe[:, sl], in0=qn[:, sl], in1=ps[:, :], op=ALU.mult), reads=[qn, ps], writes=[qe])
            P.dma('sp', self.qed[2 * pr:2 * pr + 2].rearrange('h p s -> (h p) s'), qe[:], reads=[qe], writes=[self.qed])
            stg = [P.sb([64, 32, 64], F32, 'stgA') for _ in range(2)]
            tk = [P.sb([64, 32, 64], F32, 'tkA') for _ in range(3)]
            for hh in range(2):
                h = 2 * pr + hh
                b0 = 64 * hh
                for gi, (lh, dst) in enumerate(((kn, self.kkd), (qn, self.qkd))):
                    st = stg[gi]
                    for g8 in range(4):
                        ps = psum[2 + (g8 % 2)]
                        for c in range(8):
                            n = 8 * g8 + c
                            csl = slice(64 * n, 64 * n + 64)
                            P.op('pe', lambda e, ps=ps, c=c, csl=csl, lh=lh, b0=b0: e.matmul(ps[0:64, 64 * c:64 * c + 64], lhsT=lh[b0:b0 + 64, csl], rhs=kn[b0:b0 + 64, csl],
                                                                                          start=True, stop=True), reads=[lh, kn], writes=[ps])
                        P.op('act', lambda e, ps=ps, st=st, g8=g8: e.copy(out=st[:, 8 * g8:8 * g8 + 8, :].rearrange('p n j -> p (n j)'), in_=ps[0:64, :]), reads=[ps], writes=[st])
                    P.dma('sp', dst[32 * h:32 * h + 32].rearrange('n i j -> i n j'), st[:], reads=[st], writes=[dst])
                for g8 in range(4):
                    for si, src in enumerate((kn, vv)):
                        ps = psum[4 + si + 2 * (g8 % 2)]
                        for c in range(8):
                            n = 8 * g8 + c
                            P.op('pe', lambda e, ps=ps, c=c, n=n, src=src, b0=b0: e.transpose(ps[0:64, 64 * c:64 * c + 64], src[b0:b0 + 64, 64 * n:64 * n + 64],
                                                                                           ident[b0:b0 + 64, b0:b0 + 64]), reads=[src, self.cst], writes=[ps])
                        p3 = ps[0:64, :].rearrange('p (n d) -> p n d', d=64)
                        nsl = slice(8 * g8, 8 * g8 + 8)
                        if si == 0:
                            P.op('dve', lambda e, p3=p3, nsl=nsl, h=h: e.tensor_tensor(out=tk[0][:, nsl, :], in0=p3, in1=be[:, nsl, h:h + 1].to_broadcast([64, 8, 64]), op=ALU.mult),
                                 reads=[ps, be], writes=[tk[0]])
                            P.op('dve', lambda e, p3=p3, nsl=nsl, h=h: e.tensor_tensor(out=tk[1][:, nsl, :], in0=p3, in1=SC['ed'][:, nsl, h:h + 1].to_broadcast([64, 8, 64]), op=ALU.mult),
                                 reads=[ps, SC['ed']], writes=[tk[1]])
                        else:
                            P.op('dve', lambda e, p3=p3, nsl=nsl, h=h: e.tensor_tensor(out=tk[2][:, nsl, :], in0=p3, in1=SC['beta'][:, nsl, 4 + h:5 + h].to_broadcast([64, 8, 64]), op=ALU.mult),
                                 reads=[ps, SC['beta']], writes=[tk[2]])
                for i in range(3):
                    P.dma('sp', self.tokd[h, i], tk[i][:], reads=[tk[i]], writes=[self.tokd])

    def _A_phase2(self):
        P = self.P
        with P.scope():
            KKs = P.sb([128, 64, 64], F32, 'KKs'); QKs = P.sb([128, 64, 64], F32, 'QKs')
            Dm = P.sb([128, 64, 64], F32, 'Dms'); X = P.sb([128, 64, 64], F32, 'Xs'); tmp = P.sb([128, 64, 64], F32, 'tmps')
            gcs = P.sb([128, 64], F32, 'gcs'); bts = P.sb([128, 64], F32, 'bts')
            P.dma('sp', KKs[:], self.kkd[:], reads=[self.kkd], writes=[KKs])
            P.dma('sp', QKs[:], self.qkd[:], reads=[self.qkd], writes=[QKs])
            P.dma('sp', gcs[:], self.rows8d[0, 0:4, :].rearrange('h (n c) -> (h n) c', c=64), reads=[self.rows8d], writes=[gcs])
            P.dma('sp', bts[:], self.rows8d[1, 4:8, :].rearrange('h (n c) -> (h n) c', c=64), reads=[self.rows8d], writes=[bts])
            P.op('dve', lambda e: e.tensor_tensor(out=Dm[:], in0=gcs[:].unsqueeze(2).to_broadcast([128, 64, 64]), in1=gcs[:].unsqueeze(1).to_broadcast([128, 64, 64]),
                                                  op=ALU.subtract), reads=[gcs], writes=[Dm])
            P.op('dve', lambda e: e.tensor_scalar(out=Dm[:], in0=Dm[:], scalar1=0.0, scalar2=None, op0=ALU.min), reads=[Dm], writes=[Dm])
            P.op('act', lambda e: e.activation(out=Dm[:], in_=Dm[:], func=AF.Exp), reads=[Dm], writes=[Dm])
            P.op('dve', lambda e: e.tensor_tensor(out=tmp[:].rearrange('p j i -> p i j'), in0=QKs[:], in1=Dm[:], op=ALU.mult), reads=[QKs, Dm], writes=[tmp])
            P.op('pool', lambda e: e.affine_select(out=tmp[:], in_=tmp[:], pattern=[[-1, 64], [1, 64]], compare_op=ALU.is_ge, fill=0.0, base=0, channel_multiplier=0),
                 reads=[tmp], writes=[tmp])
            P.dma('sp', self.qktd[:], tmp[:], reads=[tmp], writes=[self.qktd])
            P.op('dve', lambda e: e.tensor_tensor(out=KKs[:], in0=KKs[:], in1=Dm[:], op=ALU.mult), reads=[KKs, Dm], writes=[KKs])
            P.op('dve', lambda e: e.tensor_tensor(out=KKs[:], in0=KKs[:], in1=bts[:].unsqueeze(2).to_broadcast([128, 64, 64]), op=ALU.mult), reads=[KKs, bts], writes=[KKs])
            P.op('pool', lambda e: e.affine_select(out=KKs[:], in_=KKs[:], pattern=[[1, 64], [-1, 64]], compare_op=ALU.is_gt, fill=0.0, base=0, channel_multiplier=0),
                 reads=[KKs], writes=[KKs])
            P.op('pool', lambda e: e.memset(X[:], 1.0), writes=[X])
            P.op('pool', lambda e: e.affine_select(out=X[:], in_=X[:], pattern=[[1, 64], [-1, 64]], compare_op=ALU.is_equal, fill=0.0, base=0, channel_multiplier=0),
                 reads=[X], writes=[X])
            tmp2 = QKs
            for b in range(1, 64):
                P.op('dve', lambda e, b=b: e.tensor_tensor(out=tmp2[:, 0:b, 0:b], in0=X[:, 0:b, 0:b], in1=KKs[:, b:b + 1, 0:b].to_broadcast([128, b, b]), op=ALU.mult),
                     reads=[X, KKs, tmp], writes=[tmp2])
                P.op('dve', lambda e, b=b: e.tensor_reduce(out=X[:, 0:b, b:b + 1], in_=tmp2[:, 0:b, 0:b], axis=AX.X, op=ALU.add, negate=True), reads=[tmp2], writes=[X])
            P.dma('sp', self.ttd[:], X[:], reads=[X], writes=[self.ttd])

    def _A_phase3(self, SC, ones, gn):
        P = self.P
        psum = self.psum
        ident = self.c('ident')
        Sall = P.sb([64, 4, 33, 64], F32, 'SallA')
        for pair in range(2):
          with P.scope():
            ATa = P.sb([64, 2, 32, 64], F32, 'ATA'); Bna = P.sb([64, 2, 32, 64], F32, 'BnA')
            for hh in range(2):
                with P.scope():
                    self._A3a_head(2 * pair + hh, hh, SC, ATa, Bna)
            for hh in range(2):
                h = 2 * pair + hh
                P.op('pool', lambda e, h=h: e.memset(Sall[:, h, 0, :], 0.0), writes=[Sall.sub((h, 0))])
            for n in range(32):
                for hh in range(2):
                    h = 2 * pair + hh
                    ps = psum[4 * (n % 2) + hh]
                    P.op('pe', lambda e, ps=ps, n=n, h=h, hh=hh: e.matmul(ps[0:64, 0:64], lhsT=ATa[:, hh, n, :], rhs=Sall[:, h, n, :], start=True, stop=True),
                         reads=[ATa.sub(hh), Sall.sub((h, n))], writes=[ps])
                    P.op('dve', lambda e, ps=ps, n=n, h=h, hh=hh: e.tensor_tensor(out=Sall[:, h, n + 1, :], in0=Bna[:, hh, n, :], in1=ps[0:64, 0:64], op=ALU.add),
                         reads=[Bna.sub(hh), ps], writes=[Sall.sub((h, n + 1))])
        for h in range(4):
            with P.scope():
                self._A3c_head(h, Sall, ones, gn)

    def _A3a_head(self, h, hh, SC, ATa, Bna):
        P = self.P
        psum = self.psum
        ident = self.c('ident')
        TT = P.sb([64, 32, 64], F32, 'TTA'); QKT = P.sb([64, 32, 64], F32, 'QKTA')
        tk = [P.sb([64, 32, 64], F32, 'tk3A') for _ in range(3)]
        qe = P.sb([64, S], F32, 'qe3A')
        P.dma('sp', TT[:], self.ttd[32 * h:32 * h + 32].rearrange('n a b -> a n b'), reads=[self.ttd], writes=[TT])
        P.dma('sp', QKT[:], self.qktd[32 * h:32 * h + 32].rearrange('n a b -> a n b'), reads=[self.qktd], writes=[QKT])
        for i in range(3):
            P.dma('sp', tk[i][:], self.tokd[h, i], reads=[self.tokd], writes=[tk[i]])
        P.dma('sp', qe[:], self.qed[h], reads=[self.qed], writes=[qe])
        kbe, kd, vb = tk
        UW = P.sb([64, 32, 128], F32, 'UWA')
        for g4 in range(8):
            ps = psum[g4 % 2]
            for c in range(4):
                n = 4 * g4 + c
                P.op('pe', lambda e, ps=ps, c=c, n=n: e.matmul(ps[0:64, 128 * c:128 * c + 64], lhsT=TT[:, n, :], rhs=vb[:, n, :], start=True, stop=True),
                     reads=[TT, vb], writes=[ps])
                P.op('pe', lambda e, ps=ps, c=c, n=n: e.matmul(ps[0:64, 128 * c + 64:128 * c + 128], lhsT=TT[:, n, :], rhs=kbe[:, n, :], start=True, stop=True),
                     reads=[TT, kbe], writes=[ps])
            P.op('act', lambda e, ps=ps, g4=g4: e.copy(out=UW[:, 4 * g4:4 * g4 + 4, :].rearrange('p n d -> p (n d)'), in_=ps[0:64, :]), reads=[ps], writes=[UW])
        QhT = P.sb([64, S], F32, 'QhTA'); OhT = P.sb([64, S], F32, 'OhTA')
        id64 = P.sb([64, 64], F32, 'id64A')
        P.op('pool', lambda e: e.tensor_copy(out=id64[:], in_=ident[0:64, 0:64]), reads=[self.cst], writes=[id64])
        for g8 in range(4):
            sl = slice(512 * g8, 512 * g8 + 512)
            ps = psum[2]
            for c in range(8):
                n = 8 * g8 + c
                P.op('pe', lambda e, ps=ps, c=c, n=n: e.matmul(ps[0:64, 64 * c:64 * c + 64], lhsT=UW[:, n, 64:128], rhs=QKT[:, n, :], start=True, stop=True),
                     reads=[UW, QKT], writes=[ps])
            P.op('dve', lambda e, ps=ps, sl=sl: e.tensor_tensor(out=QhT[:, sl], in0=qe[:, sl], in1=ps[0:64, :], op=ALU.subtract), reads=[qe, ps], writes=[QhT])
            ps = psum[3]
            for c in range(8):
                n = 8 * g8 + c
                P.op('pe', lambda e, ps=ps, c=c, n=n: e.matmul(ps[0:64, 64 * c:64 * c + 64], lhsT=UW[:, n, 0:64], rhs=QKT[:, n, :], start=True, stop=True),
                     reads=[UW, QKT], writes=[ps])
            P.op('act', lambda e, ps=ps, sl=sl: e.copy(out=OhT[:, sl], in_=ps[0:64, :]), reads=[ps], writes=[OhT])
            ps = psum[4 + g8 % 2]
            for c in range(8):
                n = 8 * g8 + c
                P.op('pe', lambda e, ps=ps, c=c, n=n: e.matmul(ps[0:64, 64 * c:64 * c + 64], lhsT=UW[:, n, 64:128], rhs=kd[:, n, :], start=True, stop=True),
                     reads=[UW, kd], writes=[ps])
            for c in range(8):
                n = 8 * g8 + c
                P.op('dve', lambda e, ps=ps, c=c, n=n: e.scalar_tensor_tensor(out=ATa[:, hh, n, :], in0=id64[:], scalar=SC['egl'][:, n, h:h + 1], in1=ps[0:64, 64 * c:64 * c + 64],
                                                                         op0=ALU.mult, op1=ALU.subtract), reads=[id64, SC['egl'], ps], writes=[ATa.sub(hh)])
            ps = psum[6 + g8 % 2]
            for c in range(8):
                n = 8 * g8 + c
                P.op('pe', lambda e, ps=ps, c=c, n=n: e.matmul(ps[0:64, 64 * c:64 * c + 64], lhsT=kd[:, n, :], rhs=UW[:, n, 0:64], start=True, stop=True),
                     reads=[UW, kd], writes=[ps])
            P.op('act', lambda e, ps=ps, g8=g8: e.copy(out=Bna[:, hh, 8 * g8:8 * g8 + 8, :].rearrange('p n d -> p (n d)'), in_=ps[0:64, :]), reads=[ps], writes=[Bna.sub(hh)])
        P.dma('sp', self.qhd[h], QhT[:], reads=[QhT], writes=[self.qhd])
        P.dma('sp', self.ohd[h], OhT[:], reads=[OhT], writes=[self.ohd])

    def _A3c_head(self, h, Sall, ones, gn):
        P = self.P
        psum = self.psum
        QhT = P.sb([64, S], F32, 'QhTc'); OhT = P.sb([64, S], F32, 'OhTc'); gs = P.sb([64, S], F32, 'gs3A')
        P.dma('sp', QhT[:], self.qhd[h], reads=[self.qhd], writes=[QhT])
        P.dma('sp', OhT[:], self.ohd[h], reads=[self.ohd], writes=[OhT])
        P.dma('sp', gs[:], self.gsd[h], reads=[self.gsd], writes=[gs])
        oi = [P.sb([64, 512], F32, 'oiA') for _ in range(2)]
        sqs = [P.sb([64, 512], BF16, 'sq3A') for _ in range(2)]; rss = [P.sb([64, 512], F32, 'rs3A') for _ in range(2)]
        ob = [P.sb([64, 512], BF16, 'obA') for _ in range(2)]
        def out_gen(tb):
            sl = slice(512 * tb, 512 * tb + 512)
            ps = psum[4 + tb % 2]
            for c in range(8):
                n = 8 * tb + c
                P.op('pe', lambda e, c=c, n=n: e.matmul(ps[0:64, 64 * c:64 * c + 64], lhsT=Sall[:, h, n, :], rhs=QhT[:, 64 * n:64 * n + 64], start=True, stop=True),
                     reads=[Sall.sub((h, n)), QhT], writes=[ps])
            yield
            o = oi[tb % 2]
            P.op('dve', lambda e: e.tensor_tensor(out=o[:], in0=OhT[:, sl], in1=ps[0:64, :], op=ALU.add), reads=[ps, OhT], writes=[o])
            yield
            yield from self.rms_gate_out_gen(o, gs, gn[:, 0:1], ones, h, tb, sqs[tb % 2], rss[tb % 2], ob[tb % 2])

        pipeline(out_gen, range(4), width=2, skew=2)

    def mixer_C(self, l):
        P = self.P
        psum = self.psum
        xT = self.xT
        self._pn = 0
        WC = P.sb([128, 8, 1024], BF16, 'wC')
        P.dma('pool', WC[:], self.w_in_d[l, :, 1384:2408].rearrange('(k p) n -> p k n', p=128), writes=[WC])
        ones = P.sb([128, 128], BF16, 'onesC')
        P.op('pool', lambda e: e.memset(ones[:], 1.0), writes=[ones])
        zer = P.sb([64, 16], BF16, 'zerC')
        P.op('pool', lambda e: e.memset(zer[:], 0.0), writes=[zer])
        rmask = P.sb([64, S], F32, 'rmaskC')
        P.dma('sp', rmask[:], self.rmask_d[0:1, :].partition_broadcast(64), writes=[rmask])
        gn = P.sb([64, 1], F32, 'gnC')
        P.dma('sp', gn[:], self.hgrn_g_d[l, :].rearrange('(p o) -> p o', o=1), writes=[gn])
        lb = P.sb([64, 4], F32, 'lbC'); oml = P.sb([64, 4], F32, 'omlC'); noml = P.sb([64, 4], F32, 'nomlC')
        if l == 0:
            P.op('pool', lambda e: e.memset(lb[:], 0.0), writes=[lb])
        else:
            z = P.sb([64, 2, 4], F32, 'zC')
            for li in range(2):
                for hh in range(4):
                    P.dma('sp', z[:, li, hh:hh + 1], self.hgrn_lb_d[li, 64 * hh:64 * hh + 64].rearrange('(c o) -> c o', o=1), reads=[z], writes=[z])
            P.op('dve', lambda e: e.tensor_tensor(out=lb[:], in0=z[:, 1, :], in1=z[:, 0, :], op=ALU.subtract), reads=[z], writes=[lb])
            P.op('act', lambda e: e.activation(out=lb[:], in_=lb[:], func=AF.Sigmoid), reads=[lb], writes=[lb])
        P.op('dve', lambda e: e.tensor_scalar(out=oml[:], in0=lb[:], scalar1=-1.0, scalar2=1.0, op0=ALU.mult, op1=ALU.add), reads=[lb], writes=[oml])
        P.op('dve', lambda e: e.tensor_scalar(out=noml[:], in0=oml[:], scalar1=-1.0, scalar2=None, op0=ALU.mult), reads=[oml], writes=[noml])
        m16 = P.sb([128, 128], F32, 'm16')
        P.op('pool', lambda e: e.tensor_copy(out=m16[:], in_=self.c('mask16')), reads=[self.cst], writes=[m16])
        cm8 = P.sb([128, 8], BF16, 'cm8')
        P.op('pool', lambda e: e.tensor_copy(out=cm8[:], in_=self.c('cm8')), reads=[self.cst], writes=[cm8])
        Vtok = P.sb([128, 16, 256], BF16, 'VtokC')
        for tt in range(16):
            ps = psum[self._pn % 2]; self._pn += 1
            for kc in range(8):
                P.op('pe', lambda e, ps=ps, kc=kc, tt=tt: e.matmul(ps[:, 0:256], lhsT=xT[:, kc, 128 * tt:128 * tt + 128], rhs=WC[:, kc, 512:768],
                                                                   start=(kc == 0), stop=(kc == 7)), reads=[WC, xT.sub(tt // 4)], writes=[ps])
            P.op('act', lambda e, ps=ps, tt=tt: e.copy(out=Vtok[:, tt, :], in_=ps[:, 0:256]), reads=[ps], writes=[Vtok])
        gs = P.sb([64, S], F32, 'gsC')
        qt = P.sb([64, S], BF16, 'qtC'); kt = P.sb([64, S], BF16, 'ktC')
        adec = P.sb([64, 128], F32, 'adecC')
        Bst = P.sb([64, 64, 128], F32, 'BstC')
        kdt = [P.sb([128, 64], BF16, 'kdtC') for _ in range(2)]
        vex = [P.sb([128, 8, 64], BF16, 'vexC') for _ in range(2)]
        sm = [P.sb([128, 4, 128], BF16, 'smC') for _ in range(2)]
        oi = [P.sb([64, 512], F32, 'oiC') for _ in range(2)]
        sqs = [P.sb([64, 512], BF16, 'sqC') for _ in range(2)]; rss = [P.sb([64, 512], F32, 'rsC') for _ in range(2)]
        ob = [P.sb([64, 512], BF16, 'obC') for _ in range(2)]
        ident = self.c('ident')
        for h in range(4):
          with P.scope():
            qT = P.sb([64, S], F32, 'qTC'); sg = P.sb([64, S], F32, 'sgC')
            kk = P.sb([64, S], F32, 'kkC'); cum = P.sb([64, S], F32, 'cumC'); ex = P.sb([64, S], F32, 'exC')
            kd = P.sb([64, S], F32, 'kdC')
            self.proj_fm(WC, 64 * h, 64, lambda ps, tb: P.op('act', lambda e: e.copy(out=qT[:, 512 * tb:512 * tb + 512], in_=ps[0:64, :]),
                                                             reads=[ps], writes=[qT.sub(tb)]))
            self.proj_fm(WC, 256 + 64 * h, 64, lambda ps, tb: P.op('act', lambda e: e.activation(out=sg[:, 512 * tb:512 * tb + 512], in_=ps[0:64, :], func=AF.Sigmoid),
                                                                   reads=[ps], writes=[sg]))
            self.proj_fm(WC, 768 + 64 * h, 64, lambda ps, tb: P.op('act', lambda e: e.activation(out=gs[:, 512 * tb:512 * tb + 512], in_=ps[0:64, :], func=AF.Silu),
                                                                   reads=[ps], writes=[gs.sub(tb)]))
            P.op('dve', lambda e, h=h: e.tensor_scalar(out=kk[:], in0=sg[:], scalar1=noml[:, h:h + 1], scalar2=oml[:, h:h + 1], op0=ALU.mult, op1=ALU.add),
                 reads=[sg, noml, oml], writes=[kk])
            P.op('dve', lambda e, h=h: e.tensor_scalar(out=sg[:], in0=sg[:], scalar1=oml[:, h:h + 1], scalar2=lb[:, h:h + 1], op0=ALU.mult, op1=ALU.add),
                 reads=[sg, oml, lb], writes=[sg])
            P.op('act', lambda e: e.activation(out=sg[:], in_=sg[:], func=AF.Ln), reads=[sg], writes=[sg])
            P.op('dve', lambda e: e.tensor_tensor_scan(out=cum[:], data0=rmask[:], data1=sg[:], initial=0.0, op0=ALU.mult, op1=ALU.add),
                 reads=[rmask, sg], writes=[cum])
            P.op('act', lambda e: e.activation(out=ex[:], in_=cum[:], func=AF.Exp), reads=[cum], writes=[ex])
            P.op('dve', lambda e: e.tensor_tensor(out=qt[:], in0=qT[:], in1=ex[:], op=ALU.mult), reads=[qT.sub(0), qT.sub(1), qT.sub(2), qT.sub(3), ex], writes=[qt])
            P.op('act', lambda e: e.activation(out=ex[:], in_=cum[:], func=AF.Exp, scale=-1.0), reads=[cum], writes=[ex])
            P.op('dve', lambda e: e.tensor_tensor(out=kt[:], in0=kk[:], in1=ex[:], op=ALU.mult), reads=[kk, ex], writes=[kt])
            cum3 = cum[:].rearrange('p (c j) -> p c j', j=16)
            cl = cum3[:, :, 15:16]
            P.op('dve', lambda e, cl=cl, cum3=cum3: e.tensor_tensor(out=ex[:].rearrange('p (c j) -> p c j', j=16), in0=cl.to_broadcast([64, 128, 16]), in1=cum3, op=ALU.subtract),
                 reads=[cum], writes=[ex])
            P.op('act', lambda e: e.activation(out=ex[:], in_=ex[:], func=AF.Exp), reads=[ex], writes=[ex])
            P.op('dve', lambda e: e.tensor_tensor(out=kd[:], in0=kk[:], in1=ex[:], op=ALU.mult), reads=[kk, ex], writes=[kd])
            P.op('act', lambda e, cl=cl: e.activation(out=adec[:].rearrange('p (c o) -> p c o', o=1), in_=cl, func=AF.Exp), reads=[cum], writes=[adec])
            P.op('pool', lambda e: e.memset(adec[:, 0:1], 0.0), reads=[adec], writes=[adec])
            def tile_gen(tt, h=h, kd=kd):
                tsl = slice(128 * tt, 128 * tt + 128)
                pt = psum[2 + (tt % 2)]
                kdtt = kdt[tt % 2]; vx = vex[tt % 2]
                P.op('pe', lambda e: e.transpose(pt[:, 0:64], kd[:, tsl], ident[0:64, 0:64]), reads=[kd, self.cst], writes=[pt])
                P.op('dve', lambda e: e.tensor_tensor(
                    out=vx[:], in0=Vtok[:, tt, 64 * h:64 * h + 64].unsqueeze(1).to_broadcast([128, 8, 64]),
                    in1=cm8[:].unsqueeze(2).to_broadcast([128, 8, 64]), op=ALU.mult), reads=[Vtok, cm8], writes=[vx])
                yield
                P.op('act', lambda e: e.copy(out=kdtt[:], in_=pt[:, 0:64]), reads=[pt], writes=[kdtt])
                yield
                pb = psum[4 + (tt % 2)]
                P.op('pe', lambda e: e.matmul(pb[0:64, :], lhsT=kdtt[:], rhs=vx[:].rearrange('p n v -> p (n v)'), start=True, stop=True),
                     reads=[kdtt, vx], writes=[pb])
                yield
                P.op('act', lambda e: e.copy(out=Bst[:, :, 8 * tt:8 * tt + 8], in_=pb[0:64, :].rearrange('p (n v) -> p v n', v=64)),
                     reads=[pb], writes=[Bst])
                yield

            pipeline(tile_gen, range(16), width=2, skew=2)
          with P.scope():
            Sst = P.sb([64, 64, 128], F32, 'SstC'); Sb = P.sb([64, 64, 128], BF16, 'SbC')
            for v in range(64):
                P.op('dve', lambda e, v=v: e.tensor_tensor_scan(out=Sst[:, v, :], data0=adec[:], data1=Bst[:, v, :], initial=0.0,
                                                                                         op0=ALU.mult, op1=ALU.add), reads=[adec, Bst], writes=[Sst.sub(v)])
            P.op('act', lambda e: e.copy(out=Sb[:], in_=Sst[:]), reads=[Sst.sub(v) for v in range(64)], writes=[Sb])
            def out_gen(tb, h=h, Sb=Sb):
                pst = psum[2 + (tb % 2)]
                smt = sm[tb % 2]
                for t4 in range(4):
                    tsl = slice(512 * tb + 128 * t4, 512 * tb + 128 * t4 + 128)
                    P.op('pe', lambda e, t4=t4, tsl=tsl: e.matmul(pst[:, 128 * t4:128 * t4 + 128], lhsT=kt[:, tsl], rhs=qt[:, tsl], start=True, stop=True),
                         reads=[kt, qt], writes=[pst])
                po = psum[4 + (tb % 2)]; po2 = psum[6 + (tb % 2)]
                for c in range(32):
                    n = 32 * tb + c
                    if n == 0:
                        P.op('pe', lambda e, c=c: e.matmul(po2[0:64, 16 * c:16 * c + 16], lhsT=Sb[:, :, 0], rhs=zer[:], start=True, stop=True),
                             reads=[Sb, zer], writes=[po2])
                    else:
                        P.op('pe', lambda e, c=c, n=n: e.matmul(po2[0:64, 16 * c:16 * c + 16], lhsT=Sb[:, :, n - 1], rhs=qt[:, 16 * n:16 * n + 16], start=True, stop=True),
                             reads=[Sb, qt], writes=[po2])
                yield
                P.op('dve', lambda e: e.tensor_tensor(out=smt[:], in0=pst[:].rearrange('p (a t) -> p a t', a=4),
                                                      in1=m16[:].unsqueeze(1).to_broadcast([128, 4, 128]), op=ALU.mult),
                     reads=[pst, m16], writes=[smt])
                yield
                for t4 in range(4):
                    tt = 4 * tb + t4
                    P.op('pe', lambda e, t4=t4, tt=tt: e.matmul(po[0:64, 128 * t4:128 * t4 + 128], lhsT=Vtok[:, tt, 64 * h:64 * h + 64], rhs=smt[:, t4, :],
                                                                start=True, stop=True), reads=[Vtok, smt], writes=[po])
                yield
                o = oi[tb % 2]
                P.op('act', lambda e: e.copy(out=o[:], in_=po[0:64, :]), reads=[po], writes=[o])
                yield
                P.op('dve', lambda e: e.tensor_tensor(out=o[:], in0=o[:], in1=po2[0:64, :], op=ALU.add), reads=[o, po2], writes=[o])
                yield
                yield from self.rms_gate_out_gen(o, gs, gn[:, 0:1], ones, 8 + h, tb, sqs[tb % 2], rss[tb % 2], ob[tb % 2])

            pipeline(out_gen, range(4), width=2, skew=3)

    def proj_fm(self, W, c0, M, evac, xsrc=None, nk=8):
        P = self.P
        for tb in range(4):
            ps = self.psum[self._pn % 2]; self._pn += 1
            for kc in range(nk):
                P.op('pe', lambda e, ps=ps, kc=kc, tb=tb: e.matmul(
                    ps[0:M, :], lhsT=W[:, kc, c0:c0 + M], rhs=self.xT[:, kc, 512 * tb:512 * tb + 512],
                    start=(kc == 0), stop=(kc == nk - 1)), reads=[W, self.xT.sub(tb)], writes=[ps])
            evac(ps, tb)

    def mixer_B(self, l):
        P = self.P
        psum = self.psum
        self._pn = 0
        SC = 96 ** -0.5
        WB = P.sb([128, 8, 352], BF16, 'wB')
        P.dma('pool', WB[:], self.w_in_d[l, :, 1032:1384].rearrange('(k p) n -> p k n', p=128), writes=[WB])
        WBr = P.sb([128, 8, 32], BF16, 'wBr')
        P.op('act', lambda e: e.mul(out=WBr[:, :, 0:16], in_=WB[:, :, 336:352], mul=-1.0), reads=[WB], writes=[WBr])
        P.op('act', lambda e: e.copy(out=WBr[:, :, 16:32], in_=WB[:, :, 320:336]), reads=[WB, WBr], writes=[WBr])
        wuqa = P.sb([128, 384], BF16, 'wuqa'); wuqb = P.sb([64, 384], BF16, 'wuqb')
        P.dma('pool', wuqa[:], self.mla_wuq_d[l, 0:128, :], writes=[wuqa])
        P.dma('pool', wuqb[:], self.mla_wuq_d[l, 128:192, :], writes=[wuqb])
        wra = P.sb([128, 4, 32], BF16, 'wra'); wrb = P.sb([64, 4, 32], BF16, 'wrb')
        for (src, dst) in ((wuqa, wra), (wuqb, wrb)):
            v = src[:].rearrange('p (h c) -> p h c', c=96)
            P.op('act', lambda e, v=v, dst=dst: e.mul(out=dst[:, :, 0:16], in_=v[:, :, 80:96], mul=-1.0), reads=[src], writes=[dst])
            P.op('act', lambda e, v=v, dst=dst: e.copy(out=dst[:, :, 16:32], in_=v[:, :, 64:80]), reads=[src, dst], writes=[dst])
        wukv = P.sb([128, 512], BF16, 'wukv')
        P.dma('pool', wukv[:], self.mla_wukv_d[l], writes=[wukv])
        wv = P.sb([128, 4, 64], BF16, 'wv')
        P.op('act', lambda e: e.copy(out=wv[:], in_=wukv[:].rearrange('p (h c) -> p h c', c=128)[:, :, 64:128]), reads=[wukv], writes=[wv])
        gqa = P.sb([128, 1], F32, 'gqa'); gqb = P.sb([64, 1], F32, 'gqb'); gkv = P.sb([128, 1], F32, 'gkv')
        P.dma('sp', gqa[:], self.mla_qg_d[l, 0:128].rearrange('(p o) -> p o', o=1), writes=[gqa])
        P.dma('sp', gqb[:], self.mla_qg_d[l, 128:192].rearrange('(p o) -> p o', o=1), writes=[gqb])
        P.dma('sp', gkv[:], self.mla_kvg_d[l, :].rearrange('(p o) -> p o', o=1), writes=[gkv])
        ones = P.sb([128, 128], BF16, 'onesB')
        P.op('pool', lambda e: e.memset(ones[:], 1.0), writes=[ones])
        cosT = P.sb([32, S], F32, 'cosT'); sinT = P.sb([32, S], F32, 'sinT')
        P.dma('sp', cosT[:], self.rope_d[0], writes=[cosT])
        P.dma('sp', sinT[:], self.rope_d[1], writes=[sinT])
        cqna = P.sb([128, S], BF16, 'cqna'); cqnb = P.sb([64, S], BF16, 'cqnb'); ckvn = P.sb([128, S], BF16, 'ckvn')
        KrT = P.sb([32, S], BF16, 'KrT')
        with P.scope():
            cqa = P.sb([128, S], F32, 'cqa'); cqb = P.sb([64, S], F32, 'cqb'); ckv = P.sb([128, S], F32, 'ckv')
            sqa = P.sb([128, S], BF16, 'sqa'); sqb = P.sb([64, S], BF16, 'sqb'); sqk = P.sb([128, S], BF16, 'sqk')
            for (dst, sq, c0, M) in ((cqa, sqa, 0, 128), (cqb, sqb, 128, 64), (ckv, sqk, 192, 128)):
                def evac(ps, tb, dst=dst, sq=sq, M=M):
                    P.op('act', lambda e: e.copy(out=dst[:, 512 * tb:512 * tb + 512], in_=ps[0:M, :]), reads=[ps], writes=[dst.sub(tb)])
                    P.op('act', lambda e: e.activation(out=sq[:, 512 * tb:512 * tb + 512], in_=ps[0:M, :], func=AF.Square), reads=[ps], writes=[sq.sub(tb)])
                self.proj_fm(WB, c0, M, evac)
            kx = P.sb([32, S], F32, 'kx')
            self.proj_fm(WB, 320, 32, lambda ps, tb: P.op('dve', lambda e: e.tensor_tensor(
                out=kx[:, 512 * tb:512 * tb + 512], in0=ps[0:32, :], in1=cosT[:, 512 * tb:512 * tb + 512], op=ALU.mult),
                reads=[ps, cosT], writes=[kx.sub(tb)]))
            kx2 = P.sb([32, S], F32, 'kx2')
            self.proj_fm(WBr, 0, 32, lambda ps, tb: P.op('dve', lambda e: e.tensor_tensor(
                out=kx2[:, 512 * tb:512 * tb + 512], in0=ps[0:32, :], in1=sinT[:, 512 * tb:512 * tb + 512], op=ALU.mult),
                reads=[ps, sinT], writes=[kx2.sub(tb)]))
            for tb in range(4):
                P.op('dve', lambda e, tb=tb: e.tensor_tensor(out=KrT[:, 512 * tb:512 * tb + 512], in0=kx[:, 512 * tb:512 * tb + 512],
                                                              in1=kx2[:, 512 * tb:512 * tb + 512], op=ALU.add),
                     reads=[kx.sub(tb), kx2.sub(tb)], writes=[KrT.sub(tb)])
            rq = [P.sb([128, 512], F32, 'rq') for _ in range(2)]
            for tb in range(4):
                sl = slice(512 * tb, 512 * tb + 512)
                ps = psum[self._pn % 2]; self._pn += 1
                P.op('pe', lambda e, ps=ps, sl=sl: e.matmul(ps[:], lhsT=ones[:], rhs=sqa[:, sl], start=True, stop=False), reads=[ones, sqa.sub(tb)], writes=[ps])
                P.op('pe', lambda e, ps=ps, sl=sl: e.matmul(ps[:], lhsT=ones[0:64, :], rhs=sqb[:, sl], start=False, stop=True), reads=[ones, sqb.sub(tb)], writes=[ps])
                r = rq[0]
                P.op('act', lambda e, ps=ps, r=r: e.activation(out=r[:], in_=ps[:], func=AF.Ln, scale=1.0 / 192, bias=EPS), reads=[ps], writes=[r])
                P.op('act', lambda e, r=r: e.activation(out=r[:], in_=r[:], func=AF.Exp, scale=-0.5), reads=[r], writes=[r])
                P.op('dve', lambda e, r=r, sl=sl: e.scalar_tensor_tensor(out=cqna[:, sl], in0=cqa[:, sl], scalar=gqa[:, 0:1], in1=r[:], op0=ALU.mult, op1=ALU.mult),
                     reads=[cqa.sub(tb), gqa, r], writes=[cqna.sub(tb)])
                P.op('dve', lambda e, r=r, sl=sl: e.scalar_tensor_tensor(out=cqnb[:, sl], in0=cqb[:, sl], scalar=gqb[:, 0:1], in1=r[0:64, :], op0=ALU.mult, op1=ALU.mult),
                     reads=[cqb.sub(tb), gqb, r], writes=[cqnb.sub(tb)])
                ps = psum[self._pn % 2]; self._pn += 1
                P.op('pe', lambda e, ps=ps, sl=sl: e.matmul(ps[:], lhsT=ones[:], rhs=sqk[:, sl], start=True, stop=True), reads=[ones, sqk.sub(tb)], writes=[ps])
                r = rq[1]
                P.op('act', lambda e, ps=ps, r=r: e.activation(out=r[:], in_=ps[:], func=AF.Ln, scale=1.0 / 128, bias=EPS), reads=[ps], writes=[r])
                P.op('act', lambda e, r=r: e.activation(out=r[:], in_=r[:], func=AF.Exp, scale=-0.5), reads=[r], writes=[r])
                P.op('dve', lambda e, r=r, sl=sl: e.scalar_tensor_tensor(out=ckvn[:, sl], in0=ckv[:, sl], scalar=gkv[:, 0:1], in1=r[:], op0=ALU.mult, op1=ALU.mult),
                     reads=[ckv.sub(tb), gkv, r], writes=[ckvn.sub(tb)])
        QnT = [P.sb([64, S], BF16, 'QnT') for _ in range(4)]
        QrT = [P.sb([32, S], BF16, 'QrT') for _ in range(4)]
        KnT = [P.sb([64, S], BF16, 'KnT') for _ in range(4)]
        Vtok = P.sb([128, 16, 256], BF16, 'VtokB')
        qx = P.sb([32, 512], F32, 'qx'); qx2 = P.sb([32, 512], F32, 'qx2')
        for h in range(4):
            for tb in range(4):
                sl = slice(512 * tb, 512 * tb + 512)
                ps = psum[self._pn % 2]; self._pn += 1
                P.op('pe', lambda e, ps=ps, sl=sl, h=h: e.matmul(ps[0:64, :], lhsT=wuqa[:, 96 * h:96 * h + 64], rhs=cqna[:, sl], start=True, stop=False),
                     reads=[wuqa, cqna.sub(tb)], writes=[ps])
                P.op('pe', lambda e, ps=ps, sl=sl, h=h: e.matmul(ps[0:64, :], lhsT=wuqb[:, 96 * h:96 * h + 64], rhs=cqnb[:, sl], start=False, stop=True),
                     reads=[wuqb, cqnb.sub(tb)], writes=[ps])
                P.op('act', lambda e, ps=ps, sl=sl, h=h: e.copy(out=QnT[h][:, sl], in_=ps[0:64, :]), reads=[ps], writes=[QnT[h].sub(tb)])
                ps = psum[self._pn % 2]; self._pn += 1
                P.op('pe', lambda e, ps=ps, sl=sl, h=h: e.matmul(ps[0:64, :], lhsT=wukv[:, 128 * h:128 * h + 64], rhs=ckvn[:, sl], start=True, stop=True),
                     reads=[wukv, ckvn.sub(tb)], writes=[ps])
                P.op('act', lambda e, ps=ps, sl=sl, h=h: e.copy(out=KnT[h][:, sl], in_=ps[0:64, :]), reads=[ps], writes=[KnT[h].sub(tb)])
                ps = psum[self._pn % 2]; self._pn += 1
                P.op('pe', lambda e, ps=ps, sl=sl, h=h: e.matmul(ps[0:32, :], lhsT=wuqa[:, 96 * h + 64:96 * h + 96], rhs=cqna[:, sl], start=True, stop=False),
                     reads=[wuqa, cqna.sub(tb)], writes=[ps])
                P.op('pe', lambda e, ps=ps, sl=sl, h=h: e.matmul(ps[0:32, :], lhsT=wuqb[:, 96 * h + 64:96 * h + 96], rhs=cqnb[:, sl], start=False, stop=True),
                     reads=[wuqb, cqnb.sub(tb)], writes=[ps])
                P.op('dve', lambda e, ps=ps, sl=sl: e.tensor_tensor(out=qx[:], in0=ps[0:32, :], in1=cosT[:, sl], op=ALU.mult), reads=[ps, cosT], writes=[qx])
                ps = psum[self._pn % 2]; self._pn += 1
                P.op('pe', lambda e, ps=ps, sl=sl, h=h: e.matmul(ps[0:32, :], lhsT=wra[:, h, :], rhs=cqna[:, sl], start=True, stop=False),
                     reads=[wra, cqna.sub(tb)], writes=[ps])
                P.op('pe', lambda e, ps=ps, sl=sl, h=h: e.matmul(ps[0:32, :], lhsT=wrb[:, h, :], rhs=cqnb[:, sl], start=False, stop=True),
                     reads=[wrb, cqnb.sub(tb)], writes=[ps])
                P.op('dve', lambda e, ps=ps, sl=sl: e.tensor_tensor(out=qx2[:], in0=ps[0:32, :], in1=sinT[:, sl], op=ALU.mult), reads=[ps, sinT], writes=[qx2])
                P.op('dve', lambda e, sl=sl, h=h: e.tensor_tensor(out=QrT[h][:, sl], in0=qx[:], in1=qx2[:], op=ALU.add), reads=[qx, qx2], writes=[QrT[h].sub(tb)])
        for tt in range(16):
            ps = psum[self._pn % 2]; self._pn += 1
            P.op('pe', lambda e, ps=ps, tt=tt: e.matmul(ps[:, 0:256], lhsT=ckvn[:, 128 * tt:128 * tt + 128], rhs=wv[:].rearrange('p h c -> p (h c)'),
                                                        start=True, stop=True), reads=[ckvn.sub(tt // 4), wv], writes=[ps])
            P.op('act', lambda e, ps=ps, tt=tt: e.copy(out=Vtok[:, tt, :], in_=ps[:, 0:256]), reads=[ps], writes=[Vtok])
        cm = P.sb([128, 4, 512], BF16, 'cmB')
        P.op('dve', lambda e: e.tensor_copy(out=cm[:], in_=self.c('cmask').rearrange('p (r f) -> p r f', r=4)), reads=[self.cst], writes=[cm])
        pts = [P.sb([128, 512], BF16, 'ptB') for _ in range(4)]
        obs = [P.sb([64, 512], BF16, 'oB') for _ in range(2)]
        rec = P.sb([64, 512], F32, 'recB')
        blocks = []
        nq = 0
        for h in range(4):
            for qb in range(4):
                nkb = 4 * qb + 4
                for kb in range(nkb):
                    blocks.append((h, qb, kb, nkb, nq))
                nq += 1

        def stage1(i):
            (h, qb, kb, nkb, q_) = blocks[i]
            qsl = slice(512 * qb, 512 * qb + 512)
            ksl = slice(128 * kb, 128 * kb + 128)
            r = kb - 4 * qb
            st = psum[i % 4]; pt = pts[i % 4]
            P.op('pe', lambda e: e.matmul(st[:], lhsT=KnT[h][:, ksl], rhs=QnT[h][:, qsl], start=True, stop=False),
                 reads=[KnT[h].sub(kb // 4), QnT[h].sub(qb)], writes=[st])
            P.op('pe', lambda e: e.matmul(st[:], lhsT=KrT[:, ksl], rhs=QrT[h][:, qsl], start=False, stop=True),
                 reads=[KrT.sub(kb // 4), QrT[h].sub(qb)], writes=[st])
            P.op('act', lambda e: e.activation(out=pt[:], in_=st[:], func=AF.Exp, scale=SC), reads=[st], writes=[pt])
            if r >= 0:
                P.op('dve', lambda e: e.tensor_tensor(out=pt[:], in0=pt[:], in1=cm[:, r, :], op=ALU.mult), reads=[pt, cm], writes=[pt])

        def stage2(i):
            (h, qb, kb, nkb, q_) = blocks[i]
            qsl = slice(512 * qb, 512 * qb + 512)
            pt = pts[i % 4]
            pn = psum[4 + 2 * (q_ % 2)]; pd = psum[5 + 2 * (q_ % 2)]
            P.op('pe', lambda e: e.matmul(pn[0:64, :], lhsT=Vtok[:, kb, 64 * h:64 * h + 64], rhs=pt[:], start=(kb == 0), stop=(kb == nkb - 1)),
                 reads=[Vtok, pt], writes=[pn])
            P.op('pe', lambda e: e.matmul(pd[0:64, :], lhsT=ones[:, 0:64], rhs=pt[:], start=(kb == 0), stop=(kb == nkb - 1)),
                 reads=[ones, pt], writes=[pd])
            if kb == nkb - 1:
                ob = obs[q_ % 2]
                P.op('dve', lambda e: e.reciprocal(out=rec[:], in_=pd[0:64, :]), reads=[pd], writes=[rec])
                P.op('dve', lambda e: e.tensor_tensor(out=ob[:], in0=pn[0:64, :], in1=rec[:], op=ALU.mult), reads=[pn, rec], writes=[ob])
                P.dma('sp', self.ocat_d[4 + h, :, qsl], ob[:], reads=[ob], writes=[self.ocat_d])

        SK = 2
        for i in range(min(SK, len(blocks))):
            stage1(i)
        for i in range(len(blocks)):
            if i + SK < len(blocks):
                stage1(i + SK)
            stage2(i)

    def mixer_D(self, l):
        P = self.P
        xT = self.xT
        psum = self.psum
        self._pn = 0
        W = P.sb([128, 8, 768], BF16, 'wD')
        P.dma('pool', W[:], self.w_in_d[l, :, 2408:3176].rearrange('(k p) n -> p k n', p=128), writes=[W])
        ones = P.sb([128, 64], BF16, 'onesD')
        P.op('pool', lambda e: e.memset(ones[:], 1.0), writes=[ones])
        geoms = [(1, 0, 'tri_ge'), (1, 128, 'tri_le'), (4, 0, 'tri_ge'), (4, 128, 'tri_le'), (16, 0, 'tri_ge')]
        tabs = []
        tmpf = P.sb([128, 128], F32, 'tabtmp')
        for (dil, off, mk) in geoms:
            tab = P.sb([128, 4, 128], BF16, 'tabD')
            for h in range(4):
                slope = 2.0 ** (-8.0 * (h + 1) / 4)
                P.op('dve', lambda e, mk=mk: e.tensor_tensor(out=tmpf[:], in0=self.c('dist'), in1=self.c(mk), op=ALU.mult),
                     reads=[self.cst], writes=[tmpf])
                P.op('act', lambda e, dil=dil, off=off, slope=slope: e.activation(
                    out=tmpf[:], in_=tmpf[:], func=AF.Exp, scale=-slope * dil, bias=-slope * dil * off),
                    reads=[tmpf], writes=[tmpf])
                P.op('dve', lambda e, tab=tab, h=h, mk=mk: e.tensor_tensor(out=tab[:, h, :], in0=tmpf[:], in1=self.c(mk), op=ALU.mult),
                     reads=[tmpf, self.cst], writes=[tab])
            tabs.append(tab)
        QT = [P.sb([64, S], BF16, 'qTD') for _ in range(4)]
        KT = [P.sb([64, S], BF16, 'kTD') for _ in range(4)]
        n = 0
        for i in range(8):
            for tb in range(4):
                ps = psum[n % 2]; n += 1
                for kc in range(8):
                    P.op('pe', lambda e, ps=ps, kc=kc, i=i, tb=tb: e.matmul(
                        ps[0:64, :], lhsT=W[:, kc, 64 * i:64 * i + 64], rhs=xT[:, kc, 512 * tb:512 * tb + 512],
                        start=(kc == 0), stop=(kc == 7)), reads=[W, xT.sub(tb)], writes=[ps])
                dst = QT[i] if i < 4 else KT[i - 4]
                P.op('act', lambda e, ps=ps, dst=dst, tb=tb, i=i: e.mul(out=dst[:, 512 * tb:512 * tb + 512], in_=ps[0:64, :],
                                                                       mul=(0.125 if i < 4 else 1.0)), reads=[ps], writes=[dst])
        NUM = P.sb([64, 4, S], F32, 'numD')
        DEN = P.sb([64, 4, S], F32, 'denD')
        Vt = [P.sb([128, 256], BF16, 'vD') for _ in range(3)]
        PC = [P.sb([128, 4, 128], BF16, 'pcD') for _ in range(2)]
        PP = [P.sb([128, 4, 128], BF16, 'ppD') for _ in range(2)]
        units = []
        for nb in range(16):
            units.append((0, slice(128 * nb, 128 * nb + 128), slice(128 * (nb - 1), 128 * nb) if nb > 0 else None, 0, 1))
        for r in range(4):
            for b in range(4):
                units.append((1, slice(512 * b + r, 512 * (b + 1), 4), slice(512 * (b - 1) + r, 512 * b, 4) if b > 0 else None, 2, 3))
        for r in range(16):
            units.append((2, slice(r, S, 16), None, 4, None))
        nv = 0
        vts = {}

        def stage1(u):
            nonlocal n, nv
            (br, Tq, Tp, gc, gp) = units[u]
            vcur = Vt[nv % 3]; nv += 1
            vts[u] = vcur
            ps = psum[n % 2]; n += 1
            for kc in range(8):
                P.op('pe', lambda e, ps=ps, kc=kc, Tq=Tq: e.matmul(
                    ps[:, 0:256], lhsT=xT[:, kc, Tq], rhs=W[:, kc, 512:768], start=(kc == 0), stop=(kc == 7)),
                    reads=[W, xT.sub(0), xT.sub(1), xT.sub(2), xT.sub(3)], writes=[ps])
            P.op('act', lambda e, ps=ps, vcur=vcur: e.copy(out=vcur[:], in_=ps[:, 0:256]), reads=[ps], writes=[vcur])
            sc = psum[2 + 2 * (u % 2)]
            sp_ = psum[3 + 2 * (u % 2)]
            pc = PC[u % 2]; pp = PP[u % 2]
            for h in range(4):
                P.op('pe', lambda e, h=h, sc=sc, Tq=Tq: e.matmul(
                    sc[:, 128 * h:128 * h + 128], lhsT=KT[h][:, Tq], rhs=QT[h][:, Tq], start=True, stop=True),
                    reads=[KT[h], QT[h]], writes=[sc])
            P.op('act', lambda e, sc=sc, pc=pc: e.activation(out=pc[:].rearrange('p h q -> p (h q)'), in_=sc[:], func=AF.Exp),
                 reads=[sc], writes=[pc])
            P.op('dve', lambda e, pc=pc, gc=gc: e.tensor_tensor(out=pc[:], in0=pc[:], in1=tabs[gc][:], op=ALU.mult),
                 reads=[pc, tabs[gc]], writes=[pc])
            if Tp is not None:
                for h in range(4):
                    P.op('pe', lambda e, h=h, sp_=sp_, Tq=Tq, Tp=Tp: e.matmul(
                        sp_[:, 128 * h:128 * h + 128], lhsT=KT[h][:, Tp], rhs=QT[h][:, Tq], start=True, stop=True),
                        reads=[KT[h], QT[h]], writes=[sp_])
                P.op('act', lambda e, sp_=sp_, pp=pp: e.activation(out=pp[:].rearrange('p h q -> p (h q)'), in_=sp_[:], func=AF.Exp),
                     reads=[sp_], writes=[pp])
                P.op('dve', lambda e, pp=pp, gp=gp: e.tensor_tensor(out=pp[:], in0=pp[:], in1=tabs[gp][:], op=ALU.mult),
                     reads=[pp, tabs[gp]], writes=[pp])

        def stage2(u):
            (br, Tq, Tp, gc, gp) = units[u]
            vcur = vts[u]; vprev = vts.get(u - 1)
            pc = PC[u % 2]; pp = PP[u % 2]
            pn = psum[6]; pd = psum[7]
            for h in range(4):
                P.op('pe', lambda e, h=h, pc=pc, vcur=vcur, last=(Tp is None): e.matmul(
                    pn[0:64, 128 * h:128 * h + 128], lhsT=vcur[:, 64 * h:64 * h + 64], rhs=pc[:, h, :], start=True, stop=last),
                    reads=[vcur, pc], writes=[pn])
                if Tp is not None:
                    P.op('pe', lambda e, h=h, pp=pp, vprev=vprev: e.matmul(
                        pn[0:64, 128 * h:128 * h + 128], lhsT=vprev[:, 64 * h:64 * h + 64], rhs=pp[:, h, :], start=False, stop=True),
                        reads=[vprev, pp], writes=[pn])
                P.op('pe', lambda e, h=h, pc=pc, last=(Tp is None): e.matmul(
                    pd[0:64, 128 * h:128 * h + 128], lhsT=ones[:], rhs=pc[:, h, :], start=True, stop=last),
                    reads=[ones, pc], writes=[pd])
                if Tp is not None:
                    P.op('pe', lambda e, h=h, pp=pp: e.matmul(
                        pd[0:64, 128 * h:128 * h + 128], lhsT=ones[:], rhs=pp[:, h, :], start=False, stop=True),
                        reads=[ones, pp], writes=[pd])
            for (acc_t, pt) in ((NUM, pn), (DEN, pd)):
                src = pt[0:64, :].rearrange('p (h q) -> p h q', h=4)
                if br == 0:
                    P.op('act', lambda e, acc_t=acc_t, src=src, Tq=Tq: e.copy(out=acc_t[:, :, Tq], in_=src), reads=[pt], writes=[acc_t])
                else:
                    P.op('dve', lambda e, acc_t=acc_t, src=src, Tq=Tq: e.tensor_tensor(out=acc_t[:, :, Tq], in0=acc_t[:, :, Tq], in1=src, op=ALU.add),
                         reads=[pt, acc_t], writes=[acc_t])

        stage1(0)
        for u in range(len(units)):
            if u + 1 < len(units):
                stage1(u + 1)
            stage2(u)
        ob = [P.sb([64, S], BF16, 'oD') for _ in range(2)]
        for h in range(4):
            o = ob[h % 2]
            P.op('dve', lambda e, h=h: e.reciprocal(out=DEN[:, h, :], in_=DEN[:, h, :]), reads=[DEN], writes=[DEN])
            P.op('dve', lambda e, h=h, o=o: e.tensor_tensor(out=o[:], in0=NUM[:, h, :], in1=DEN[:, h, :], op=ALU.mult), reads=[NUM, DEN], writes=[o])
            P.dma('sp', self.ocat_d[12 + h], o[:], reads=[o], writes=[self.ocat_d])

    def mixer_stub(self, l):
        P = self.P
        W = P.sb([128, 8, 1024], BF16, 'wstub')
        P.dma('pool', W[:], self.w_in_d[l, :, 0:1024].rearrange('(k p) n -> p k n', p=128), writes=[W])
        ob = [P.sb([64, 512], BF16, 'ostub') for _ in range(2)]
        n = 0
        for c in range(16):
            for tb in range(4):
                ps = self.psum[n % 2]
                o = ob[n % 2]
                n += 1
                for kc in range(8):
                    P.op('pe', lambda e, ps=ps, kc=kc, c=c, tb=tb: e.matmul(
                        ps[0:64, :], lhsT=W[:, kc, 64 * c:64 * c + 64], rhs=self.xT[:, kc, 512 * tb:512 * tb + 512],
                        start=(kc == 0), stop=(kc == 7)), reads=[W, self.xT.sub(tb)], writes=[ps])
                P.op('act', lambda e, ps=ps, o=o: e.copy(out=o[:], in_=ps[0:64, :]), reads=[ps], writes=[o])
                P.dma('sp', self.ocat_d[c, :, 512 * tb:512 * tb + 512], o[:], reads=[o], writes=[self.ocat_d])

    def ln_tile(self, r, g_rep, b_rep, out, r_ap=None, r_res=None, slot=None):
        for _ in self.ln_tile_gen(r, g_rep, b_rep, out, r_ap, r_res, slot):
            pass

    def ln_tile_gen(self, r, g_rep, b_rep, out, r_ap=None, r_res=None, slot=None):
        P = self.P
        if r_ap is None:
            r_ap = r[:]
            r_res = r
        rl = list(r_res) if isinstance(r_res, (list, tuple)) else [r_res]
        if slot is None:
            self._lnk += 1
            slot = self._lnk % 2
        st, mv, sc = self._lntmp[slot]
        for h in range(2):
            P.op('dve', lambda e, h=h: e.bn_stats(out=st[:, h, :], in_=r_ap[:, 512 * h:512 * h + 512]), reads=rl, writes=[st])
        P.op('dve', lambda e: e.bn_aggr(out=mv[:], in_=st[:].rearrange('p a b -> p (a b)')), reads=[st], writes=[mv])
        yield
        P.op('act', lambda e: e.activation(out=sc[:, 0:1], in_=mv[:, 1:2], func=AF.Ln, bias=EPS), reads=[mv], writes=[sc])
        P.op('act', lambda e: e.activation(out=sc[:, 0:1], in_=sc[:, 0:1], func=AF.Exp, scale=-0.5), reads=[sc], writes=[sc])
        yield
        P.op('dve', lambda e: e.scalar_tensor_tensor(out=sc[:, 1:2], in0=mv[:, 0:1], scalar=-1.0, in1=sc[:, 0:1],
                                                     op0=ALU.mult, op1=ALU.mult), reads=[mv, sc], writes=[sc])
        yield
        P.op('act', lambda e: e.activation(out=out[:], in_=r_ap, func=AF.Identity, bias=sc[:, 1:2], scale=sc[:, 0:1]),
             reads=rl + [sc], writes=[out])
        yield
        P.op('dve', lambda e: e.tensor_tensor(out=out[:], in0=out[:], in1=g_rep[:], op=ALU.mult), reads=[out, g_rep], writes=[out])
        P.op('dve', lambda e: e.tensor_tensor(out=out[:], in0=out[:], in1=b_rep[:], op=ALU.add), reads=[out, b_rep], writes=[out])
        yield

    def dense(self, l, last):
        P = self.P
        xT = self.xT
        acc = P.sb([128, 16, D], F32, 'acc')
        self._lnk = 0
        self._lntmp = [(P.sb([128, 2, 6], F32, 'bnst'), P.sb([128, 2], F32, 'mv'), P.sb([128, 2], F32, 'lnsc')) for _ in range(2)]
        x1 = [P.sb([128, D], F32, 'x1') for _ in range(2)]
        x_src = self.x_d if l == 0 else self.xres_d
        with P.scope():
            g1 = P.sb([128, D], F32, 'g1'); b1 = P.sb([128, D], F32, 'b1')
            for t, d in ((g1, self.ln1_g_d), (b1, self.ln1_b_d)):
                P.dma('sp', t[:], d[l:l + 1, :].partition_broadcast(128), writes=[t])
            wout = P.sb([128, 8, D], BF16, 'wout')
            P.dma('pool', wout[:], self.w_out_d[l].rearrange('(c p) n -> p c n', p=128), writes=[wout])
            ocs = [P.sb([128, 8, 512], BF16, 'oc') for _ in range(2)]
            xr = [P.sb([128, D], F32, 'xr') for _ in range(2)]
            rr = [P.sb([128, D], F32, 'rr') for _ in range(2)]
            def ln1_gen(tt):
                tb = tt // 4
                oc = ocs[tb % 2]
                if tt % 4 == 0:
                    for two in range(2):
                        P.dma('sp', oc[64 * two:64 * two + 64], self.ocat_d[:, :, 512 * tb:512 * tb + 512].rearrange('(c two) p t -> two p c t', two=2)[two],
                              reads=[self.ocat_d], writes=[oc])
                xrt = xr[tt % 2]; r = rr[tt % 2]; x1t = x1[tt % 2]
                P.dma('sp', xrt[:], x_src[128 * tt:128 * tt + 128, :], reads=[x_src], writes=[xrt])
                for half in range(2):
                    ps = self.psum[2 * (tt % 2) + half]
                    for c in range(8):
                        P.op('pe', lambda e, ps=ps, c=c, half=half: e.matmul(
                            ps[:], lhsT=oc[:, c, 128 * (tt % 4):128 * (tt % 4) + 128], rhs=wout[:, c, 512 * half:512 * half + 512],
                            start=(c == 0), stop=(c == 7)), reads=[oc, wout], writes=[ps])
                yield
                for half in range(2):
                    ps = self.psum[2 * (tt % 2) + half]
                    P.op('dve', lambda e, ps=ps, half=half: e.scalar_tensor_tensor(
                        out=r[:, 512 * half:512 * half + 512], in0=xrt[:, 512 * half:512 * half + 512], scalar=ALPHA, in1=ps[:],
                        op0=ALU.mult, op1=ALU.add), reads=[ps, xrt], writes=[r])
                yield
                yield from self.ln_tile_gen(r, g1, b1, x1t, slot=tt % 2)
                P.op('act', lambda e: e.mul(out=acc[:, tt, :], in_=x1t[:], mul=ALPHA), reads=[x1t], writes=[acc.sub((tt, 0)), acc.sub((tt, 1))])
                self.transpose_tile_to_xT(x1t, tt, pbase=4 + 2 * (tt % 2))
                yield

            pipeline(ln1_gen, range(16), width=2, skew=4)
        with P.scope():
            g2 = P.sb([128, D], F32, 'g2'); b2 = P.sb([128, D], F32, 'b2')
            for t, d in ((g2, self.ln2_g_d), (b2, self.ln2_b_d)):
                P.dma('sp', t[:], d[l:l + 1, :].partition_broadcast(128), writes=[t])
            HC = 1024
            nhc = D_FF // HC
            NM = HC // 128
            w1s = [P.sb([128, 8, HC], BF16, 'w1') for _ in range(2)]
            w2s = [P.sb([128, NM, D], BF16, 'w2') for _ in range(2)]
            fts = [P.sb([128, NM, 512], BF16, 'fT') for _ in range(2)]
            nf = 0
            n1 = 0
            n2 = 0
            for j in range(nhc):
                w1 = w1s[j % 2]; w2 = w2s[j % 2]
                P.dma('pool', w1[:], self.w_ff1_d[l, :, HC * j:HC * j + HC].rearrange('(k p) n -> p k n', p=128), writes=[w1])
                P.dma('pool', w2[:], self.w_ff2_d[l, HC * j:HC * j + HC, :].rearrange('(m p) n -> p m n', p=128), writes=[w2])
                for tb in range(4):
                    ft = fts[nf % 2]; nf += 1
                    for m in range(NM):
                        ps = self.psum[n1 % 3]; n1 += 1
                        for kc in range(8):
                            P.op('pe', lambda e, ps=ps, kc=kc, m=m, w1=w1, tb=tb: e.matmul(
                                ps[:], lhsT=w1[:, kc, 128 * m:128 * m + 128], rhs=xT[:, kc, 512 * tb:512 * tb + 512],
                                start=(kc == 0), stop=(kc == 7)), reads=[w1, xT.sub(tb)], writes=[ps])
                        P.op('act', lambda e, ps=ps, m=m, ft=ft: e.activation(out=ft[:, m, :], in_=ps[:], func=AF.Relu),
                             reads=[ps], writes=[ft.sub(m)])
                        P.op('act', lambda e, m=m, ft=ft: e.activation(out=ft[:, m, :], in_=ft[:, m, :], func=AF.Square),
                             reads=[ft.sub(m)], writes=[ft.sub(m)])
                    for t4 in range(4):
                        tt = 4 * tb + t4
                        for half in range(2):
                            ps = self.psum[3 + (n2 % 5)]; n2 += 1
                            for m in range(NM):
                                P.op('pe', lambda e, ps=ps, m=m, ft=ft, t4=t4, w2=w2, half=half: e.matmul(
                                    ps[:], lhsT=ft[:, m, 128 * t4:128 * t4 + 128], rhs=w2[:, m, 512 * half:512 * half + 512],
                                    start=(m == 0), stop=(m == NM - 1)), reads=[ft.sub(m), w2], writes=[ps])
                            P.op('dve', lambda e, ps=ps, tt=tt, half=half: e.tensor_tensor(
                                out=acc[:, tt, 512 * half:512 * half + 512], in0=acc[:, tt, 512 * half:512 * half + 512], in1=ps[:], op=ALU.add),
                                reads=[ps, acc.sub((tt, half))], writes=[acc.sub((tt, half))])
            def ln2_gen(tt):
                x2 = x1[tt % 2]
                yield from self.ln_tile_gen(None, g2, b2, x2, r_ap=acc[:, tt, :], r_res=[acc.sub((tt, 0)), acc.sub((tt, 1))], slot=tt % 2)
                if last:
                    P.dma('sp', self.y_d[128 * tt:128 * tt + 128, :], x2[:], reads=[x2], writes=[self.y_d])
                else:
                    P.dma('sp', self.xres_d[128 * tt:128 * tt + 128, :], x2[:], reads=[x2], writes=[self.xres_d])
                    self.transpose_tile_to_xT(x2, tt, pbase=4 + 2 * (tt % 2))
                yield

            pipeline(ln2_gen, range(16), width=2, skew=3)


def pipeline(make_gen, items, width=2, skew=3):
    items = list(items)
    active = []
    nxt = 0
    while nxt < len(items) or active:
        if nxt < len(items) and len(active) < width and (not active or active[-1][1] >= skew):
            active.append([make_gen(items[nxt]), 0])
            nxt += 1
        keep = []
        for a in active:
            try:
                next(a[0])
                a[1] += 1
                keep.append(a)
            except StopIteration:
                pass
        active = keep


def lockstep(gens):
    gens = list(gens)
    while gens:
        nxt = []
        for g in gens:
            try:
                next(g)
                nxt.append(g)
            except StopIteration:
                pass
        gens = nxt


def _rope_table():
    half = 16
    inv_freq = (np.float32(10000.0) ** (-np.arange(half, dtype=np.float32) / np.float32(half))).astype(np.float32)
    ang = (np.arange(S, dtype=np.float32)[None, :] * inv_freq[:, None]).astype(np.float32)
    ang = np.concatenate([ang, ang], axis=0)
    return np.stack([np.cos(ang), np.sin(ang)]).astype(np.float32)


ROPE_TAB = _rope_table()
RMASK = np.stack([(np.arange(S) % 16 != 0), (np.arange(S) % 64 != 0)]).astype(np.float32)
_CACHE = {}


def get_builder(stage='full', nlayers=DEPTH):
    key = (stage, nlayers)
    if key not in _CACHE:
        _CACHE[key] = Builder(stage, nlayers)
    return _CACHE[key]


def make_in_maps(b, inputs, cores):
    maps = []
    for ci in cores:
        m = {'x': np.ascontiguousarray(inputs['x'][ci]), 'cst': b.cst_np, 'rope': ROPE_TAB, 'rmask': RMASK}
        for k in ('w_in', 'w_out', 'w_ff1', 'w_ff2', 'ln1_g', 'ln1_b', 'ln2_g', 'ln2_b',
                  'mla_q_norm_g', 'mla_kv_norm_g', 'mla_w_uq', 'mla_w_ukv', 'hgrn_lb_logits', 'hgrn_norm_g',
                  'gdn_conv_w', 'gdn_a_log', 'gdn_dt_bias', 'gdn_norm_g'):
            m[k] = np.ascontiguousarray(inputs[k])
        maps.append(m)
    return maps


def kernel(**inputs):
    inputs = {k: np.asarray(v) for k, v in inputs.items()}
    b = get_builder('full')
    maps = make_in_maps(b, inputs, list(range(NCORES)))
    res = run_bass_kernel_spmd(b.nc, maps, core_ids=list(range(NCORES)))
    return np.stack([r['y'] for r in res.results], axis=0).astype(np.float32)
```
